# Optimizing a Trainium2 kernel written in Bass

```python
import math
import jax, jax.numpy as jnp
from jax import lax
import numpy as np

D_MODEL = 1024
BATCH = 4
SEQ = 4096
DEPTH = 1

N_Q_HEADS = 8
N_KV_HEADS = 2
HEAD_DIM = 64
Q_GROUP = N_Q_HEADS // N_KV_HEADS
ATTN_WIDTH = N_Q_HEADS * HEAD_DIM
KV_WIDTH = N_KV_HEADS * HEAD_DIM
WINDOW = 128
BLOCK = 128
N_BUCKETS = 32
MAX_DISTANCE = 128
NEG_INF = -1e30
SSM_WIDTH = D_MODEL // 2
SSM_GROUP = 16
SSM_GROUPS = SSM_WIDTH // SSM_GROUP
SSM_STATE = 64
DT_MIN = 1e-3
DT_MAX = 1e-1
N_BRANCHES = 2
D_FF = 4 * D_MODEL
IN_WIDTH = ATTN_WIDTH + 2 * KV_WIDTH + SSM_WIDTH + N_BRANCHES * D_MODEL
SPLITS = (ATTN_WIDTH, ATTN_WIDTH + KV_WIDTH, ATTN_WIDTH + 2 * KV_WIDTH,
          ATTN_WIDTH + 2 * KV_WIDTH + SSM_WIDTH, ATTN_WIDTH + 2 * KV_WIDTH + SSM_WIDTH + D_MODEL)
RMS_EPS = 1e-6

kernel_name = "hybrid_swa_sink_s5_gated_block"


def rmsnorm(x, g):
    xf = x.astype(jnp.float32)
    y = xf * lax.rsqrt(jnp.mean(xf * xf, axis=-1, keepdims=True) + RMS_EPS)
    return (y * g.astype(jnp.float32)).astype(x.dtype)


def t5_causal_bucket(dist):
    max_exact = N_BUCKETS // 2
    d = jnp.maximum(dist, 0)
    df = jnp.maximum(d, 1).astype(jnp.float32)
    large = max_exact + (jnp.log(df / max_exact) / math.log(MAX_DISTANCE / max_exact)
                         * (N_BUCKETS - max_exact)).astype(jnp.int32)
    large = jnp.minimum(large, N_BUCKETS - 1)
    return jnp.where(d < max_exact, d, large)


def sliding_window_attention(q, k, v, sinks, rel_bias):
    B, L = q.shape[0], q.shape[1]
    nb = L // BLOCK
    qb = q.reshape(B, nb, BLOCK, N_KV_HEADS, Q_GROUP, HEAD_DIM)

    def band(t):
        tb = t.reshape(B, nb, BLOCK, N_KV_HEADS, HEAD_DIM)
        prev = jnp.pad(tb, ((0, 0), (1, 0), (0, 0), (0, 0), (0, 0)))[:, :-1]
        return jnp.concatenate([prev, tb], axis=2)

    kb, vb = band(k), band(v)
    logits = jnp.einsum('bnqkgd,bnskd->bnkgqs', qb, kb,
                        preferred_element_type=jnp.float32) * (HEAD_DIM ** -0.5)
    qi = jnp.arange(BLOCK)[:, None]
    kj = jnp.arange(2 * BLOCK)[None, :]
    dist = qi + BLOCK - kj
    band_ok = (dist >= 0) & (dist < WINDOW)
    key_pos = jnp.arange(nb)[:, None] * BLOCK - BLOCK + jnp.arange(2 * BLOCK)[None, :]
    mask = band_ok[None] & (key_pos >= 0)[:, None, :]
    bias = rel_bias.astype(jnp.float32)[t5_causal_bucket(dist)]
    bias = jnp.transpose(bias, (2, 0, 1)).reshape(N_KV_HEADS, Q_GROUP, BLOCK, 2 * BLOCK)
    logits = jnp.where(mask[None, :, None, None], logits + bias[None, None], NEG_INF)
    s = sinks.astype(jnp.float32).reshape(1, 1, N_KV_HEADS, Q_GROUP, 1, 1)
    m = jnp.maximum(jnp.max(logits, axis=-1, keepdims=True), s)
    p = jnp.exp(logits - m)
    denom = jnp.sum(p, axis=-1, keepdims=True) + jnp.exp(s - m)
    probs = (p / denom).astype(v.dtype)
    out = jnp.einsum('bnkgqs,bnskd->bnqkgd', probs, vb)
    return out.reshape(B, L, ATTN_WIDTH)


def s5_ssm(u, lam_re, lam_im, log_dt, b_re, b_im, c_re, c_im, d_skip):
    B, L = u.shape[0], u.shape[1]
    uf = u.astype(jnp.float32).reshape(B, L, SSM_GROUPS, SSM_GROUP)
    dt = jnp.exp(log_dt.astype(jnp.float32))[:, None]
    lr = lam_re.astype(jnp.float32)
    li = lam_im.astype(jnp.float32)
    mag = jnp.exp(lr * dt)
    ab_re = mag * jnp.cos(li * dt)
    ab_im = mag * jnp.sin(li * dt)
    nr = ab_re - 1.0
    den = lr * lr + li * li
    f_re = (nr * lr + ab_im * li) / den
    f_im = (ab_im * lr - nr * li) / den
    br = b_re.astype(jnp.float32)
    bi = b_im.astype(jnp.float32)
    bb_re = f_re[..., None] * br - f_im[..., None] * bi
    bb_im = f_re[..., None] * bi + f_im[..., None] * br
    bu_re = jnp.einsum('blgc,gpc->blgp', uf, bb_re)
    bu_im = jnp.einsum('blgc,gpc->blgp', uf, bb_im)
    a_re = jnp.broadcast_to(ab_re, bu_re.shape)
    a_im = jnp.broadcast_to(ab_im, bu_im.shape)

    def combine(e1, e2):
        a1r, a1i, b1r, b1i = e1
        a2r, a2i, b2r, b2i = e2
        return (a2r * a1r - a2i * a1i,
                a2r * a1i + a2i * a1r,
                a2r * b1r - a2i * b1i + b2r,
                a2r * b1i + a2i * b1r + b2i)

    _, _, h_re, h_im = lax.associative_scan(combine, (a_re, a_im, bu_re, bu_im), axis=1)
    y = (jnp.einsum('blgp,gcp->blgc', h_re, c_re.astype(jnp.float32))
         - jnp.einsum('blgp,gcp->blgc', h_im, c_im.astype(jnp.float32)))
    y = y + d_skip.astype(jnp.float32).reshape(SSM_GROUPS, SSM_GROUP) * uf
    return y.reshape(B, L, SSM_WIDTH).astype(u.dtype)


def setup_inputs(seed: int = 0) -> dict:
    key = jax.random.key(seed)
    ks = jax.random.split(key, 24)
    f32 = jnp.float32
    nrm = lambda k, shape, scale: jax.random.normal(k, shape, f32) * scale
    n_idx = jnp.arange(SSM_STATE, dtype=f32)
    return {
        "x": jax.random.normal(ks[0], (BATCH, SEQ, D_MODEL), f32),
        "norm_mix_pre": 1.0 + nrm(ks[1], (DEPTH, D_MODEL), 0.05),
        "norm_mix_post": 1.0 + nrm(ks[2], (DEPTH, D_MODEL), 0.05),
        "norm_mlp_pre": 1.0 + nrm(ks[3], (DEPTH, D_MODEL), 0.05),
        "norm_mlp_post": 1.0 + nrm(ks[4], (DEPTH, D_MODEL), 0.05),
        "w_in": nrm(ks[5], (DEPTH, D_MODEL, IN_WIDTH), D_MODEL ** -0.5),
        "rel_bias": nrm(ks[6], (N_BUCKETS, N_Q_HEADS), 0.5),
        "sinks": nrm(ks[7], (DEPTH, N_Q_HEADS), 0.5),
        "lam_re": -0.5 + nrm(ks[8], (DEPTH, SSM_GROUPS, SSM_STATE), 0.02),
        "lam_im": math.pi * n_idx + nrm(ks[9], (DEPTH, SSM_GROUPS, SSM_STATE), 0.02),
        "log_dt": jax.random.uniform(ks[10], (DEPTH, SSM_GROUPS), f32,
                                     math.log(DT_MIN), math.log(DT_MAX)),
        "b_re": nrm(ks[11], (DEPTH, SSM_GROUPS, SSM_STATE, SSM_GROUP), SSM_GROUP ** -0.5),
        "b_im": nrm(ks[12], (DEPTH, SSM_GROUPS, SSM_STATE, SSM_GROUP), SSM_GROUP ** -0.5),
        "c_re": nrm(ks[13], (DEPTH, SSM_GROUPS, SSM_GROUP, SSM_STATE), SSM_STATE ** -0.5),
        "c_im": nrm(ks[14], (DEPTH, SSM_GROUPS, SSM_GROUP, SSM_STATE), SSM_STATE ** -0.5),
        "d_skip": nrm(ks[15], (DEPTH, SSM_WIDTH), 1.0),
        "w_glu": nrm(ks[16], (DEPTH, SSM_WIDTH, SSM_WIDTH), SSM_WIDTH ** -0.5),
        "w_attn_branch": nrm(ks[17], (DEPTH, ATTN_WIDTH, D_MODEL), ATTN_WIDTH ** -0.5),
        "w_ssm_branch": nrm(ks[18], (DEPTH, SSM_WIDTH, D_MODEL), SSM_WIDTH ** -0.5),
        "w_out": nrm(ks[19], (DEPTH, D_MODEL, D_MODEL), D_MODEL ** -0.5),
        "w_ff_in": nrm(ks[20], (DEPTH, D_MODEL, D_FF), D_MODEL ** -0.5),
        "w_ff_out": nrm(ks[21], (DEPTH, D_FF, D_MODEL), D_FF ** -0.5),
    }


def reference(x, norm_mix_pre, norm_mix_post, norm_mlp_pre, norm_mlp_post, w_in, rel_bias,
              sinks, lam_re, lam_im, log_dt, b_re, b_im, c_re, c_im, d_skip, w_glu,
              w_attn_branch, w_ssm_branch, w_out, w_ff_in, w_ff_out):
    B, L = x.shape[0], x.shape[1]
    for l in range(DEPTH):
        h = rmsnorm(x, norm_mix_pre[l])
        proj = h @ w_in[l]
        q, k, v, u, g_attn, g_ssm = jnp.split(proj, SPLITS, axis=-1)
        q = q.reshape(B, L, N_Q_HEADS, HEAD_DIM)
        k = k.reshape(B, L, N_KV_HEADS, HEAD_DIM)
        v = v.reshape(B, L, N_KV_HEADS, HEAD_DIM)
        y_attn = sliding_window_attention(q, k, v, sinks[l], rel_bias) @ w_attn_branch[l]
        z = jax.nn.gelu(s5_ssm(u, lam_re[l], lam_im[l], log_dt[l], b_re[l], b_im[l],
                               c_re[l], c_im[l], d_skip[l]))
        z = z * jax.nn.sigmoid(z @ w_glu[l])
        y_ssm = z @ w_ssm_branch[l]
        merged = jax.nn.sigmoid(g_attn) * y_attn + jax.nn.sigmoid(g_ssm) * y_ssm
        x = x + rmsnorm(merged @ w_out[l], norm_mix_post[l])
        h = rmsnorm(x, norm_mlp_pre[l])
        f = jnp.square(jax.nn.relu(h @ w_ff_in[l])) @ w_ff_out[l]
        x = x + rmsnorm(f, norm_mlp_post[l])
    return x
```

```python
import math
from contextlib import ExitStack

import numpy as np
import concourse.bass as bass
import concourse.mybir as mybir
from concourse.bass_utils import run_bass_kernel_spmd

F32 = mybir.dt.float32
BF16 = mybir.dt.bfloat16
AF = mybir.ActivationFunctionType
ALU = mybir.AluOpType
AX = mybir.AxisListType

NEG = -30000.0
EPS = 1e-6
PI = math.pi
TWO_PI = 2.0 * math.pi
MAGIC = 12582912.0
D = 1024
NTOK = 2048
DBG = False
STOP = 99


class _Stop(Exception):
    pass


class Buf:
    __slots__ = ("w", "r")

    def __init__(self):
        self.w = {}
        self.r = {}


class Tl:
    def __init__(self, t):
        self.t = t
        self.b = Buf()

    def __getitem__(self, k):
        return self.t[k]


class Prog:
    ENGS = ("pe", "act", "dve", "pool", "sp")

    def __init__(self, nc, es):
        self.nc, self.es = nc, es
        self.ops = {e: [] for e in self.ENGS}
        self.sem, self.cnt = {}, {}
        self.waited = {e: {} for e in self.ENGS}
        for e in ("pe", "act", "dve", "pool"):
            self.mksem("E_" + e)

    def mksem(self, name):
        if name not in self.sem:
            self.sem[name] = self.es.enter_context(self.nc.semaphore(name))
            self.cnt[name] = 0
        return name

    def _waits(self, eng, r, w, is_dma):
        own = None if is_dma else "E_" + eng
        need = {}

        def add(sem, val):
            if val > need.get(sem, 0):
                need[sem] = val

        for b in r:
            for sem, val in b.w.items():
                if sem == own and eng == "pe":
                    continue
                add(sem, val)
        for b in w:
            for sem, val in b.w.items():
                if sem != own or eng != "pe":
                    add(sem, val)
            for sem, val in b.r.items():
                if sem != own or eng != "pe":
                    add(sem, val)
        out = []
        wd = self.waited[eng]
        for sem, val in need.items():
            if wd.get(sem, 0) < val:
                out.append((sem, val))
                wd[sem] = val
        return out

    @staticmethod
    def _bufs(xs):
        return [x.b if isinstance(x, Tl) else x for x in xs]

    def op(self, eng, fn, r=(), w=(), signal=True):
        r, w = self._bufs(r), self._bufs(w)
        waits = self._waits(eng, r, w, False)
        name = "E_" + eng
        if signal:
            self.cnt[name] += 1
            val = self.cnt[name]
            inc = (name, 1)
        else:
            val = self.cnt[name] + 1
            inc = None
        self.ops[eng].append((waits, fn, inc))
        for b in r:
            b.r[name] = max(b.r.get(name, 0), val)
        for b in w:
            b.w = {name: val}
            b.r = {}

    def dma(self, q, out, in_, r=(), w=(), sem=None, **kw):
        r, w = self._bufs(r), self._bufs(w)
        self.mksem(sem)
        waits = [(sm, v) for (sm, v) in self._waits(q, r, w, True) if sm != sem]
        self.cnt[sem] += 16
        val = self.cnt[sem]
        self.ops[q].append((waits, lambda e: e.dma_start(out=out, in_=in_, **kw), (sem, 16)))
        for b in r:
            b.r[sem] = max(b.r.get(sem, 0), val)
        for b in w:
            b.w = {sem: val}
            b.r = {}

    def seal(self, sem, bufs):
        for b in self._bufs(bufs):
            b.w = {sem: self.cnt[sem]}

    def barrier(self):
        for eng in self.ENGS:
            waits = []
            for sem, c in self.cnt.items():
                if sem == "E_" + eng:
                    continue
                if c > self.waited[eng].get(sem, 0):
                    waits.append((sem, c))
                    self.waited[eng][sem] = c
            self.ops[eng].append((waits, None, None))

    def emit(self, block):
        def mk(eng):
            def f(e):
                for waits, fn, inc in self.ops[eng]:
                    for sem, val in waits:
                        e.wait_ge(self.sem[sem], val)
                    if fn is not None:
                        ins = fn(e)
                        if inc is not None:
                            ins.then_inc(self.sem[inc[0]], inc[1])

            return f

        block.tensor(mk("pe"))
        block.scalar(mk("act"))
        block.vector(mk("dve"))
        block.gpsimd(mk("pool"))
        block.sync(mk("sp"))

    def mm(self, out, lhsT, rhs, start, stop, r, w, signal=None):
        if signal is None:
            signal = stop
        self.op("pe", lambda e: e.matmul(out, lhsT=lhsT, rhs=rhs, start=start, stop=stop), r, w, signal)

    def tp(self, out, in_, ident, r, w, signal=True):
        self.op("pe", lambda e: e.transpose(out=out, in_=in_, identity=ident), r, w, signal)

    def act(self, out, in_, func, r, w, bias=None, scale=None, accum=None):
        kw = {}
        if bias is not None:
            kw["bias"] = bias
        if scale is not None:
            kw["scale"] = scale
        if accum is not None:
            kw["accum_out"] = accum
        self.op("act", lambda e: e.activation(out=out, in_=in_, func=func, **kw), r, w)

    def tt(self, eng, out, in0, in1, op, r, w):
        self.op(eng, lambda e: e.tensor_tensor(out=out, in0=in0, in1=in1, op=op), r, w)

    def ts(self, eng, out, in0, s1, s2, op0, op1, r, w):
        if s2 is None:
            self.op(eng, lambda e: e.tensor_scalar(out=out, in0=in0, scalar1=s1, scalar2=None, op0=op0), r, w)
        else:
            self.op(eng, lambda e: e.tensor_scalar(out=out, in0=in0, scalar1=s1, scalar2=s2, op0=op0, op1=op1), r, w)

    def stt(self, eng, out, in0, scalar, in1, op0, op1, r, w):
        self.op(eng, lambda e: e.scalar_tensor_tensor(out=out, in0=in0, scalar=scalar, in1=in1, op0=op0, op1=op1), r, w)

    def cp(self, eng, out, in_, r, w):
        if eng == "act":
            self.op(eng, lambda e: e.copy(out=out, in_=in_), r, w)
        else:
            self.op(eng, lambda e: e.tensor_copy(out=out, in_=in_), r, w)

    def rmax(self, out, in_, r, w):
        self.op("dve", lambda e: e.tensor_reduce(out=out, in_=in_, axis=AX.X, op=ALU.max), r, w)

    def recip(self, out, in_, r, w):
        self.op("dve", lambda e: e.reciprocal(out=out, in_=in_), r, w)

    def scan(self, out, d0, d1, r, w):
        self.op("dve", lambda e: e.tensor_tensor_scan(out=out, data0=d0, data1=d1, initial=0.0, op0=ALU.mult, op1=ALU.add), r, w)


class Ring:
    def __init__(self, items):
        self.items, self.i = items, 0

    def next(self):
        x = self.items[self.i % len(self.items)]
        self.i += 1
        return x


def build():
    nc = bass.Bass("TRN2", target_bir_lowering=False)

    def din(name, shape, dt=F32):
        return nc.dram_tensor(name, list(shape), dt, kind="ExternalInput").ap()

    x = din("x", [4096, D])
    w_in = din("w_in", [D, 3328])
    w_glu = din("w_glu", [512, 512])
    w_ab = din("w_ab", [512, D])
    w_sb = din("w_sb", [512, D])
    w_out = din("w_out", [D, D])
    w_ffi = din("w_ffi", [D, 4096])
    w_ffo = din("w_ffo", [4096, D])
    gains = din("gains", [4, 128, D])
    relb = din("relb", [33, 8, 128])
    sinks_d = din("sinks", [128, 8])
    oh_d = din("oh", [33, 384])
    halo_d = din("halo", [128, 128])
    ident_d = din("ident", [128, 128])
    cmask_d = din("cmask", [128, 2, 256])
    identq_d = din("identq", [128, 2, 256])
    ph_d = din("ph", [128, 8])
    tauN_d = din("tauN", [128, 16])
    tauP_d = din("tauP", [128, 17])
    Jv_d = din("Jv", [128, 256])
    lamr_d = din("lamr", [128, 32])
    lami_d = din("lami", [128, 32])
    ldt_d = din("ldt", [128, 32])
    bre_d = din("bre", [128, 32, 16])
    bim_d = din("bim", [128, 32, 16])
    cre_d = din("cre", [128, 32, 16])
    cim_d = din("cim", [128, 32, 16])
    dcol_d = din("dcol", [128, 32])
    ddiag_d = din("ddiag", [16, 32, 16])
    out_d = nc.dram_tensor("out", [NTOK, D], F32, kind="ExternalOutput").ap()
    x1scr = nc.dram_tensor("x1scr", [NTOK, D], F32).ap()
    wffi_b = nc.dram_tensor("wffi_b", [D, 4096], BF16).ap()
    wffo_b = nc.dram_tensor("wffo_b", [4096, D], BF16).ap()
    tbscr_t = nc.dram_tensor("tbscr", [8, 128 * 383], F32)
    tbscr = tbscr_t.ap()
    if DBG:
        dbg_zT = nc.dram_tensor("dbg_zT", [128, 4, NTOK], BF16, kind="ExternalOutput").ap()
        dbg_X = nc.dram_tensor("dbg_X", [128, 2 * 32 * 16 * 16], BF16, kind="ExternalOutput").ap()
        dbg_tb = nc.dram_tensor("dbg_tb", [128, 8 * 256], F32, kind="ExternalOutput").ap()

    with ExitStack() as es:
        P = Prog(nc, es)
        global _LASTP
        _LASTP = P

        def sb(scope, name, shape, dt):
            return Tl(scope.enter_context(nc.sbuf_tensor("sb_" + name, list(shape), dt)))

        def psum(name, shape, dt):
            return Tl(es.enter_context(nc.psum_tensor(name, list(shape), dt)))

        pf = [psum(f"pf{i}", [128, 512], F32) for i in range(6)]
        pb = [psum(f"pb{i}", [128, 1024], BF16) for i in range(2)]
        pfr = Ring(pf)
        pfr5 = Ring(pf[0:5])
        pbr = Ring(pb)

        ident = sb(es, "ident", [128, 128], BF16)
        epsc = sb(es, "epsc", [128, 1], F32)
        halfpi = sb(es, "halfpi", [128, 1], F32)
        scABC = es.enter_context(ExitStack())
        scS = ExitStack()
        ident_f = sb(scABC, "ident_f", [128, 128], F32)
        Tb = sb(scABC, "Tb", [128, 8, 256], F32)
        Tb0 = sb(scABC, "Tb0", [128, 8, 128], F32)
        sinks = sb(scABC, "sinks", [128, 8], F32)
        gpre = sb(scABC, "gpre", [128, D], F32)
        gpost = sb(scABC, "gpost", [128, D], F32)
        zT = sb(scABC, "zT", [128, 4, NTOK], BF16)
        wab = sb(scABC, "wab", [128, 4, D], BF16)
        wsb_ = sb(scABC, "wsb", [128, 4, D], BF16)
        wglu = sb(scABC, "wglu", [128, 4, 512], BF16)
        wout = sb(scABC, "wout", [128, 8, D], BF16)

        cvb_i, cvb_o = Buf(), Buf()

        def body():
            P.dma("sp", ident_f[:], ident_d[:, :], w=[ident_f], sem="S_c0")
            P.dma("sp", sinks[:], sinks_d[:, :], w=[sinks], sem="S_c0")
            P.dma("sp", gpre[:], gains[0, :, :], w=[gpre], sem="S_c0")
            P.dma("sp", gpost[:], gains[1, :, :], w=[gpost], sem="S_c0")
            P.seal("S_c0", [ident_f, sinks, gpre, gpost])
            P.cp("dve", ident[:], ident_f[:], [ident_f], [ident])
            P.op("dve", lambda e: e.memset(epsc[:], EPS), [], [epsc])
            P.op("dve", lambda e: e.memset(halfpi[:], PI / 2), [], [halfpi])

            def wload(dst_tl, dst_ap_fn, src, nk, sem):
                srcv = src.rearrange("(k p) n -> p k n", p=128)
                for k in range(nk):
                    P.dma("pool", dst_ap_fn(k), srcv[:, k, :], w=[dst_tl], sem=sem)
                P.seal(sem, [dst_tl])

            def rstd_from_ss(ss_ap, ss_tl, rstd_tl, tmp_tl):
                P.act(tmp_tl[:], ss_ap, AF.Ln, [ss_tl, epsc], [tmp_tl], bias=epsc[:], scale=1.0 / D)
                P.act(rstd_tl[:], tmp_tl[:], AF.Exp, [tmp_tl], [rstd_tl], scale=-0.5)

            def norm_transpose(xt, g_tl, xs, ss, tmp, rstd, dst_aps, dst_tl, evac_eng, defer=False):
                P.act(xs[:], xt[:], AF.Square, [xt], [xs, ss], accum=ss[:])
                rstd_from_ss(ss[:], ss, rstd, tmp)
                P.stt("dve", xs[:], xt[:], rstd[:], g_tl[:], ALU.mult, ALU.mult, [xt, rstd, g_tl], [xs])

                def stage2():
                    bank = pbr.next()
                    for k in range(8):
                        P.tp(bank[:, k * 128:(k + 1) * 128], xs[:, k * 128:(k + 1) * 128], ident[:], [xs, ident], [bank], signal=(k == 7))
                    P.cp(evac_eng, dst_aps, bank[:, :].rearrange("p (k j) -> p k j", k=8), [bank], [dst_tl])

                if defer:
                    return stage2
                stage2()

            with ExitStack() as scAB:
                def small(name, shape, dt=F32):
                    return sb(scAB, name, shape, dt)

                ph = small("ph", [128, 8]); Jv = small("Jv", [128, 256])
                Y1 = small("Y1", [128, 32, 16]); Y2 = small("Y2", [128, 32, 16])
                dcol = small("dcol", [128, 32])
                ddiag = small("ddiag", [16, 32, 16])
                phir = small("phir", [128, 32]); th15r = small("th15r", [128, 32])
                mag15 = small("mag15", [128, 32]); r16 = small("r16", [128, 32])
                X1 = small("X1", [128, 32, 16]); X2 = small("X2", [128, 32, 16])
                E1n = small("E1n", [128, 32, 16]); E2n = small("E2n", [128, 32, 16])
                F1 = small("F1", [128, 32, 17]); F2 = small("F2", [128, 32, 17])
                X = small("X", [128, 2, 32, 16, 16], BF16)

                scS.__enter__()

                def smallt(name, shape, dt=F32):
                    return sb(scS, name, shape, dt)

                lr = smallt("lr", [128, 32]); li = smallt("li", [128, 32]); ldt = smallt("ldt", [128, 32])
                tauN = smallt("tauN", [128, 16]); tauP = smallt("tauP", [128, 17])
                Br = smallt("Br", [128, 32, 16]); Bi = smallt("Bi", [128, 32, 16])
                dt_ = smallt("dt_", [128, 32]); lrdt = smallt("lrdt", [128, 32]); th = smallt("th", [128, 32])
                thr = smallt("thr", [128, 32]); t32a = smallt("t32a", [128, 32]); t32b = smallt("t32b", [128, 32])
                t32c = smallt("t32c", [128, 32])
                mag1 = smallt("mag1", [128, 32]); cth = smallt("cth", [128, 32]); sth = smallt("sth", [128, 32])
                ar = smallt("ar", [128, 32]); ai = smallt("ai", [128, 32]); nr = smallt("nr", [128, 32])
                den = smallt("den", [128, 32]); fre = smallt("fre", [128, 32]); fim = smallt("fim", [128, 32])
                magP = smallt("magP", [128, 32, 17]); angP = smallt("angP", [128, 32, 17])
                tb17a = smallt("tb17a", [128, 32, 17]); tb17b = smallt("tb17b", [128, 32, 17])
                relb_s = smallt("relb_s", [33, 8, 128])
                oh_s = smallt("oh_s", [33, 384])
                halo_s = smallt("halo_s", [128, 128])
                rrow = [smallt(f"rrow{i}", [128, 383]) for i in range(2)]

                def reduce_pm_pi(dst_ap, dst_tl, src_ap, tmp_ap, tmp_tl, rlist, eng="dve"):
                    P.ts(eng, tmp_ap, src_ap, 1.0 / TWO_PI, MAGIC, ALU.mult, ALU.add, rlist, [tmp_tl])
                    P.ts(eng, tmp_ap, tmp_ap, -MAGIC, None, ALU.add, None, [tmp_tl], [tmp_tl])
                    P.stt(eng, dst_ap, tmp_ap, -TWO_PI, src_ap, ALU.mult, ALU.add, [tmp_tl] + rlist, [dst_tl])
                    P.ts(eng, dst_ap, dst_ap, 3.14159, -3.14159, ALU.min, ALU.max, [dst_tl], [dst_tl])

                def sin_of(dst, ang_ap, ang_tl, phase, tA, tB, sl=None):
                    sl = sl if sl is not None else (slice(None),) * 3
                    rl = [ang_tl] + ([ph] if not isinstance(phase, float) else [])
                    P.ts("dve", tA[sl], ang_ap, phase, None, ALU.add, None, rl, [tA])
                    reduce_pm_pi(tB[sl], tB, tA[sl], dst[sl], dst, [tA])
                    yield
                    P.act(dst[sl], tB[sl], AF.Sin, [tB], [dst])
                    yield

                def background():
                    P.dma("sp", relb_s[:], relb[:, :, :], w=[relb_s], sem="S_c1")
                    P.dma("sp", oh_s[:], oh_d[:, :], w=[oh_s], sem="S_c1")
                    P.dma("sp", halo_s[:], halo_d[:, :], w=[halo_s], sem="S_c1")
                    P.seal("S_c1", [relb_s, oh_s, halo_s])
                    for tl, src in ((lr, lamr_d), (li, lami_d), (ldt, ldt_d), (ph, ph_d), (tauN, tauN_d), (tauP, tauP_d),
                                    (Jv, Jv_d), (dcol, dcol_d)):
                        P.dma("sp", tl[:], src[:, :], w=[tl], sem="S_c2")
                    for tl, src in ((Br, bre_d), (Bi, bim_d), (Y1, cre_d), (Y2, cim_d), (ddiag, ddiag_d)):
                        P.dma("sp", tl[:], src[:, :, :], w=[tl], sem="S_c2")
                    P.seal("S_c2", [lr, li, ldt, ph, tauN, tauP, Jv, dcol, Br, Bi, Y1, Y2, ddiag])
                    yield
                    scrb = [Buf() for _ in range(8)]
                    for h in range(8):
                        bank = pf[5]
                        P.mm(bank[:, 0:383], relb_s[:, h, :], oh_s[:, 0:383], True, True, [relb_s, oh_s], [bank])
                        rr = rrow[h % 2]
                        P.cp("dve", rr[:], bank[:, 0:383], [bank], [rr])
                        dst = bass.AP(tbscr_t, h * 128 * 383, [[383, 128], [1, 383]])
                        P.dma("pool", dst, rr[:], r=[rr], w=[scrb[h]], sem=f"S_tbw{h % 2}")
                        yield
                    for h in range(8):
                        src = bass.AP(tbscr_t, h * 128 * 383 + 127, [[382, 128], [1, 256]])
                        P.dma("pool", Tb[:, h, :], src, r=[scrb[h]], w=[Tb], sem="S_tbr")
                    P.seal("S_tbr", [Tb])
                    yield
                    P.act(dt_[:], ldt[:], AF.Exp, [ldt], [dt_])
                    yield
                    P.tt("dve", lrdt[:], lr[:], dt_[:], ALU.mult, [lr, dt_], [lrdt])
                    P.tt("dve", th[:], li[:], dt_[:], ALU.mult, [li, dt_], [th])
                    reduce_pm_pi(thr[:], thr, th[:], t32a[:], t32a, [th])
                    yield
                    P.act(mag1[:], lrdt[:], AF.Exp, [lrdt], [mag1])
                    P.act(mag15[:], lrdt[:], AF.Exp, [lrdt], [mag15], scale=15.0)
                    P.act(r16[:], lrdt[:], AF.Exp, [lrdt], [r16], scale=16.0)
                    s2 = (slice(None), slice(None))
                    yield from sin_of(cth, thr[:], thr, PI / 2, t32a, t32b, s2)
                    yield
                    yield from sin_of(sth, thr[:], thr, 0.0, t32a, t32b, s2)
                    yield
                    P.tt("dve", ar[:], mag1[:], cth[:], ALU.mult, [mag1, cth], [ar])
                    P.tt("dve", ai[:], mag1[:], sth[:], ALU.mult, [mag1, sth], [ai])
                    P.ts("dve", nr[:], ar[:], -1.0, None, ALU.add, None, [ar], [nr])
                    P.tt("dve", den[:], lr[:], lr[:], ALU.mult, [lr], [den])
                    yield
                    P.tt("dve", t32a[:], li[:], li[:], ALU.mult, [li], [t32a])
                    P.tt("dve", den[:], den[:], t32a[:], ALU.add, [den, t32a], [den])
                    P.recip(den[:], den[:], [den], [den])
                    yield
                    P.tt("dve", t32a[:], nr[:], lr[:], ALU.mult, [nr, lr], [t32a])
                    P.tt("dve", t32b[:], ai[:], li[:], ALU.mult, [ai, li], [t32b])
                    P.tt("dve", t32a[:], t32a[:], t32b[:], ALU.add, [t32a, t32b], [t32a])
                    P.tt("dve", fre[:], t32a[:], den[:], ALU.mult, [t32a, den], [fre])
                    yield
                    P.tt("dve", t32b[:], ai[:], lr[:], ALU.mult, [ai, lr], [t32b])
                    P.tt("dve", t32c[:], nr[:], li[:], ALU.mult, [nr, li], [t32c])
                    P.tt("dve", t32b[:], t32b[:], t32c[:], ALU.subtract, [t32b, t32c], [t32b])
                    P.tt("dve", fim[:], t32b[:], den[:], ALU.mult, [t32b, den], [fim])
                    yield
                    s16 = (slice(None), slice(None), slice(0, 16))
                    fre_b = fre[:].unsqueeze(2).to_broadcast([128, 32, 16])
                    fim_b = fim[:].unsqueeze(2).to_broadcast([128, 32, 16])
                    P.tt("dve", tb17a[s16], Br[:], fre_b, ALU.mult, [Br, fre], [tb17a])
                    P.tt("dve", tb17b[s16], Bi[:], fim_b, ALU.mult, [Bi, fim], [tb17b])
                    P.tt("dve", X1[:], tb17a[s16], tb17b[s16], ALU.subtract, [tb17a, tb17b], [X1])
                    yield
                    P.tt("dve", tb17a[s16], Bi[:], fre_b, ALU.mult, [Bi, fre], [tb17a])
                    P.tt("dve", tb17b[s16], Br[:], fim_b, ALU.mult, [Br, fim], [tb17b])
                    P.tt("dve", X2[:], tb17a[s16], tb17b[s16], ALU.add, [tb17a, tb17b], [X2])
                    yield
                    thr_b16 = thr[:].unsqueeze(2).to_broadcast([128, 32, 16])
                    lrdt_b16 = lrdt[:].unsqueeze(2).to_broadcast([128, 32, 16])
                    tauN_b = tauN[:].unsqueeze(1).to_broadcast([128, 32, 16])
                    P.tt("dve", angP[s16], thr_b16, tauN_b, ALU.mult, [thr, tauN], [angP])
                    P.tt("dve", tb17a[s16], lrdt_b16, tauN_b, ALU.mult, [lrdt, tauN], [tb17a])
                    yield
                    P.act(magP[s16], tb17a[s16], AF.Exp, [tb17a], [magP])
                    yield
                    yield from sin_of(E1n, angP[s16], angP, ph[:, 0:1], tb17a, tb17b, s16)
                    P.tt("dve", E1n[:], E1n[:], magP[s16], ALU.mult, [E1n, magP], [E1n])
                    yield
                    yield from sin_of(E2n, angP[s16], angP, ph[:, 1:2], tb17a, tb17b, s16)
                    P.tt("dve", E2n[:], E2n[:], magP[s16], ALU.mult, [E2n, magP], [E2n])
                    yield
                    thr_b17 = thr[:].unsqueeze(2).to_broadcast([128, 32, 17])
                    lrdt_b17 = lrdt[:].unsqueeze(2).to_broadcast([128, 32, 17])
                    tauP_b = tauP[:].unsqueeze(1).to_broadcast([128, 32, 17])
                    P.tt("dve", angP[:], thr_b17, tauP_b, ALU.mult, [thr, tauP], [angP])
                    P.tt("dve", tb17a[:], lrdt_b17, tauP_b, ALU.mult, [lrdt, tauP], [tb17a])
                    yield
                    P.act(magP[:], tb17a[:], AF.Exp, [tb17a], [magP])
                    yield
                    yield from sin_of(F1, angP[:], angP, ph[:, 2:3], tb17a, tb17b)
                    P.tt("dve", F1[:], F1[:], magP[:], ALU.mult, [F1, magP], [F1])
                    yield
                    yield from sin_of(F2, angP[:], angP, ph[:, 3:4], tb17a, tb17b)
                    P.tt("dve", F2[:], F2[:], magP[:], ALU.mult, [F2, magP], [F2])
                    yield
                    P.ts("dve", t32c[:], thr[:], 16.0, None, ALU.mult, None, [thr], [t32c])
                    reduce_pm_pi(phir[:], phir, t32c[:], t32a[:], t32a, [t32c])
                    yield
                    P.ts("dve", t32c[:], thr[:], 15.0, None, ALU.mult, None, [thr], [t32c])
                    reduce_pm_pi(th15r[:], th15r, t32c[:], t32a[:], t32a, [t32c])
                    yield
                    P.tt("dve", Tb0[:], Tb[:, :, 0:128], halo_s[:].unsqueeze(1).to_broadcast([128, 8, 128]), ALU.add, [Tb, halo_s], [Tb0])
                    if DBG:
                        P.dma("sp", dbg_tb, Tb[:].rearrange("p h j -> p (h j)"), r=[Tb], w=[Buf()], sem="S_dbg")

                bg = background()

                with ExitStack() as scA:
                    wu = sb(scA, "wu", [128, 8, 512], BF16)
                    srcu = w_in.rearrange("(k p) n -> p k n", p=128)
                    for k2 in range(2):
                        P.dma("pool", wu[:, 4 * k2:4 * k2 + 4, :], srcu[:, 4 * k2:4 * k2 + 4, 768:1280], w=[wu], sem="S_wu")
                    P.seal("S_wu", [wu])
                    hT2 = [sb(scA, f"hT{i}", [128, 8, 1024], BF16) for i in range(2)]
                    xts = Ring([sb(scA, f"xtA{i}", [128, D], F32) for i in range(3)])
                    xss = Ring([sb(scA, f"xsA{i}", [128, D], BF16) for i in range(2)])
                    sss = Ring([sb(scA, f"ssA{i}", [128, 1], F32) for i in range(4)])
                    tmps = Ring([sb(scA, f"tmA{i}", [128, 1], F32) for i in range(4)])
                    rsts = Ring([sb(scA, f"rsA{i}", [128, 1], F32) for i in range(4)])

                    def proj_steps(hb):
                        rb, jh = divmod(hb, 2)
                        hTc = hT2[hb % 2]
                        for s in range(16):
                            bank = pfr5.next()
                            for k in range(8):
                                P.mm(bank[0:64, :], hTc[:, k, s:1024:16], wu[:, k, :], k == 0, k == 7, [hTc, wu], [bank])
                            P.cp("act" if s % 2 == 0 else "dve", X[jh * 64:(jh + 1) * 64, rb, :, s, :],
                                 bank[0:64, :].rearrange("p (g c) -> p g c", g=32), [bank], [X])
                            yield

                    prev = None
                    ntile = 0
                    for hb in range(4):
                        pend = None
                        hTc = hT2[hb % 2]
                        for i in range(8):
                            xt = xts.next()
                            tok0 = hb * 1024 + i * 128
                            P.dma("sp", xt[:], x[tok0:tok0 + 128, :], w=[xt], sem=f"S_xA{ntile % 3}")
                            ntile += 1
                            next(bg, None)
                            if ntile == 1:
                                next(bg, None)
                            st2 = norm_transpose(xt, gpre, xss.next(), sss.next(), tmps.next(), rsts.next(),
                                                 hTc[:, :, i * 128:(i + 1) * 128], hTc, "act", defer=True)
                            if pend is not None:
                                pend()
                            pend = st2
                            if prev is not None:
                                next(prev, None)
                                next(prev, None)
                            next(bg, None)
                        pend()
                        if prev is not None:
                            for _ in prev:
                                pass
                        prev = proj_steps(hb)
                        if hb == 1:
                            dep = Buf()
                            dep.w = dict(hTc.b.w)

                            def wload_late(dst_tl, src, nk, sem):
                                srcv = src.rearrange("(k p) n -> p k n", p=128)
                                for k in range(nk):
                                    P.dma("pool", dst_tl[:, k, :], srcv[:, k, :], r=[dep], w=[dst_tl], sem=sem)
                                P.seal(sem, [dst_tl])
                            wload_late(wab, w_ab, 4, "S_wab")
                            wload_late(wsb_, w_sb, 4, "S_wsb")
                            wload_late(wglu, w_glu, 4, "S_wglu")
                            wload_late(wout, w_out, 8, "S_wout")
                    for _ in prev:
                        pass
                    for _ in bg:
                        pass
                    if DBG:
                        P.dma("sp", dbg_X, X[:].rearrange("p a g s c -> p (a g s c)"), r=[X], w=[Buf()], sem="S_dbg")
                    P.barrier()
                scS.close()
                if STOP == 3:
                    return True
                conv_jobs = []
                for k in range(8):
                    conv_jobs.append((wffi_b[k * 128:(k + 1) * 128, :], w_ffi[k * 128:(k + 1) * 128, :], cvb_i, "S_cvi"))
                for k in range(8):
                    conv_jobs.append((wffo_b[k * 512:(k + 1) * 512, :], w_ffo[k * 512:(k + 1) * 512, :], cvb_o, "S_cvo"))

                with ExitStack() as scB:
                    zc2 = sb(scB, "zc2", [128, 16, 128], BF16)
                    tmA = sb(scB, "tmA", [128, 4, 17, 16], F32)
                    tmB = sb(scB, "tmB", [128, 4, 17, 16], F32)
                    tmC = sb(scB, "tmC", [128, 4, 16, 16], F32)
                    tmD = sb(scB, "tmD", [128, 4, 16, 16], F32)
                    Pst = sb(scB, "Pst", [128, 4, 16, 16], BF16)
                    Qst = sb(scB, "Qst", [128, 4, 17, 16], BF16)
                    Toep = sb(scB, "Toep", [128, 4, 512], BF16)
                    Kt = sb(scB, "Kt", [16, 4, 256], BF16)
                    PT = sb(scB, "PT", [128, 4, 2, 128], BF16)
                    UT = [sb(scB, f"UT{rb}", [128, 4, 2, 128], BF16) for rb in range(2)]
                    psi = sb(scB, "psi", [128, 4, 256], F32)
                    C1p = sb(scB, "C1p", [128, 4, 256], F32)
                    S1p = sb(scB, "S1p", [128, 4, 256], F32)
                    tbA = sb(scB, "tbA", [128, 4, 256], F32)
                    tbB = sb(scB, "tbB", [128, 4, 256], F32)
                    C1 = sb(scB, "C1", [128, 4, 128], F32)
                    S1 = sb(scB, "S1", [128, 4, 128], F32)
                    Zt = sb(scB, "Zt", [128, 4, 256], F32)
                    Zts = sb(scB, "Zts", [128, 4, 256], F32)
                    Sts = sb(scB, "Sts", [128, 4, 256], F32)
                    Sb = sb(scB, "Sb", [128, 4, 128], BF16)
                    St = psi
                    rtab = C1p
                    pz = [pf[1], pf[2]]
                    pzs = [pf[3], pf[4]]
                    Jpos = sb(scB, "Jpos", [128, 256], F32)
                    P.ts("dve", Jpos[:], Jv[:], 1.0, None, ALU.min, None, [Jv], [Jpos])
                    sgn = ph[:, 5:6]
                    P.op("dve", lambda e: e.memset(Toep[:], 0.0), [], [Toep])

                    def sincos(cos_tl, cos_ap, sin_tl, sin_ap, ang_ap, ang_tl, tA_tl, tB_tl, tA_ap, tB_ap):
                        reduce_pm_pi(tB_ap, tB_tl, ang_ap, tA_ap, tA_tl, [ang_tl])
                        P.act(sin_ap, tB_ap, AF.Sin, [tB_tl], [sin_tl])
                        P.act(tA_ap, tB_ap, AF.Abs, [tB_tl], [tA_tl])
                        P.act(cos_ap, tA_ap, AF.Sin, [tA_tl], [cos_tl], bias=halfpi[:], scale=-1.0)

                    for gb in range(8):
                        g0 = gb * 4
                        gs = slice(g0, g0 + 4)
                        P.tt("dve", tmA[:], Y1[:, gs, :].unsqueeze(2).to_broadcast([128, 4, 17, 16]),
                             F1[:, gs, :].unsqueeze(3).to_broadcast([128, 4, 17, 16]), ALU.mult, [Y1, F1], [tmA])
                        P.tt("dve", tmB[:], Y2[:, gs, :].unsqueeze(2).to_broadcast([128, 4, 17, 16]),
                             F2[:, gs, :].unsqueeze(3).to_broadcast([128, 4, 17, 16]), ALU.mult, [Y2, F2], [tmB])
                        P.tt("dve", Qst[:], tmA[:], tmB[:], ALU.add, [tmA, tmB], [Qst])
                        P.tt("dve", tmC[:], X1[:, gs, :].unsqueeze(2).to_broadcast([128, 4, 16, 16]),
                             E1n[:, gs, :].unsqueeze(3).to_broadcast([128, 4, 16, 16]), ALU.mult, [X1, E1n], [tmC])
                        P.tt("dve", tmD[:], X2[:, gs, :].unsqueeze(2).to_broadcast([128, 4, 16, 16]),
                             E2n[:, gs, :].unsqueeze(3).to_broadcast([128, 4, 16, 16]), ALU.mult, [X2, E2n], [tmD])
                        P.tt("dve", Pst[:], tmC[:], tmD[:], ALU.add, [tmC, tmD], [Pst])
                        for gp in range(2):
                            bank = pf[0] if gp == 0 else pf[5]
                            for gl in range(2):
                                g = gp * 2 + gl
                                P.mm(bank[0:16, gl * 256:(gl + 1) * 256], Pst[:, g, 0, :],
                                     Qst[:, g, 0:16, :].rearrange("p a b -> p (a b)"), True, True, [Pst, Qst], [bank], signal=(gl == 1))
                            P.cp("act", Kt[:, gp * 2:gp * 2 + 2, :].rearrange("p g n -> p (g n)"), bank[0:16, :], [bank], [Kt])
                        bank = pb[0]
                        for g in range(4):
                            for q in range(2):
                                P.tp(bank[:, (g * 2 + q) * 128:(g * 2 + q + 1) * 128],
                                     Pst[:, g, 8 * q:8 * q + 8, :].rearrange("p a b -> p (a b)"), ident[:], [Pst, ident], [bank],
                                     signal=(g == 3 and q == 1))
                        P.cp("act", PT[:].rearrange("p g q m -> p (g q m)"), bank[:, :], [bank], [PT])
                        for rb in range(2):
                            bank = pb[1]
                            for g in range(4):
                                for q in range(2):
                                    P.tp(bank[:, (g * 2 + q) * 128:(g * 2 + q + 1) * 128],
                                         X[:, rb, g0 + g, 8 * q:8 * q + 8, :].rearrange("p a b -> p (a b)"), ident[:], [X, ident], [bank],
                                         signal=(g == 3 and q == 1))
                            P.cp("act", UT[rb][:].rearrange("p g q m -> p (g q m)"), bank[:, :], [bank], [UT[rb]])
                        P.tt("dve", psi[:], phir[:, gs].unsqueeze(2).to_broadcast([128, 4, 256]),
                             Jv[:].unsqueeze(1).to_broadcast([128, 4, 256]), ALU.mult, [phir, Jv], [psi])
                        sincos(C1, C1[:], S1, S1[:], psi[:, :, 127:255], psi, Zt, Zts, Zt[:, :, 0:128], Zts[:, :, 0:128])
                        P.tt("dve", Kt[:, :, 0:16], Kt[:, :, 0:16], ddiag[:, gs, :], ALU.add, [Kt, ddiag], [Kt])
                        for q in range(2):
                            for sl in range(8):
                                sft = 8 * q + sl
                                P.dma("sp", Toep[16 * sl:16 * sl + 16, :, q * 256 + 16 * sft:(q + 1) * 256],
                                      Kt[:, :, 0:256 - 16 * sft], r=[Kt], w=[Toep], sem="S_toep")
                        P.seal("S_toep", [Toep])
                        P.tt("dve", psi[:], psi[:], th15r[:, gs].unsqueeze(2).to_broadcast([128, 4, 256]), ALU.subtract, [psi, th15r], [psi])
                        sincos(C1p, C1p[:], S1p, S1p[:], psi[:], psi, tbA, tbB, tbA[:], tbB[:])
                        for rb in range(2):
                            for g in range(4):
                                cs = slice(g * 128, (g + 1) * 128)
                                for q in range(2):
                                    P.mm(pz[rb][:, cs], PT[:, g, q, :], UT[rb][:, g, q, :], q == 0, q == 1, [PT, UT[rb]], [pz[rb]],
                                         signal=(q == 1 and g == 3))
                            for g in range(4):
                                cs = slice(g * 128, (g + 1) * 128)
                                for q in range(2):
                                    P.mm(pzs[rb][0:64, cs], PT[:, g, q, 64:128], UT[rb][:, g, q, :], q == 0, q == 1, [PT, UT[rb]], [pzs[rb]],
                                         signal=False)
                                for q in range(2):
                                    P.mm(pzs[rb][64:128, cs], PT[:, g, q, 0:64], UT[rb][:, g, q, :], q == 0, q == 1, [PT, UT[rb]], [pzs[rb]],
                                         signal=(q == 1 and g == 3))
                        m15b = mag15[:, gs].unsqueeze(2).to_broadcast([128, 4, 128])
                        P.tt("dve", C1[:], C1[:], m15b, ALU.mult, [C1, mag15], [C1])
                        P.tt("dve", S1[:], S1[:], m15b, ALU.mult, [S1, mag15], [S1])
                        for rb in range(2):
                            js = slice(rb * 128, (rb + 1) * 128)
                            zv = pz[rb][:, :].rearrange("p (g j) -> p g j", g=4)
                            zsv = pzs[rb][:, :].rearrange("p (g j) -> p g j", g=4)
                            P.tt("dve", tbA[:, :, js], zv, C1p[:, :, js], ALU.mult, [pz[rb], C1p], [tbA])
                            P.stt("dve", tbB[:, :, js], zsv, sgn, S1p[:, :, js], ALU.mult, ALU.mult, [pzs[rb], S1p, ph], [tbB])
                            P.tt("dve", Zt[:, :, js], tbA[:, :, js], tbB[:, :, js], ALU.add, [tbA, tbB], [Zt])
                            P.tt("dve", tbA[:, :, js], zsv, C1p[:, :, js], ALU.mult, [pzs[rb], C1p], [tbA])
                            P.stt("dve", tbB[:, :, js], zv, sgn, S1p[:, :, js], ALU.mult, ALU.mult, [pz[rb], S1p, ph], [tbB])
                            P.tt("dve", Zts[:, :, js], tbA[:, :, js], tbB[:, :, js], ALU.subtract, [tbA, tbB], [Zts])
                        P.tt("dve", rtab[:], r16[:, gs].unsqueeze(2).to_broadcast([128, 4, 256]),
                             Jpos[:].unsqueeze(1).to_broadcast([128, 4, 256]), ALU.mult, [r16, Jpos], [rtab])
                        P.scan(St[:].rearrange("p g j -> p (g j)"), rtab[:].rearrange("p g j -> p (g j)"),
                               Zt[:].rearrange("p g j -> p (g j)"), [rtab, Zt], [St])
                        P.scan(Sts[:].rearrange("p g j -> p (g j)"), rtab[:].rearrange("p g j -> p (g j)"),
                               Zts[:].rearrange("p g j -> p (g j)"), [rtab, Zts], [Sts])
                        P.tt("dve", tbA[:, :, 0:128], St[:, :, 127:255], C1[:], ALU.mult, [St, C1], [tbA])
                        P.stt("dve", tbB[:, :, 0:128], Sts[:, :, 127:255], sgn, S1[:], ALU.mult, ALU.mult, [Sts, S1, ph], [tbB])
                        P.tt("dve", Sb[:], tbA[:, :, 0:128], tbB[:, :, 0:128], ALU.subtract, [tbA, tbB], [Sb])
                        depc = Buf()
                        depc.w = dict(Sb.b.w)
                        for _ in range(2):
                            o_, i_, cb_, sm_ = conv_jobs.pop(0)
                            P.dma("pool", o_, i_, r=[depc], w=[cb_], sem=sm_)
                        if gb == 7:
                            P.seal("S_cvi", [cvb_i])
                            P.seal("S_cvo", [cvb_o])
                        for g in range(4):
                            bank = pf[0] if g % 2 == 0 else pf[5]
                            cs = slice(0, 256)
                            P.mm(bank[:, cs], UT[1][:, g, 0, :], Toep[:, g, 0:256], True, False, [UT[1], Toep], [bank], signal=False)
                            P.mm(bank[:, cs], UT[1][:, g, 1, :], Toep[:, g, 256:512], False, False, [UT[1], Toep], [bank], signal=False)
                            P.mm(bank[:, cs], Sb[:, g, :], Qst[:, g, 1:17, :].rearrange("p a b -> p (a b)"), False, True, [Sb, Qst], [bank])
                            gl = (g0 + g) % 8
                            P.act(zc2[:, :, gl * 16:(gl + 1) * 16], bank[:, cs].rearrange("p (s c) -> p s c", s=16),
                                  AF.Gelu_apprx_tanh, [bank], [zc2])
                        if gb % 2 == 1:
                            cc = gb // 2
                            for sh in range(2):
                                bank = pb[sh]
                                for s8 in range(8):
                                    s = sh * 8 + s8
                                    P.tp(bank[:, s8 * 128:(s8 + 1) * 128], zc2[:, s, :], ident[:], [zc2, ident], [bank], signal=(s8 == 7))
                                P.cp("act",
                                     zT[:, cc, :].rearrange("p (j s) -> p s j", s=16)[:, sh * 8:(sh + 1) * 8, :],
                                     bank[:, :].rearrange("p (s j) -> p s j", s=8), [bank], [zT])
                    if DBG:
                        P.dma("sp", dbg_zT, zT[:], r=[zT], w=[Buf()], sem="S_dbg")
                    P.barrier()
                    if STOP == 4:
                        return True

            with ExitStack() as scC:
                wq = sb(scC, "wq", [128, 8, 512], BF16)
                wk = sb(scC, "wk", [128, 8, 2, 128], BF16)
                wv = sb(scC, "wv", [128, 8, 128], BF16)
                wg = sb(scC, "wg", [128, 8, 2048], BF16)
                srcw = w_in.rearrange("(k p) n -> p k n", p=128)
                P.dma("pool", wv[:, :, :], srcw[:, :, 640:768], w=[wv], sem="S_wv")
                for kv in range(2):
                    for hh in range(2):
                        P.dma("pool", wk[:, :, kv, hh * 64:(hh + 1) * 64], srcw[:, :, 512 + kv * 64:512 + (kv + 1) * 64], w=[wk], sem="S_wk")
                P.seal("S_wv", [wv])
                P.seal("S_wk", [wk])
                for k2 in range(2):
                    P.dma("pool", wq[:, 4 * k2:4 * k2 + 4, :], srcw[:, 4 * k2:4 * k2 + 4, 0:512], w=[wq], sem="S_wq")
                P.seal("S_wq", [wq])
                depq = Buf()
                depq.w = dict(wq.b.w)
                for k2 in range(4):
                    P.dma("pool", wg[:, 2 * k2:2 * k2 + 2, :], srcw[:, 2 * k2:2 * k2 + 2, 1280:3328], r=[depq], w=[wg], sem="S_wg")
                P.seal("S_wg", [wg])

                xg = [sb(scC, f"xg{i}", [128, D], F32) for i in range(2)]
                xgr = Ring(xg)
                xrr = [sb(scC, f"xr{i}", [128, D], F32) for i in range(3)]
                xr_i = [0]
                xss = Ring([sb(scC, f"xsC{i}", [128, D], BF16) for i in range(2)])
                sss = Ring([sb(scC, f"ssC{i}", [128, 1], F32) for i in range(4)])
                tmps = Ring([sb(scC, f"tmC{i}", [128, 1], F32) for i in range(4)])
                rsts = Ring([sb(scC, f"rsC{i}", [128, 1], F32) for i in range(4)])
                hTg = sb(scC, "hTg", [128, 8, 512], BF16)
                qT = sb(scC, "qT", [128, 4, 512], BF16)
                kT = sb(scC, "kT", [128, 2, 640], BF16)
                vtok = sb(scC, "vtok", [128, 5, 128], BF16)
                attT = sb(scC, "attT", [128, 4, 512], BF16)
                zg = sb(scC, "zg", [128, 4, 512], BF16)
                sgt = Ring([sb(scC, f"sgt{i}", [128, 512], BF16) for i in range(3)])
                mT = sb(scC, "mT", [128, 8, 512], BF16)
                slog2 = [sb(scC, f"slog{i}", [128, 4, 256], F32) for i in range(2)]
                Pm2 = [sb(scC, f"Pm{i}", [128, 4, 256], BF16) for i in range(2)]
                PTs2 = [sb(scC, f"PTs{i}", [128, 4, 2, 128], BF16) for i in range(2)]
                attn2 = [sb(scC, f"attn{i}", [128, 512], BF16) for i in range(2)]
                mx2 = [sb(scC, f"mx{i}", [128, 4], F32) for i in range(2)]
                nmx2 = [sb(scC, f"nmx{i}", [128, 4], F32) for i in range(2)]
                rs2 = [sb(scC, f"rs{i}", [128, 4], F32) for i in range(4)]
                es2 = [sb(scC, f"es{i}", [128, 4], F32) for i in range(4)]
                dn2 = [sb(scC, f"dn{i}", [128, 4], F32) for i in range(4)]
                t1s = Ring([sb(scC, f"t1s{i}", [128, 512], F32) for i in range(2)])
                t2s = Ring([sb(scC, f"t2s{i}", [128, 512], F32) for i in range(1)])
                x1t = Ring([sb(scC, f"x1t{i}", [128, D], F32) for i in range(1)])
                ssp = Ring([sb(scC, f"ssp{i}", [128, 2], F32) for i in range(2)])
                ss1 = Ring([sb(scC, f"ss1{i}", [128, 1], F32) for i in range(2)])

                def proj_kv(src_hT_ap_fn, ntiles, hT_tl, kcol0, vt0):
                    n = ntiles * 128
                    for kv in range(2):
                        bank = pfr.next()
                        for k in range(8):
                            P.mm(bank[:, 0:n], wk[:, k, kv, :], src_hT_ap_fn(k, 0, n), k == 0, k == 7, [wk, hT_tl], [bank])
                        P.cp("act", kT[:, kv, kcol0:kcol0 + n], bank[:, 0:n], [bank], [kT])
                    for t in range(ntiles):
                        bank = pfr.next()
                        for k in range(8):
                            P.mm(bank[:, 0:128], src_hT_ap_fn(k, t * 128, 128), wv[:, k, :], k == 0, k == 7, [hT_tl, wv], [bank])
                        P.cp("dve", vtok[:, vt0 + t, :], bank[:, 0:128], [bank], [vtok])

                xt = xgr.next()
                P.dma("sp", xt[:], x[1920:2048, :], w=[xt], sem="S_xC0")
                norm_transpose(xt, gpre, xss.next(), sss.next(), tmps.next(), rsts.next(),
                               hTg[:, :, 0:128], hTg, "act")
                proj_kv(lambda k, c0, n: hTg[:, k, c0:c0 + n], 1, hTg, 0, 0)

                if STOP == 41:
                    return True
                xc_cnt = [1]

                def norm_tile_C(Gn, t):
                    xt = xgr.next()
                    tok0 = 2048 + Gn * 512 + t * 128
                    P.dma("sp", xt[:], x[tok0:tok0 + 128, :], w=[xt], sem=f"S_xC{xc_cnt[0] % 2}")
                    xc_cnt[0] += 1
                    return norm_transpose(xt, gpre, xss.next(), sss.next(), tmps.next(), rsts.next(),
                                          hTg[:, :, t * 128:(t + 1) * 128], hTg, "act", defer=True)

                for G in range(4):
                    m0 = G * 512
                    if G == 0:
                        pend = None
                        for t in range(4):
                            st2 = norm_tile_C(0, t)
                            if pend is not None:
                                pend()
                            pend = st2
                        pend()
                    if STOP == 42 and G == 0:
                        return True
                    for c in range(4):
                        bank = pfr.next()
                        for k in range(8):
                            P.mm(bank[:, :], wq[:, k, c * 128:(c + 1) * 128], hTg[:, k, :], k == 0, k == 7, [wq, hTg], [bank])
                        P.cp("act" if c % 2 == 0 else "dve", qT[:, c, :], bank[:, :], [bank], [qT])
                    proj_kv(lambda k, c0, n: hTg[:, k, c0:c0 + n], 4, hTg, 128, 1)
                    if STOP == 43 and G == 0:
                        return True
                    def S1(u):
                        t, hh = divmod(u, 2)
                        p = u % 2
                        for hl in range(4):
                            h = hh * 4 + hl
                            bank = pf[2 * p + (hl % 2)]
                            half = hl // 2
                            hs = slice(64 * (hl % 2), 64 * (hl % 2) + 64)
                            P.mm(bank[:, half * 256:half * 256 + 256], qT[hs, h // 2, t * 128:(t + 1) * 128],
                                 kT[hs, hh, t * 128:t * 128 + 256], True, True, [qT, kT], [bank], signal=(hl >= 2))

                    def S2(u):
                        t, hh = divmod(u, 2)
                        p = u % 2
                        first = (G == 0 and t == 0)
                        slog, Pm, mx, nmx, rs, es_ = slog2[p], Pm2[p], mx2[p], nmx2[p], rs2[u % 4], es2[u % 4]
                        for par in range(2):
                            bank = pf[2 * p + par]
                            bv = bank[:, :].rearrange("p (h j) -> p h j", h=2)
                            lsl = slice(par, par + 3, 2)
                            gsl = slice(hh * 4 + par, hh * 4 + par + 3, 2)
                            if first:
                                P.stt("dve", slog[:, lsl, 0:128], bv[:, :, 0:128], 0.125, Tb0[:, gsl, :],
                                      ALU.mult, ALU.add, [bank, Tb0], [slog])
                                P.stt("dve", slog[:, lsl, 128:256], bv[:, :, 128:256], 0.125, Tb[:, gsl, 128:256],
                                      ALU.mult, ALU.add, [bank, Tb], [slog])
                            else:
                                P.stt("dve", slog[:, lsl, :], bv, 0.125, Tb[:, gsl, :], ALU.mult, ALU.add, [bank, Tb], [slog])
                        sk = sinks[:, hh * 4:hh * 4 + 4]
                        P.rmax(mx[:], slog[:], [slog], [mx])
                        P.tt("dve", mx[:], mx[:], sk, ALU.max, [mx, sinks], [mx])
                        P.ts("dve", nmx[:], mx[:], -1.0, None, ALU.mult, None, [mx], [nmx])
                        P.tt("dve", es_[:], sk, mx[:], ALU.subtract, [sinks, mx], [es_])
                        for hl in range(4):
                            P.act(Pm[:, hl, :], slog[:, hl, :], AF.Exp, [slog, nmx], [Pm, rs], bias=nmx[:, hl:hl + 1], accum=rs[:, hl:hl + 1])
                        P.act(es_[:], es_[:], AF.Exp, [es_], [es_])

                    def S3(u):
                        p = u % 2
                        bank = pb[p]
                        for hl in range(4):
                            for kc in range(2):
                                P.tp(bank[:, (hl * 2 + kc) * 128:(hl * 2 + kc + 1) * 128], Pm2[p][:, hl, kc * 128:(kc + 1) * 128], ident[:],
                                     [Pm2[p], ident], [bank], signal=(hl == 3 and kc == 1))
                        P.cp("act", PTs2[p][:].rearrange("p h k q -> p (h k q)"), bank[:, :], [bank], [PTs2[p]])

                    def S4(u):
                        t, hh = divmod(u, 2)
                        p = u % 2
                        po = pf[4 + p]
                        dn, rs, es_ = dn2[u % 4], rs2[u % 4], es2[u % 4]
                        P.tt("dve", dn[:], rs[:], es_[:], ALU.add, [rs, es_], [dn])
                        P.recip(dn[:], dn[:], [dn], [dn])
                        for hl in range(4):
                            for kc in range(2):
                                P.mm(po[:, hl * 64:(hl + 1) * 64], PTs2[p][:, hl, kc, :], vtok[:, t + kc, hh * 64:hh * 64 + 64],
                                     kc == 0, kc == 1, [PTs2[p], vtok], [po], signal=(hl == 3 and kc == 1))
                        at = attn2[t % 2]
                        P.tt("dve", at[:, hh * 256:(hh + 1) * 256].rearrange("p (h d) -> p h d", h=4),
                             po[:, 0:256].rearrange("p (h d) -> p h d", h=4),
                             dn2[u % 4][:].unsqueeze(2).to_broadcast([128, 4, 64]), ALU.mult, [po, dn2[u % 4]], [at])
                        if hh == 1:
                            bank = pb[p]
                            for c in range(4):
                                P.tp(bank[:, c * 128:(c + 1) * 128], at[:, c * 128:(c + 1) * 128], ident[:], [at, ident], [bank], signal=(c == 3))
                            P.cp("act", attT[:, :, t * 128:(t + 1) * 128], bank[:, 0:512].rearrange("p (c j) -> p c j", c=4), [bank], [attT])

                    NU = 8
                    for i in range(NU + 3):
                        if i < NU:
                            S1(i)
                        if 0 <= i - 1 < NU:
                            S2(i - 1)
                        if 0 <= i - 2 < NU:
                            S3(i - 2)
                        if 0 <= i - 3 < NU:
                            S4(i - 3)
                    if STOP == 44 and G == 0:
                        return True
                    P.cp("dve", kT[:, :, 0:128], kT[:, :, 512:640], [kT], [kT])
                    P.cp("dve", vtok[:, 0, :], vtok[:, 4, :], [vtok], [vtok])
                    if STOP == 45 and G == 0:
                        return True
                    for co in range(4):
                        bank = pfr.next()
                        for c in range(4):
                            P.mm(bank[:, :], wglu[:, c, co * 128:(co + 1) * 128], zT[:, c, m0:m0 + 512], c == 0, c == 3, [wglu, zT], [bank])
                        sg = sgt.next()
                        P.act(sg[:], bank[:, :], AF.Sigmoid, [bank], [sg])
                        P.tt("dve", zg[:, co, :], zT[:, co, m0:m0 + 512], sg[:], ALU.mult, [zT, sg], [zg])
                    if STOP == 46 and G == 0:
                        return True
                    for fo in range(8):
                        fs = slice(fo * 128, (fo + 1) * 128)
                        bga = pfr.next()
                        for k in range(8):
                            P.mm(bga[:, :], wg[:, k, fo * 128:(fo + 1) * 128], hTg[:, k, :], k == 0, k == 7, [wg, hTg], [bga])
                        sga = sgt.next()
                        P.act(sga[:], bga[:, :], AF.Sigmoid, [bga], [sga])
                        bgs = pfr.next()
                        for k in range(8):
                            P.mm(bgs[:, :], wg[:, k, 1024 + fo * 128:1024 + (fo + 1) * 128], hTg[:, k, :], k == 0, k == 7, [wg, hTg], [bgs])
                        sgs = sgt.next()
                        P.act(sgs[:], bgs[:, :], AF.Sigmoid, [bgs], [sgs])
                        ba = pfr.next()
                        for c in range(4):
                            P.mm(ba[:, :], wab[:, c, fs], attT[:, c, :], c == 0, c == 3, [wab, attT], [ba])
                        t1 = t1s.next()
                        P.tt("dve", t1[:], ba[:, :], sga[:], ALU.mult, [ba, sga], [t1])
                        bs = pfr.next()
                        for c in range(4):
                            P.mm(bs[:, :], wsb_[:, c, fs], zg[:, c, :], c == 0, c == 3, [wsb_, zg], [bs])
                        t2 = t2s.next()
                        P.tt("dve", t2[:], bs[:, :], sgs[:], ALU.mult, [bs, sgs], [t2])
                        P.tt("dve", mT[:, fo, :], t1[:], t2[:], ALU.add, [t1, t2], [mT])
                    if STOP == 47 and G == 0:
                        return True
                    xrs, xr_sem = {}, {}

                    def load_xr(t):
                        idx = xr_i[0] % 3
                        xr_i[0] += 1
                        xr = xrr[idx]
                        tok0 = 2048 + m0 + t * 128
                        P.dma("sp", xr[:], x[tok0:tok0 + 128, :], w=[xr], sem=f"S_xR{idx}")
                        xrs[t] = xr
                        xr_sem[t] = idx

                    load_xr(0)
                    load_xr(1)
                    pendn = None
                    for t in range(4):
                        if G + 1 < 4:
                            st2n = norm_tile_C(G + 1, t)
                            if pendn is not None:
                                pendn()
                            pendn = st2n
                        b0, b1 = pfr.next(), pfr.next()
                        for half, bank in ((0, b0), (1, b1)):
                            for k in range(8):
                                P.mm(bank[:, :], mT[:, k, t * 128:(t + 1) * 128], wout[:, k, half * 512:(half + 1) * 512], k == 0, k == 7,
                                     [mT, wout], [bank])
                        sp_ = ssp.next()
                        j0, j1 = sgt.next(), sgt.next()
                        P.act(j0[:], b0[:, :], AF.Square, [b0], [j0, sp_], accum=sp_[:, 0:1])
                        P.act(j1[:], b1[:, :], AF.Square, [b1], [j1, sp_], accum=sp_[:, 1:2])
                        s1 = ss1.next()
                        P.tt("dve", s1[:], sp_[:, 0:1], sp_[:, 1:2], ALU.add, [sp_], [s1])
                        tm, rsd = tmps.next(), rsts.next()
                        rstd_from_ss(s1[:], s1, rsd, tm)
                        xo = x1t.next()
                        for half, bank in ((0, b0), (1, b1)):
                            hs_ = slice(half * 512, (half + 1) * 512)
                            P.stt("dve", xo[:, hs_], bank[:, :], rsd[:], gpost[:, hs_], ALU.mult, ALU.mult, [bank, rsd, gpost], [xo])
                        xr = xrs[t]
                        P.tt("dve", xr[:], xo[:], xr[:], ALU.add, [xo, xr], [xr])
                        r0 = m0 + t * 128
                        P.dma("pool", x1scr[r0:r0 + 128, :], xr[:], r=[xr], w=[Buf()], sem=f"S_x1w{xr_sem[t]}")
                        if t + 2 < 4:
                            load_xr(t + 2)
                    if pendn is not None:
                        pendn()
                P.barrier()
                if STOP == 5:
                    return True

            scABC.close()
            with ExitStack() as scD:
                wffi = sb(scD, "wffi", [128, 8, 4096], BF16)
                wffo = sb(scD, "wffo", [128, 32, D], BF16)
                wffi_bv = wffi_b.rearrange("(k p) n -> p k n", p=128)
                wffo_bv = wffo_b.rearrange("(k p) n -> p k n", p=128)
                wffi_q = [Buf() for _ in range(4)]
                for q4 in range(4):
                    P.dma("act", wffi[:, :, q4 * 1024:(q4 + 1) * 1024], wffi_bv[:, :, q4 * 1024:(q4 + 1) * 1024],
                          r=[cvb_i], w=[wffi_q[q4]], sem=f"S_wffi{q4}")
                wffo_p = [Buf() for _ in range(8)]
                depw = Buf()
                depw.w = dict(wffi_q[3].w)
                for k4 in range(8):
                    P.dma("pool", wffo[:, 4 * k4:4 * k4 + 4, :], wffo_bv[:, 4 * k4:4 * k4 + 4, :], r=[cvb_o, depw], w=[wffo_p[k4]],
                          sem=f"S_wffo{k4}")
                g2pre = sb(scD, "g2pre", [128, D], F32)
                g2post = sb(scD, "g2post", [128, D], F32)
                P.dma("sp", g2pre[:], gains[2, :, :], w=[g2pre], sem="S_c3")
                P.dma("sp", g2post[:], gains[3, :, :], w=[g2post], sem="S_c3")
                P.seal("S_c3", [g2pre, g2post])
                x1g = Ring([sb(scD, f"x1g{i}", [128, D], F32) for i in range(4)])
                xss = Ring([sb(scD, f"xsD{i}", [128, D], BF16) for i in range(2)])
                sss = Ring([sb(scD, f"ssD{i}", [128, 1], F32) for i in range(6)])
                tmps = Ring([sb(scD, f"tmD{i}", [128, 1], F32) for i in range(4)])
                rsts = Ring([sb(scD, f"rsD{i}", [128, 1], F32) for i in range(4)])
                h2T = sb(scD, "h2T", [128, 8, 256], BF16)
                ffT = sb(scD, "ffT", [128, 32, 256], BF16)
                rl = Ring([sb(scD, f"rl{i}", [128, 512], BF16) for i in range(2)])
                ot = Ring([sb(scD, f"ot{i}", [128, D], F32) for i in range(2)])
                ssp = Ring([sb(scD, f"sspD{i}", [128, 2], F32) for i in range(2)])
                ss1 = Ring([sb(scD, f"ss1D{i}", [128, 1], F32) for i in range(2)])
                outb = Buf()
                h2Tb = sb(scD, "h2Tb", [128, 8, 256], BF16)
                h2T2 = [h2T, h2Tb]
                xtiles = {}

                def norm_group(Gn, defer):
                    st2s = []
                    tl = []
                    for t in range(2):
                        xt = x1g.next()
                        r0 = Gn * 256 + t * 128
                        P.dma("sp", xt[:], x1scr[r0:r0 + 128, :], w=[xt], sem=f"S_xD{(Gn * 2 + t) % 4}")
                        tl.append(xt)
                        hdst = h2T2[Gn % 2]
                        st2s.append(norm_transpose(xt, g2pre, xss.next(), sss.next(), tmps.next(), rsts.next(),
                                                   hdst[:, :, t * 128:(t + 1) * 128], hdst, "act", defer=True))
                    xtiles[Gn] = tl
                    if defer:
                        return st2s
                    for f in st2s:
                        f()
                    return []

                norm_group(0, False)
                for G in range(8):
                    m0 = G * 256
                    xs_g = xtiles[G]
                    hcur = h2T2[G % 2]
                    for fp in range(16):
                        bank = pfr.next()
                        for j in range(2):
                            fc = fp * 2 + j
                            for k in range(8):
                                P.mm(bank[:, j * 256:(j + 1) * 256], wffi[:, k, fc * 128:(fc + 1) * 128], hcur[:, k, :], k == 0, k == 7,
                                     [wffi_q[fc // 8], hcur], [bank], signal=(j == 1 and k == 7))
                        r_ = rl.next()
                        P.act(r_[:], bank[:, :], AF.Relu, [bank], [r_])
                        P.tt("dve", ffT[:, fp * 2:fp * 2 + 2, :], r_[:].rearrange("p (a n) -> p a n", a=2),
                             r_[:].rearrange("p (a n) -> p a n", a=2), ALU.mult, [r_], [ffT])
                    nxt = norm_group(G + 1, True) if G + 1 < 8 else []
                    for t in range(2):
                        b0, b1 = pfr.next(), pfr.next()
                        for half, bank in ((0, b0), (1, b1)):
                            for fc in range(32):
                                P.mm(bank[:, :], ffT[:, fc, t * 128:(t + 1) * 128], wffo[:, fc, half * 512:(half + 1) * 512], fc == 0, fc == 31,
                                     [ffT, wffo_p[fc // 4]], [bank])
                        if t == 0:
                            for f in nxt:
                                f()
                        sp_ = ssp.next()
                        j0, j1 = rl.next(), rl.next()
                        P.act(j0[:], b0[:, :], AF.Square, [b0], [j0, sp_], accum=sp_[:, 0:1])
                        P.act(j1[:], b1[:, :], AF.Square, [b1], [j1, sp_], accum=sp_[:, 1:2])
                        s1 = ss1.next()
                        P.tt("dve", s1[:], sp_[:, 0:1], sp_[:, 1:2], ALU.add, [sp_], [s1])
                        tm, rsd = tmps.next(), rsts.next()
                        rstd_from_ss(s1[:], s1, rsd, tm)
                        xo = ot.next()
                        for half, bank in ((0, b0), (1, b1)):
                            hs_ = slice(half * 512, (half + 1) * 512)
                            P.stt("dve", xo[:, hs_], bank[:, :], rsd[:], g2post[:, hs_], ALU.mult, ALU.mult, [bank, rsd, g2post], [xo])
                        P.tt("dve", xo[:], xo[:], xs_g[t][:], ALU.add, [xo, xs_g[t]], [xo])
                        r0 = m0 + t * 128
                        P.dma("pool", out_d[r0:r0 + 128, :], xo[:], r=[xo], w=[outb], sem=f"S_ow{(G * 2 + t) % 2}")
                P.barrier()


            return False

        if body():
            scS.close()
            scABC.close()
            P.barrier()

        block = es.enter_context(nc.Block())
        P.emit(block)
    return nc


def _bucket(d):
    d = np.asarray(d)
    df = np.maximum(d, 1).astype(np.float32)
    large = 16 + (np.log(df / np.float32(16)) / np.float32(math.log(128 / 16)) * np.float32(16)).astype(np.int32)
    large = np.minimum(large, 31)
    return np.where(d < 16, d, large)


def _constants():
    c = {}
    c["ident"] = np.eye(128, dtype=np.float32)
    oh = np.zeros((33, 384), np.float32)
    for m in range(383):
        d = 255 - m
        if 0 <= d < 128:
            oh[int(_bucket(d)), m] = 1.0
        else:
            oh[32, m] = 1.0
    c["oh"] = oh
    cm = np.zeros((128, 2, 256), np.float32)
    iq = np.zeros((128, 2, 256), np.float32)
    for p in range(128):
        sl, cc = divmod(p, 16)
        for q in range(2):
            s = 8 * q + sl
            for s2 in range(16):
                if s2 >= s:
                    cm[p, q, s2 * 16:(s2 + 1) * 16] = 1.0
            iq[p, q, s * 16 + cc] = 1.0
    c["cmask"], c["identq"] = cm, iq
    ph = np.zeros((128, 8), np.float32)
    top, bot = slice(0, 64), slice(64, 128)
    ph[top, 0], ph[bot, 0] = PI / 2, 0.0
    ph[top, 1], ph[bot, 1] = PI, PI / 2
    ph[top, 2], ph[bot, 2] = PI / 2, PI
    ph[top, 3], ph[bot, 3] = PI, 3 * PI / 2
    ph[top, 4], ph[bot, 4] = 0.0, PI
    ph[top, 5], ph[bot, 5] = 1.0, -1.0
    c["ph"] = ph
    c["tauN"] = np.tile(-np.arange(16, dtype=np.float32), (128, 1))
    c["tauP"] = np.tile(np.arange(17, dtype=np.float32), (128, 1))
    c["Jv"] = np.tile(np.arange(256, dtype=np.float32), (128, 1))
    return c


def _prep_inputs(inp):
    f = lambda a: np.ascontiguousarray(np.asarray(a, dtype=np.float32))
    shared = dict(_constants())
    shared["w_in"] = f(inp["w_in"][0])
    shared["w_glu"] = f(inp["w_glu"][0])
    shared["w_ab"] = f(inp["w_attn_branch"][0])
    shared["w_sb"] = f(inp["w_ssm_branch"][0])
    shared["w_out"] = f(inp["w_out"][0])
    shared["w_ffi"] = f(inp["w_ff_in"][0])
    shared["w_ffo"] = f(inp["w_ff_out"][0])
    gains = np.stack([inp["norm_mix_pre"][0], inp["norm_mix_post"][0], inp["norm_mlp_pre"][0], inp["norm_mlp_post"][0]])
    shared["gains"] = f(np.broadcast_to(np.asarray(gains, np.float32)[:, None, :], (4, 128, D)))
    relb = np.empty((33, 8, 128), np.float32)
    relb[:32] = np.asarray(inp["rel_bias"], np.float32)[:, :, None]
    relb[32] = NEG
    shared["relb"] = relb
    shared["sinks"] = f(np.broadcast_to(np.asarray(inp["sinks"][0], np.float32)[None, :], (128, 8)))
    dup = lambda a: f(np.concatenate([a, a], axis=0))
    shared["lamr"] = dup(np.asarray(inp["lam_re"][0], np.float32).T)
    shared["lami"] = dup(np.asarray(inp["lam_im"][0], np.float32).T)
    shared["ldt"] = f(np.broadcast_to(np.asarray(inp["log_dt"][0], np.float32)[None, :], (128, 32)))
    shared["bre"] = dup(np.transpose(np.asarray(inp["b_re"][0], np.float32), (1, 0, 2)))
    shared["bim"] = dup(np.transpose(np.asarray(inp["b_im"][0], np.float32), (1, 0, 2)))
    shared["cre"] = dup(np.transpose(np.asarray(inp["c_re"][0], np.float32), (2, 0, 1)))
    shared["cim"] = dup(np.transpose(np.asarray(inp["c_im"][0], np.float32), (2, 0, 1)))
    dsk = np.asarray(inp["d_skip"][0], np.float32).reshape(32, 16)
    ddiag = np.zeros((16, 32, 16), np.float32)
    for c_ in range(16):
        ddiag[c_, :, c_] = dsk[:, c_]
    shared["ddiag"] = ddiag
    shared["dcol"] = f(np.tile(dsk.T, (8, 1)))
    xs = np.asarray(inp["x"], np.float32)
    in_maps = []
    for core in range(8):
        b, half = divmod(core, 2)
        m = dict(shared)
        if half == 0:
            xc = np.concatenate([np.zeros((2048, D), np.float32), xs[b, :2048]], axis=0)
            halo = np.full((128, 128), NEG, np.float32)
        else:
            xc = xs[b]
            halo = np.zeros((128, 128), np.float32)
        m["x"] = np.ascontiguousarray(xc)
        m["halo"] = halo
        in_maps.append(m)
    return in_maps


_NC_CACHE = {}


def kernel(**inputs):
    in_maps = _prep_inputs(inputs)
    if "nc" not in _NC_CACHE:
        _NC_CACHE["nc"] = build()
    nc = _NC_CACHE["nc"]
    res = run_bass_kernel_spmd(nc, in_maps, core_ids=list(range(8)))
    out = np.empty((4, 4096, D), np.float32)
    for core in range(8):
        b, half = divmod(core, 2)
        out[b, half * 2048:(half + 1) * 2048] = np.asarray(res.results[core]["out"], np.float32)
    return out
```

```python
import math
from contextlib import ExitStack

import numpy as np
import concourse.bass as bass
import concourse.mybir as mybir
from concourse.bass_utils import run_bass_kernel_spmd

F32 = mybir.dt.float32
BF16 = mybir.dt.bfloat16
AF = mybir.ActivationFunctionType
ALU = mybir.AluOpType
AX = mybir.AxisListType

NEG = -30000.0
EPS = 1e-6
PI = math.pi
TWO_PI = 2.0 * math.pi
MAGIC = 12582912.0
D = 1024
NTOK = 2048
DBG = False
STOP = 99


class _Stop(Exception):
    pass


class Buf:
    __slots__ = ("w", "r")

    def __init__(self):
        self.w = {}
        self.r = {}


class Tl:
    def __init__(self, t):
        self.t = t
        self.b = Buf()

    def __getitem__(self, k):
        return self.t[k]


class Prog:
    ENGS = ("pe", "act", "dve", "pool", "sp")

    def __init__(self, nc, es):
        self.nc, self.es = nc, es
        self.ops = {e: [] for e in self.ENGS}
        self.sem, self.cnt = {}, {}
        self.waited = {e: {} for e in self.ENGS}
        for e in ("pe", "act", "dve", "pool"):
            self.mksem("E_" + e)

    def mksem(self, name):
        if name not in self.sem:
            self.sem[name] = self.es.enter_context(self.nc.semaphore(name))
            self.cnt[name] = 0
        return name

    def _waits(self, eng, r, w, is_dma):
        own = None if is_dma else "E_" + eng
        need = {}

        def add(sem, val):
            if val > need.get(sem, 0):
                need[sem] = val

        for b in r:
            for sem, val in b.w.items():
                if sem == own and eng == "pe":
                    continue
                add(sem, val)
        for b in w:
            for sem, val in b.w.items():
                if sem != own or eng != "pe":
                    add(sem, val)
            for sem, val in b.r.items():
                if sem != own or eng != "pe":
                    add(sem, val)
        out = []
        wd = self.waited[eng]
        for sem, val in need.items():
            if wd.get(sem, 0) < val:
                out.append((sem, val))
                wd[sem] = val
        return out

    @staticmethod
    def _bufs(xs):
        return [x.b if isinstance(x, Tl) else x for x in xs]

    def op(self, eng, fn, r=(), w=(), signal=True):
        r, w = self._bufs(r), self._bufs(w)
        waits = self._waits(eng, r, w, False)
        name = "E_" + eng
        if signal:
            self.cnt[name] += 1
            val = self.cnt[name]
            inc = (name, 1)
        else:
            val = self.cnt[name] + 1
            inc = None
        self.ops[eng].append((waits, fn, inc))
        for b in r:
            b.r[name] = max(b.r.get(name, 0), val)
        for b in w:
            b.w = {name: val}
            b.r = {}

    def dma(self, q, out, in_, r=(), w=(), sem=None, **kw):
        r, w = self._bufs(r), self._bufs(w)
        self.mksem(sem)
        waits = [(sm, v) for (sm, v) in self._waits(q, r, w, True) if sm != sem]
        self.cnt[sem] += 16
        val = self.cnt[sem]
        self.ops[q].append((waits, lambda e: e.dma_start(out=out, in_=in_, **kw), (sem, 16)))
        for b in r:
            b.r[sem] = max(b.r.get(sem, 0), val)
        for b in w:
            b.w = {sem: val}
            b.r = {}

    def seal(self, sem, bufs):
        for b in self._bufs(bufs):
            b.w = {sem: self.cnt[sem]}

    def barrier(self):
        for eng in self.ENGS:
            waits = []
            for sem, c in self.cnt.items():
                if sem == "E_" + eng:
                    continue
                if c > self.waited[eng].get(sem, 0):
                    waits.append((sem, c))
                    self.waited[eng][sem] = c
            self.ops[eng].append((waits, None, None))

    def emit(self, block):
        def mk(eng):
            def f(e):
                for waits, fn, inc in self.ops[eng]:
                    for sem, val in waits:
                        e.wait_ge(self.sem[sem], val)
                    if fn is not None:
                        ins = fn(e)
                        if inc is not None:
                            ins.then_inc(self.sem[inc[0]], inc[1])

            return f

        block.tensor(mk("pe"))
        block.scalar(mk("act"))
        block.vector(mk("dve"))
        block.gpsimd(mk("pool"))
        block.sync(mk("sp"))

    def mm(self, out, lhsT, rhs, start, stop, r, w, signal=None):
        if signal is None:
            signal = stop
        self.op("pe", lambda e: e.matmul(out, lhsT=lhsT, rhs=rhs, start=start, stop=stop), r, w, signal)

    def tp(self, out, in_, ident, r, w, signal=True):
        self.op("pe", lambda e: e.transpose(out=out, in_=in_, identity=ident), r, w, signal)

    def act(self, out, in_, func, r, w, bias=None, scale=None, accum=None):
        kw = {}
        if bias is not None:
            kw["bias"] = bias
        if scale is not None:
            kw["scale"] = scale
        if accum is not None:
            kw["accum_out"] = accum
        self.op("act", lambda e: e.activation(out=out, in_=in_, func=func, **kw), r, w)

    def tt(self, eng, out, in0, in1, op, r, w):
        self.op(eng, lambda e: e.tensor_tensor(out=out, in0=in0, in1=in1, op=op), r, w)

    def ts(self, eng, out, in0, s1, s2, op0, op1, r, w):
        if s2 is None:
            self.op(eng, lambda e: e.tensor_scalar(out=out, in0=in0, scalar1=s1, scalar2=None, op0=op0), r, w)
        else:
            self.op(eng, lambda e: e.tensor_scalar(out=out, in0=in0, scalar1=s1, scalar2=s2, op0=op0, op1=op1), r, w)

    def stt(self, eng, out, in0, scalar, in1, op0, op1, r, w):
        self.op(eng, lambda e: e.scalar_tensor_tensor(out=out, in0=in0, scalar=scalar, in1=in1, op0=op0, op1=op1), r, w)

    def cp(self, eng, out, in_, r, w):
        if eng == "act":
            self.op(eng, lambda e: e.copy(out=out, in_=in_), r, w)
        else:
            self.op(eng, lambda e: e.tensor_copy(out=out, in_=in_), r, w)

    def rmax(self, out, in_, r, w):
        self.op("dve", lambda e: e.tensor_reduce(out=out, in_=in_, axis=AX.X, op=ALU.max), r, w)

    def recip(self, out, in_, r, w):
        self.op("dve", lambda e: e.reciprocal(out=out, in_=in_), r, w)

    def scan(self, out, d0, d1, r, w):
        self.op("dve", lambda e: e.tensor_tensor_scan(out=out, data0=d0, data1=d1, initial=0.0, op0=ALU.mult, op1=ALU.add), r, w)


class Ring:
    def __init__(self, items):
        self.items, self.i = items, 0

    def next(self):
        x = self.items[self.i % len(self.items)]
        self.i += 1
        return x


def build():
    nc = bass.Bass("TRN2", target_bir_lowering=False)

    def din(name, shape, dt=F32):
        return nc.dram_tensor(name, list(shape), dt, kind="ExternalInput").ap()

    x = din("x", [4096, D])
    w_in = din("w_in", [D, 3328])
    w_glu = din("w_glu", [512, 512])
    w_ab = din("w_ab", [512, D])
    w_sb = din("w_sb", [512, D])
    w_out = din("w_out", [D, D])
    w_ffi = din("w_ffi", [D, 4096])
    w_ffo = din("w_ffo", [4096, D])
    gains = din("gains", [4, 128, D])
    relb = din("relb", [33, 8, 128])
    sinks_d = din("sinks", [128, 8])
    oh_d = din("oh", [33, 384])
    halo_d = din("halo", [128, 128])
    ident_d = din("ident", [128, 128])
    cmask_d = din("cmask", [128, 2, 256])
    identq_d = din("identq", [128, 2, 256])
    ph_d = din("ph", [128, 8])
    tauN_d = din("tauN", [128, 16])
    tauP_d = din("tauP", [128, 17])
    Jv_d = din("Jv", [128, 256])
    lamr_d = din("lamr", [128, 32])
    lami_d = din("lami", [128, 32])
    ldt_d = din("ldt", [128, 32])
    bre_d = din("bre", [128, 32, 16])
    bim_d = din("bim", [128, 32, 16])
    cre_d = din("cre", [128, 32, 16])
    cim_d = din("cim", [128, 32, 16])
    dcol_d = din("dcol", [128, 32])
    ddiag_d = din("ddiag", [16, 32, 16])
    out_d = nc.dram_tensor("out", [NTOK, D], F32, kind="ExternalOutput").ap()
    x1scr = nc.dram_tensor("x1scr", [NTOK, D], F32).ap()
    wffi_b = nc.dram_tensor("wffi_b", [D, 4096], BF16).ap()
    wffo_b = nc.dram_tensor("wffo_b", [4096, D], BF16).ap()
    tbscr_t = nc.dram_tensor("tbscr", [8, 128 * 383], F32)
    tbscr = tbscr_t.ap()
    if DBG:
        dbg_zT = nc.dram_tensor("dbg_zT", [128, 4, NTOK], BF16, kind="ExternalOutput").ap()
        dbg_X = nc.dram_tensor("dbg_X", [128, 2 * 32 * 16 * 16], BF16, kind="ExternalOutput").ap()
        dbg_tb = nc.dram_tensor("dbg_tb", [128, 8 * 256], F32, kind="ExternalOutput").ap()

    with ExitStack() as es:
        P = Prog(nc, es)
        global _LASTP
        _LASTP = P

        def sb(scope, name, shape, dt):
            return Tl(scope.enter_context(nc.sbuf_tensor("sb_" + name, list(shape), dt)))

        def psum(name, shape, dt):
            return Tl(es.enter_context(nc.psum_tensor(name, list(shape), dt)))

        pf = [psum(f"pf{i}", [128, 512], F32) for i in range(6)]
        pb = [psum(f"pb{i}", [128, 1024], BF16) for i in range(2)]
        pfr = Ring(pf)
        pfr5 = Ring(pf[0:5])
        pbr = Ring(pb)

        ident = sb(es, "ident", [128, 128], BF16)
        epsc = sb(es, "epsc", [128, 1], F32)
        halfpi = sb(es, "halfpi", [128, 1], F32)
        scABC = es.enter_context(ExitStack())
        scS = ExitStack()
        ident_f = sb(scABC, "ident_f", [128, 128], F32)
        Tb = sb(scABC, "Tb", [128, 8, 256], F32)
        Tb0 = sb(scABC, "Tb0", [128, 8, 128], F32)
        sinks = sb(scABC, "sinks", [128, 8], F32)
        gpre = sb(scABC, "gpre", [128, D], F32)
        gpost = sb(scABC, "gpost", [128, D], F32)
        zT = sb(scABC, "zT", [128, 4, NTOK], BF16)
        wab = sb(scABC, "wab", [128, 4, D], BF16)
        wsb_ = sb(scABC, "wsb", [128, 4, D], BF16)
        wglu = sb(scABC, "wglu", [128, 4, 512], BF16)
        wout = sb(scABC, "wout", [128, 8, D], BF16)

        cvb_i, cvb_o = Buf(), Buf()

        def body():
            P.dma("sp", ident_f[:], ident_d[:, :], w=[ident_f], sem="S_c0")
            P.dma("sp", sinks[:], sinks_d[:, :], w=[sinks], sem="S_c0")
            P.dma("sp", gpre[:], gains[0, :, :], w=[gpre], sem="S_c0")
            P.dma("sp", gpost[:], gains[1, :, :], w=[gpost], sem="S_c0")
            P.seal("S_c0", [ident_f, sinks, gpre, gpost])
            P.cp("dve", ident[:], ident_f[:], [ident_f], [ident])
            P.op("dve", lambda e: e.memset(epsc[:], EPS), [], [epsc])
            P.op("dve", lambda e: e.memset(halfpi[:], PI / 2), [], [halfpi])

            def wload(dst_tl, dst_ap_fn, src, nk, sem):
                srcv = src.rearrange("(k p) n -> p k n", p=128)
                for k in range(nk):
                    P.dma("pool", dst_ap_fn(k), srcv[:, k, :], w=[dst_tl], sem=sem)
                P.seal(sem, [dst_tl])

            def rstd_from_ss(ss_ap, ss_tl, rstd_tl, tmp_tl):
                P.act(tmp_tl[:], ss_ap, AF.Ln, [ss_tl, epsc], [tmp_tl], bias=epsc[:], scale=1.0 / D)
                P.act(rstd_tl[:], tmp_tl[:], AF.Exp, [tmp_tl], [rstd_tl], scale=-0.5)

            def norm_transpose(xt, g_tl, xs, ss, tmp, rstd, dst_aps, dst_tl, evac_eng, defer=False):
                P.act(xs[:], xt[:], AF.Square, [xt], [xs, ss], accum=ss[:])
                rstd_from_ss(ss[:], ss, rstd, tmp)
                P.stt("dve", xs[:], xt[:], rstd[:], g_tl[:], ALU.mult, ALU.mult, [xt, rstd, g_tl], [xs])

                def stage2():
                    bank = pbr.next()
                    for k in range(8):
                        P.tp(bank[:, k * 128:(k + 1) * 128], xs[:, k * 128:(k + 1) * 128], ident[:], [xs, ident], [bank], signal=(k == 7))
                    P.cp(evac_eng, dst_aps, bank[:, :].rearrange("p (k j) -> p k j", k=8), [bank], [dst_tl])

                if defer:
                    return stage2
                stage2()

            with ExitStack() as scAB:
                def small(name, shape, dt=F32):
                    return sb(scAB, name, shape, dt)

                ph = small("ph", [128, 8]); Jv = small("Jv", [128, 256])
                Y1 = small("Y1", [128, 32, 16]); Y2 = small("Y2", [128, 32, 16])
                dcol = small("dcol", [128, 32])
                ddiag = small("ddiag", [16, 32, 16])
                phir = small("phir", [128, 32]); th15r = small("th15r", [128, 32])
                mag15 = small("mag15", [128, 32]); r16 = small("r16", [128, 32])
                X1 = small("X1", [128, 32, 16]); X2 = small("X2", [128, 32, 16])
                E1n = small("E1n", [128, 32, 16]); E2n = small("E2n", [128, 32, 16])
                F1 = small("F1", [128, 32, 17]); F2 = small("F2", [128, 32, 17])
                X = small("X", [128, 2, 32, 16, 16], BF16)

                scS.__enter__()

                def smallt(name, shape, dt=F32):
                    return sb(scS, name, shape, dt)

                lr = smallt("lr", [128, 32]); li = smallt("li", [128, 32]); ldt = smallt("ldt", [128, 32])
                tauN = smallt("tauN", [128, 16]); tauP = smallt("tauP", [128, 17])
                Br = smallt("Br", [128, 32, 16]); Bi = smallt("Bi", [128, 32, 16])
                dt_ = smallt("dt_", [128, 32]); lrdt = smallt("lrdt", [128, 32]); th = smallt("th", [128, 32])
                thr = smallt("thr", [128, 32]); t32a = smallt("t32a", [128, 32]); t32b = smallt("t32b", [128, 32])
                t32c = smallt("t32c", [128, 32])
                mag1 = smallt("mag1", [128, 32]); cth = smallt("cth", [128, 32]); sth = smallt("sth", [128, 32])
                ar = smallt("ar", [128, 32]); ai = smallt("ai", [128, 32]); nr = smallt("nr", [128, 32])
                den = smallt("den", [128, 32]); fre = smallt("fre", [128, 32]); fim = smallt("fim", [128, 32])
                magP = smallt("magP", [128, 32, 17]); angP = smallt("angP", [128, 32, 17])
                tb17a = smallt("tb17a", [128, 32, 17]); tb17b = smallt("tb17b", [128, 32, 17])
                relb_s = smallt("relb_s", [33, 8, 128])
                oh_s = smallt("oh_s", [33, 384])
                halo_s = smallt("halo_s", [128, 128])
                rrow = [smallt(f"rrow{i}", [128, 383]) for i in range(2)]

                def reduce_pm_pi(dst_ap, dst_tl, src_ap, tmp_ap, tmp_tl, rlist, eng="dve"):
                    P.ts(eng, tmp_ap, src_ap, 1.0 / TWO_PI, MAGIC, ALU.mult, ALU.add, rlist, [tmp_tl])
                    P.ts(eng, tmp_ap, tmp_ap, -MAGIC, None, ALU.add, None, [tmp_tl], [tmp_tl])
                    P.stt(eng, dst_ap, tmp_ap, -TWO_PI, src_ap, ALU.mult, ALU.add, [tmp_tl] + rlist, [dst_tl])
                    P.ts(eng, dst_ap, dst_ap, 3.14159, -3.14159, ALU.min, ALU.max, [dst_tl], [dst_tl])

                def sin_of(dst, ang_ap, ang_tl, phase, tA, tB, sl=None):
                    sl = sl if sl is not None else (slice(None),) * 3
                    rl = [ang_tl] + ([ph] if not isinstance(phase, float) else [])
                    P.ts("dve", tA[sl], ang_ap, phase, None, ALU.add, None, rl, [tA])
                    reduce_pm_pi(tB[sl], tB, tA[sl], dst[sl], dst, [tA])
                    yield
                    P.act(dst[sl], tB[sl], AF.Sin, [tB], [dst])
                    yield

                def background():
                    P.dma("sp", relb_s[:], relb[:, :, :], w=[relb_s], sem="S_c1")
                    P.dma("sp", oh_s[:], oh_d[:, :], w=[oh_s], sem="S_c1")
                    P.dma("sp", halo_s[:], halo_d[:, :], w=[halo_s], sem="S_c1")
                    P.seal("S_c1", [relb_s, oh_s, halo_s])
                    for tl, src in ((lr, lamr_d), (li, lami_d), (ldt, ldt_d), (ph, ph_d), (tauN, tauN_d), (tauP, tauP_d),
                                    (Jv, Jv_d), (dcol, dcol_d)):
                        P.dma("sp", tl[:], src[:, :], w=[tl], sem="S_c2")
                    for tl, src in ((Br, bre_d), (Bi, bim_d), (Y1, cre_d), (Y2, cim_d), (ddiag, ddiag_d)):
                        P.dma("sp", tl[:], src[:, :, :], w=[tl], sem="S_c2")
                    P.seal("S_c2", [lr, li, ldt, ph, tauN, tauP, Jv, dcol, Br, Bi, Y1, Y2, ddiag])
                    yield
                    scrb = [Buf() for _ in range(8)]
                    for h in range(8):
                        bank = pf[5] if h % 2 == 0 else pf[4]
                        P.mm(bank[:, 0:383], relb_s[:, h, :], oh_s[:, 0:383], True, True, [relb_s, oh_s], [bank])
                        rr = rrow[h % 2]
                        P.cp("dve", rr[:], bank[:, 0:383], [bank], [rr])
                        dst = bass.AP(tbscr_t, h * 128 * 383, [[383, 128], [1, 383]])
                        P.dma("pool", dst, rr[:], r=[rr], w=[scrb[h]], sem=f"S_tbw{h % 2}")
                        yield
                    for h in range(8):
                        src = bass.AP(tbscr_t, h * 128 * 383 + 127, [[382, 128], [1, 256]])
                        P.dma("pool", Tb[:, h, :], src, r=[scrb[h]], w=[Tb], sem="S_tbr")
                    P.seal("S_tbr", [Tb])
                    yield
                    P.act(dt_[:], ldt[:], AF.Exp, [ldt], [dt_])
                    yield
                    P.tt("dve", lrdt[:], lr[:], dt_[:], ALU.mult, [lr, dt_], [lrdt])
                    P.tt("dve", th[:], li[:], dt_[:], ALU.mult, [li, dt_], [th])
                    reduce_pm_pi(thr[:], thr, th[:], t32a[:], t32a, [th])
                    yield
                    P.act(mag1[:], lrdt[:], AF.Exp, [lrdt], [mag1])
                    P.act(mag15[:], lrdt[:], AF.Exp, [lrdt], [mag15], scale=15.0)
                    P.act(r16[:], lrdt[:], AF.Exp, [lrdt], [r16], scale=16.0)
                    s2 = (slice(None), slice(None))
                    yield from sin_of(cth, thr[:], thr, PI / 2, t32a, t32b, s2)
                    yield
                    yield from sin_of(sth, thr[:], thr, 0.0, t32a, t32b, s2)
                    yield
                    P.tt("dve", ar[:], mag1[:], cth[:], ALU.mult, [mag1, cth], [ar])
                    P.tt("dve", ai[:], mag1[:], sth[:], ALU.mult, [mag1, sth], [ai])
                    P.ts("dve", nr[:], ar[:], -1.0, None, ALU.add, None, [ar], [nr])
                    P.tt("dve", den[:], lr[:], lr[:], ALU.mult, [lr], [den])
                    yield
                    P.tt("dve", t32a[:], li[:], li[:], ALU.mult, [li], [t32a])
                    P.tt("dve", den[:], den[:], t32a[:], ALU.add, [den, t32a], [den])
                    P.recip(den[:], den[:], [den], [den])
                    yield
                    P.tt("dve", t32a[:], nr[:], lr[:], ALU.mult, [nr, lr], [t32a])
                    P.tt("dve", t32b[:], ai[:], li[:], ALU.mult, [ai, li], [t32b])
                    P.tt("dve", t32a[:], t32a[:], t32b[:], ALU.add, [t32a, t32b], [t32a])
                    P.tt("dve", fre[:], t32a[:], den[:], ALU.mult, [t32a, den], [fre])
                    yield
                    P.tt("dve", t32b[:], ai[:], lr[:], ALU.mult, [ai, lr], [t32b])
                    P.tt("dve", t32c[:], nr[:], li[:], ALU.mult, [nr, li], [t32c])
                    P.tt("dve", t32b[:], t32b[:], t32c[:], ALU.subtract, [t32b, t32c], [t32b])
                    P.tt("dve", fim[:], t32b[:], den[:], ALU.mult, [t32b, den], [fim])
                    yield
                    s16 = (slice(None), slice(None), slice(0, 16))
                    fre_b = fre[:].unsqueeze(2).to_broadcast([128, 32, 16])
                    fim_b = fim[:].unsqueeze(2).to_broadcast([128, 32, 16])
                    P.tt("dve", tb17a[s16], Br[:], fre_b, ALU.mult, [Br, fre], [tb17a])
                    P.tt("dve", tb17b[s16], Bi[:], fim_b, ALU.mult, [Bi, fim], [tb17b])
                    P.tt("dve", X1[:], tb17a[s16], tb17b[s16], ALU.subtract, [tb17a, tb17b], [X1])
                    yield
                    P.tt("dve", tb17a[s16], Bi[:], fre_b, ALU.mult, [Bi, fre], [tb17a])
                    P.tt("dve", tb17b[s16], Br[:], fim_b, ALU.mult, [Br, fim], [tb17b])
                    P.tt("dve", X2[:], tb17a[s16], tb17b[s16], ALU.add, [tb17a, tb17b], [X2])
                    yield
                    thr_b16 = thr[:].unsqueeze(2).to_broadcast([128, 32, 16])
                    lrdt_b16 = lrdt[:].unsqueeze(2).to_broadcast([128, 32, 16])
                    tauN_b = tauN[:].unsqueeze(1).to_broadcast([128, 32, 16])
                    P.tt("dve", angP[s16], thr_b16, tauN_b, ALU.mult, [thr, tauN], [angP])
                    P.tt("dve", tb17a[s16], lrdt_b16, tauN_b, ALU.mult, [lrdt, tauN], [tb17a])
                    yield
                    P.act(magP[s16], tb17a[s16], AF.Exp, [tb17a], [magP])
                    yield
                    yield from sin_of(E1n, angP[s16], angP, ph[:, 0:1], tb17a, tb17b, s16)
                    P.tt("dve", E1n[:], E1n[:], magP[s16], ALU.mult, [E1n, magP], [E1n])
                    yield
                    yield from sin_of(E2n, angP[s16], angP, ph[:, 1:2], tb17a, tb17b, s16)
                    P.tt("dve", E2n[:], E2n[:], magP[s16], ALU.mult, [E2n, magP], [E2n])
                    yield
                    thr_b17 = thr[:].unsqueeze(2).to_broadcast([128, 32, 17])
                    lrdt_b17 = lrdt[:].unsqueeze(2).to_broadcast([128, 32, 17])
                    tauP_b = tauP[:].unsqueeze(1).to_broadcast([128, 32, 17])
                    P.tt("dve", angP[:], thr_b17, tauP_b, ALU.mult, [thr, tauP], [angP])
                    P.tt("dve", tb17a[:], lrdt_b17, tauP_b, ALU.mult, [lrdt, tauP], [tb17a])
                    yield
                    P.act(magP[:], tb17a[:], AF.Exp, [tb17a], [magP])
                    yield
                    yield from sin_of(F1, angP[:], angP, ph[:, 2:3], tb17a, tb17b)
                    P.tt("dve", F1[:], F1[:], magP[:], ALU.mult, [F1, magP], [F1])
                    yield
                    yield from sin_of(F2, angP[:], angP, ph[:, 3:4], tb17a, tb17b)
                    P.tt("dve", F2[:], F2[:], magP[:], ALU.mult, [F2, magP], [F2])
                    yield
                    P.ts("dve", t32c[:], thr[:], 16.0, None, ALU.mult, None, [thr], [t32c])
                    reduce_pm_pi(phir[:], phir, t32c[:], t32a[:], t32a, [t32c])
                    yield
                    P.ts("dve", t32c[:], thr[:], 15.0, None, ALU.mult, None, [thr], [t32c])
                    reduce_pm_pi(th15r[:], th15r, t32c[:], t32a[:], t32a, [t32c])
                    yield
                    P.tt("dve", Tb0[:], Tb[:, :, 0:128], halo_s[:].unsqueeze(1).to_broadcast([128, 8, 128]), ALU.add, [Tb, halo_s], [Tb0])
                    if DBG:
                        P.dma("sp", dbg_tb, Tb[:].rearrange("p h j -> p (h j)"), r=[Tb], w=[Buf()], sem="S_dbg")

                bg = background()

                with ExitStack() as scA:
                    wu = sb(scA, "wu", [128, 8, 512], BF16)
                    srcu = w_in.rearrange("(k p) n -> p k n", p=128)
                    for k2 in range(2):
                        P.dma("pool", wu[:, 4 * k2:4 * k2 + 4, :], srcu[:, 4 * k2:4 * k2 + 4, 768:1280], w=[wu], sem="S_wu")
                    P.seal("S_wu", [wu])
                    hT2 = [sb(scA, f"hT{i}", [128, 8, 1024], BF16) for i in range(2)]
                    xts = Ring([sb(scA, f"xtA{i}", [128, D], F32) for i in range(3)])
                    xss = Ring([sb(scA, f"xsA{i}", [128, D], BF16) for i in range(2)])
                    sss = Ring([sb(scA, f"ssA{i}", [128, 1], F32) for i in range(4)])
                    tmps = Ring([sb(scA, f"tmA{i}", [128, 1], F32) for i in range(4)])
                    rsts = Ring([sb(scA, f"rsA{i}", [128, 1], F32) for i in range(4)])

                    def proj_steps(hb):
                        rb, jh = divmod(hb, 2)
                        hTc = hT2[hb % 2]
                        for s in range(16):
                            bank = pfr5.next()
                            for k in range(8):
                                P.mm(bank[0:64, :], hTc[:, k, s:1024:16], wu[:, k, :], k == 0, k == 7, [hTc, wu], [bank])
                            P.cp("act" if s % 2 == 0 else "dve", X[jh * 64:(jh + 1) * 64, rb, :, s, :],
                                 bank[0:64, :].rearrange("p (g c) -> p g c", g=32), [bank], [X])
                            yield

                    prev = None
                    ntile = 0
                    for hb in range(4):
                        pend = None
                        hTc = hT2[hb % 2]
                        for i in range(8):
                            xt = xts.next()
                            tok0 = hb * 1024 + i * 128
                            P.dma("sp", xt[:], x[tok0:tok0 + 128, :], w=[xt], sem=f"S_xA{ntile % 3}")
                            ntile += 1
                            next(bg, None)
                            if ntile == 1:
                                next(bg, None)
                            st2 = norm_transpose(xt, gpre, xss.next(), sss.next(), tmps.next(), rsts.next(),
                                                 hTc[:, :, i * 128:(i + 1) * 128], hTc, "act", defer=True)
                            if pend is not None:
                                pend()
                            pend = st2
                            if prev is not None:
                                next(prev, None)
                                next(prev, None)
                            next(bg, None)
                        pend()
                        if prev is not None:
                            for _ in prev:
                                pass
                        prev = proj_steps(hb)
                        if hb == 1:
                            dep = Buf()
                            dep.w = dict(hTc.b.w)

                            def wload_late(dst_tl, src, nk, sem):
                                srcv = src.rearrange("(k p) n -> p k n", p=128)
                                for k in range(nk):
                                    P.dma("pool", dst_tl[:, k, :], srcv[:, k, :], r=[dep], w=[dst_tl], sem=sem)
                                P.seal(sem, [dst_tl])
                            wload_late(wab, w_ab, 4, "S_wab")
                            wload_late(wsb_, w_sb, 4, "S_wsb")
                            wload_late(wglu, w_glu, 4, "S_wglu")
                            wload_late(wout, w_out, 8, "S_wout")
                    for _ in prev:
                        pass
                    for _ in bg:
                        pass
                    if DBG:
                        P.dma("sp", dbg_X, X[:].rearrange("p a g s c -> p (a g s c)"), r=[X], w=[Buf()], sem="S_dbg")
                    P.barrier()
                scS.close()
                if STOP == 3:
                    return True
                conv_jobs = []
                for k in range(8):
                    conv_jobs.append((wffi_b[k * 128:(k + 1) * 128, :], w_ffi[k * 128:(k + 1) * 128, :], cvb_i, "S_cvi"))
                for k in range(8):
                    conv_jobs.append((wffo_b[k * 512:(k + 1) * 512, :], w_ffo[k * 512:(k + 1) * 512, :], cvb_o, "S_cvo"))

                with ExitStack() as scB:
                    zc2 = sb(scB, "zc2", [128, 16, 128], BF16)
                    tmA = sb(scB, "tmA", [128, 4, 17, 16], F32)
                    tmB = sb(scB, "tmB", [128, 4, 17, 16], F32)
                    tmC = sb(scB, "tmC", [128, 4, 16, 16], F32)
                    tmD = sb(scB, "tmD", [128, 4, 16, 16], F32)
                    Pst = sb(scB, "Pst", [128, 4, 16, 16], BF16)
                    Qst = sb(scB, "Qst", [128, 4, 17, 16], BF16)
                    Toep = sb(scB, "Toep", [128, 4, 512], BF16)
                    Kt = sb(scB, "Kt", [16, 4, 256], BF16)
                    PT = sb(scB, "PT", [128, 4, 2, 128], BF16)
                    UT = [sb(scB, f"UT{rb}", [128, 4, 2, 128], BF16) for rb in range(2)]
                    psi = sb(scB, "psi", [128, 4, 256], F32)
                    C1p = sb(scB, "C1p", [128, 4, 256], F32)
                    S1p = sb(scB, "S1p", [128, 4, 256], F32)
                    tbA = sb(scB, "tbA", [128, 4, 256], F32)
                    tbB = sb(scB, "tbB", [128, 4, 256], F32)
                    C1 = sb(scB, "C1", [128, 4, 128], F32)
                    S1 = sb(scB, "S1", [128, 4, 128], F32)
                    Zt = sb(scB, "Zt", [128, 4, 256], F32)
                    Zts = sb(scB, "Zts", [128, 4, 256], F32)
                    Sts = sb(scB, "Sts", [128, 4, 256], F32)
                    Sb = sb(scB, "Sb", [128, 4, 128], BF16)
                    St = psi
                    rtab = C1p
                    pz = [pf[1], pf[2]]
                    pzs = [pf[3], pf[4]]
                    Jpos = sb(scB, "Jpos", [128, 256], F32)
                    P.ts("dve", Jpos[:], Jv[:], 1.0, None, ALU.min, None, [Jv], [Jpos])
                    sgn = ph[:, 5:6]
                    P.op("dve", lambda e: e.memset(Toep[:], 0.0), [], [Toep])

                    def sincos(cos_tl, cos_ap, sin_tl, sin_ap, ang_ap, ang_tl, tA_tl, tB_tl, tA_ap, tB_ap):
                        reduce_pm_pi(tB_ap, tB_tl, ang_ap, tA_ap, tA_tl, [ang_tl])
                        P.act(sin_ap, tB_ap, AF.Sin, [tB_tl], [sin_tl])
                        P.act(tA_ap, tB_ap, AF.Abs, [tB_tl], [tA_tl])
                        P.act(cos_ap, tA_ap, AF.Sin, [tA_tl], [cos_tl], bias=halfpi[:], scale=-1.0)

                    for gb in range(8):
                        g0 = gb * 4
                        gs = slice(g0, g0 + 4)
                        P.tt("dve", tmA[:], Y1[:, gs, :].unsqueeze(2).to_broadcast([128, 4, 17, 16]),
                             F1[:, gs, :].unsqueeze(3).to_broadcast([128, 4, 17, 16]), ALU.mult, [Y1, F1], [tmA])
                        P.tt("dve", tmB[:], Y2[:, gs, :].unsqueeze(2).to_broadcast([128, 4, 17, 16]),
                             F2[:, gs, :].unsqueeze(3).to_broadcast([128, 4, 17, 16]), ALU.mult, [Y2, F2], [tmB])
                        P.tt("dve", Qst[:], tmA[:], tmB[:], ALU.add, [tmA, tmB], [Qst])
                        P.tt("dve", tmC[:], X1[:, gs, :].unsqueeze(2).to_broadcast([128, 4, 16, 16]),
                             E1n[:, gs, :].unsqueeze(3).to_broadcast([128, 4, 16, 16]), ALU.mult, [X1, E1n], [tmC])
                        P.tt("dve", tmD[:], X2[:, gs, :].unsqueeze(2).to_broadcast([128, 4, 16, 16]),
                             E2n[:, gs, :].unsqueeze(3).to_broadcast([128, 4, 16, 16]), ALU.mult, [X2, E2n], [tmD])
                        P.tt("dve", Pst[:], tmC[:], tmD[:], ALU.add, [tmC, tmD], [Pst])
                        for gp in range(2):
                            bank = pf[0] if gp == 0 else pf[5]
                            for gl in range(2):
                                g = gp * 2 + gl
                                P.mm(bank[0:16, gl * 256:(gl + 1) * 256], Pst[:, g, 0, :],
                                     Qst[:, g, 0:16, :].rearrange("p a b -> p (a b)"), True, True, [Pst, Qst], [bank], signal=(gl == 1))
                            P.cp("act", Kt[:, gp * 2:gp * 2 + 2, :].rearrange("p g n -> p (g n)"), bank[0:16, :], [bank], [Kt])
                        bank = pb[0]
                        for g in range(4):
                            for q in range(2):
                                P.tp(bank[:, (g * 2 + q) * 128:(g * 2 + q + 1) * 128],
                                     Pst[:, g, 8 * q:8 * q + 8, :].rearrange("p a b -> p (a b)"), ident[:], [Pst, ident], [bank],
                                     signal=(g == 3 and q == 1))
                        P.cp("act", PT[:].rearrange("p g q m -> p (g q m)"), bank[:, :], [bank], [PT])
                        for rb in range(2):
                            bank = pb[1]
                            for g in range(4):
                                for q in range(2):
                                    P.tp(bank[:, (g * 2 + q) * 128:(g * 2 + q + 1) * 128],
                                         X[:, rb, g0 + g, 8 * q:8 * q + 8, :].rearrange("p a b -> p (a b)"), ident[:], [X, ident], [bank],
                                         signal=(g == 3 and q == 1))
                            P.cp("act", UT[rb][:].rearrange("p g q m -> p (g q m)"), bank[:, :], [bank], [UT[rb]])
                        P.tt("dve", psi[:], phir[:, gs].unsqueeze(2).to_broadcast([128, 4, 256]),
                             Jv[:].unsqueeze(1).to_broadcast([128, 4, 256]), ALU.mult, [phir, Jv], [psi])
                        sincos(C1, C1[:], S1, S1[:], psi[:, :, 127:255], psi, Zt, Zts, Zt[:, :, 0:128], Zts[:, :, 0:128])
                        P.tt("dve", Kt[:, :, 0:16], Kt[:, :, 0:16], ddiag[:, gs, :], ALU.add, [Kt, ddiag], [Kt])
                        for q in range(2):
                            for sl in range(8):
                                sft = 8 * q + sl
                                P.dma("sp", Toep[16 * sl:16 * sl + 16, :, q * 256 + 16 * sft:(q + 1) * 256],
                                      Kt[:, :, 0:256 - 16 * sft], r=[Kt], w=[Toep], sem="S_toep")
                        P.seal("S_toep", [Toep])
                        P.tt("dve", psi[:], psi[:], th15r[:, gs].unsqueeze(2).to_broadcast([128, 4, 256]), ALU.subtract, [psi, th15r], [psi])
                        sincos(C1p, C1p[:], S1p, S1p[:], psi[:], psi, tbA, tbB, tbA[:], tbB[:])
                        for rb in range(2):
                            for g in range(4):
                                cs = slice(g * 128, (g + 1) * 128)
                                for q in range(2):
                                    P.mm(pz[rb][:, cs], PT[:, g, q, :], UT[rb][:, g, q, :], q == 0, q == 1, [PT, UT[rb]], [pz[rb]],
                                         signal=(q == 1 and g == 3))
                            for g in range(4):
                                cs = slice(g * 128, (g + 1) * 128)
                                for q in range(2):
                                    P.mm(pzs[rb][0:64, cs], PT[:, g, q, 64:128], UT[rb][:, g, q, :], q == 0, q == 1, [PT, UT[rb]], [pzs[rb]],
                                         signal=False)
                                for q in range(2):
                                    P.mm(pzs[rb][64:128, cs], PT[:, g, q, 0:64], UT[rb][:, g, q, :], q == 0, q == 1, [PT, UT[rb]], [pzs[rb]],
                                         signal=(q == 1 and g == 3))
                        m15b = mag15[:, gs].unsqueeze(2).to_broadcast([128, 4, 128])
                        P.tt("dve", C1[:], C1[:], m15b, ALU.mult, [C1, mag15], [C1])
                        P.tt("dve", S1[:], S1[:], m15b, ALU.mult, [S1, mag15], [S1])
                        for rb in range(2):
                            js = slice(rb * 128, (rb + 1) * 128)
                            zv = pz[rb][:, :].rearrange("p (g j) -> p g j", g=4)
                            zsv = pzs[rb][:, :].rearrange("p (g j) -> p g j", g=4)
                            P.tt("dve", tbA[:, :, js], zv, C1p[:, :, js], ALU.mult, [pz[rb], C1p], [tbA])
                            P.stt("dve", tbB[:, :, js], zsv, sgn, S1p[:, :, js], ALU.mult, ALU.mult, [pzs[rb], S1p, ph], [tbB])
                            P.tt("dve", Zt[:, :, js], tbA[:, :, js], tbB[:, :, js], ALU.add, [tbA, tbB], [Zt])
                            P.tt("dve", tbA[:, :, js], zsv, C1p[:, :, js], ALU.mult, [pzs[rb], C1p], [tbA])
                            P.stt("dve", tbB[:, :, js], zv, sgn, S1p[:, :, js], ALU.mult, ALU.mult, [pz[rb], S1p, ph], [tbB])
                            P.tt("dve", Zts[:, :, js], tbA[:, :, js], tbB[:, :, js], ALU.subtract, [tbA, tbB], [Zts])
                        P.tt("dve", rtab[:], r16[:, gs].unsqueeze(2).to_broadcast([128, 4, 256]),
                             Jpos[:].unsqueeze(1).to_broadcast([128, 4, 256]), ALU.mult, [r16, Jpos], [rtab])
                        P.scan(St[:].rearrange("p g j -> p (g j)"), rtab[:].rearrange("p g j -> p (g j)"),
                               Zt[:].rearrange("p g j -> p (g j)"), [rtab, Zt], [St])
                        P.scan(Sts[:].rearrange("p g j -> p (g j)"), rtab[:].rearrange("p g j -> p (g j)"),
                               Zts[:].rearrange("p g j -> p (g j)"), [rtab, Zts], [Sts])
                        P.tt("dve", tbA[:, :, 0:128], St[:, :, 127:255], C1[:], ALU.mult, [St, C1], [tbA])
                        P.stt("dve", tbB[:, :, 0:128], Sts[:, :, 127:255], sgn, S1[:], ALU.mult, ALU.mult, [Sts, S1, ph], [tbB])
                        P.tt("dve", Sb[:], tbA[:, :, 0:128], tbB[:, :, 0:128], ALU.subtract, [tbA, tbB], [Sb])
                        depc = Buf()
                        depc.w = dict(Sb.b.w)
                        for _ in range(2):
                            o_, i_, cb_, sm_ = conv_jobs.pop(0)
                            P.dma("pool", o_, i_, r=[depc], w=[cb_], sem=sm_)
                        if gb == 7:
                            P.seal("S_cvi", [cvb_i])
                            P.seal("S_cvo", [cvb_o])
                        for g in range(4):
                            bank = pf[0] if g % 2 == 0 else pf[5]
                            cs = slice(0, 256)
                            P.mm(bank[:, cs], UT[1][:, g, 0, :], Toep[:, g, 0:256], True, False, [UT[1], Toep], [bank], signal=False)
                            P.mm(bank[:, cs], UT[1][:, g, 1, :], Toep[:, g, 256:512], False, False, [UT[1], Toep], [bank], signal=False)
                            P.mm(bank[:, cs], Sb[:, g, :], Qst[:, g, 1:17, :].rearrange("p a b -> p (a b)"), False, True, [Sb, Qst], [bank])
                            gl = (g0 + g) % 8
                            P.act(zc2[:, :, gl * 16:(gl + 1) * 16], bank[:, cs].rearrange("p (s c) -> p s c", s=16),
                                  AF.Gelu_apprx_tanh, [bank], [zc2])
                        if gb % 2 == 1:
                            cc = gb // 2
                            for sh in range(2):
                                bank = pb[sh]
                                for s8 in range(8):
                                    s = sh * 8 + s8
                                    P.tp(bank[:, s8 * 128:(s8 + 1) * 128], zc2[:, s, :], ident[:], [zc2, ident], [bank], signal=(s8 == 7))
                                P.cp("act",
                                     zT[:, cc, :].rearrange("p (j s) -> p s j", s=16)[:, sh * 8:(sh + 1) * 8, :],
                                     bank[:, :].rearrange("p (s j) -> p s j", s=8), [bank], [zT])
                    if DBG:
                        P.dma("sp", dbg_zT, zT[:], r=[zT], w=[Buf()], sem="S_dbg")
                    P.barrier()
                    if STOP == 4:
                        return True

            with ExitStack() as scC:
                wq = sb(scC, "wq", [128, 8, 512], BF16)
                wk = sb(scC, "wk", [128, 8, 2, 128], BF16)
                wv = sb(scC, "wv", [128, 8, 128], BF16)
                wg = sb(scC, "wg", [128, 8, 2048], BF16)
                srcw = w_in.rearrange("(k p) n -> p k n", p=128)
                P.dma("pool", wv[:, :, :], srcw[:, :, 640:768], w=[wv], sem="S_wv")
                for kv in range(2):
                    for hh in range(2):
                        P.dma("pool", wk[:, :, kv, hh * 64:(hh + 1) * 64], srcw[:, :, 512 + kv * 64:512 + (kv + 1) * 64], w=[wk], sem="S_wk")
                P.seal("S_wv", [wv])
                P.seal("S_wk", [wk])
                for k2 in range(2):
                    P.dma("pool", wq[:, 4 * k2:4 * k2 + 4, :], srcw[:, 4 * k2:4 * k2 + 4, 0:512], w=[wq], sem="S_wq")
                P.seal("S_wq", [wq])
                depq = Buf()
                depq.w = dict(wq.b.w)
                for k2 in range(4):
                    P.dma("pool", wg[:, 2 * k2:2 * k2 + 2, :], srcw[:, 2 * k2:2 * k2 + 2, 1280:3328], r=[depq], w=[wg], sem="S_wg")
                P.seal("S_wg", [wg])

                xg = [sb(scC, f"xg{i}", [128, D], F32) for i in range(2)]
                xgr = Ring(xg)
                xrr = [sb(scC, f"xr{i}", [128, D], F32) for i in range(3)]
                xr_i = [0]
                xss = Ring([sb(scC, f"xsC{i}", [128, D], BF16) for i in range(2)])
                sss = Ring([sb(scC, f"ssC{i}", [128, 1], F32) for i in range(4)])
                tmps = Ring([sb(scC, f"tmC{i}", [128, 1], F32) for i in range(4)])
                rsts = Ring([sb(scC, f"rsC{i}", [128, 1], F32) for i in range(4)])
                hTg = sb(scC, "hTg", [128, 8, 512], BF16)
                qT = sb(scC, "qT", [128, 4, 512], BF16)
                kT = sb(scC, "kT", [128, 2, 640], BF16)
                vtok = sb(scC, "vtok", [128, 5, 128], BF16)
                attT = sb(scC, "attT", [128, 4, 512], BF16)
                zg = sb(scC, "zg", [128, 4, 512], BF16)
                sgt = Ring([sb(scC, f"sgt{i}", [128, 512], BF16) for i in range(3)])
                mT = sb(scC, "mT", [128, 8, 512], BF16)
                slog2 = [sb(scC, f"slog{i}", [128, 4, 256], F32) for i in range(2)]
                Pm2 = [sb(scC, f"Pm{i}", [128, 4, 256], BF16) for i in range(2)]
                PTs2 = [sb(scC, f"PTs{i}", [128, 4, 2, 128], BF16) for i in range(2)]
                attn2 = [sb(scC, f"attn{i}", [128, 512], BF16) for i in range(2)]
                mx2 = [sb(scC, f"mx{i}", [128, 4], F32) for i in range(2)]
                nmx2 = [sb(scC, f"nmx{i}", [128, 4], F32) for i in range(2)]
                rs2 = [sb(scC, f"rs{i}", [128, 4], F32) for i in range(4)]
                es2 = [sb(scC, f"es{i}", [128, 4], F32) for i in range(4)]
                dn2 = [sb(scC, f"dn{i}", [128, 4], F32) for i in range(4)]
                t1s = Ring([sb(scC, f"t1s{i}", [128, 512], F32) for i in range(2)])
                t2s = Ring([sb(scC, f"t2s{i}", [128, 512], F32) for i in range(1)])
                x1t = Ring([sb(scC, f"x1t{i}", [128, D], F32) for i in range(1)])
                ssp = Ring([sb(scC, f"ssp{i}", [128, 2], F32) for i in range(2)])
                ss1 = Ring([sb(scC, f"ss1{i}", [128, 1], F32) for i in range(2)])

                def proj_kv(src_hT_ap_fn, ntiles, hT_tl, kcol0, vt0):
                    n = ntiles * 128
                    for kv in range(2):
                        bank = pfr.next()
                        for k in range(8):
                            P.mm(bank[:, 0:n], wk[:, k, kv, :], src_hT_ap_fn(k, 0, n), k == 0, k == 7, [wk, hT_tl], [bank])
                        P.cp("act", kT[:, kv, kcol0:kcol0 + n], bank[:, 0:n], [bank], [kT])
                    for t in range(ntiles):
                        bank = pfr.next()
                        for k in range(8):
                            P.mm(bank[:, 0:128], src_hT_ap_fn(k, t * 128, 128), wv[:, k, :], k == 0, k == 7, [hT_tl, wv], [bank])
                        P.cp("dve", vtok[:, vt0 + t, :], bank[:, 0:128], [bank], [vtok])

                xt = xgr.next()
                P.dma("sp", xt[:], x[1920:2048, :], w=[xt], sem="S_xC0")
                norm_transpose(xt, gpre, xss.next(), sss.next(), tmps.next(), rsts.next(),
                               hTg[:, :, 0:128], hTg, "act")
                proj_kv(lambda k, c0, n: hTg[:, k, c0:c0 + n], 1, hTg, 0, 0)

                if STOP == 41:
                    return True
                xc_cnt = [1]

                def norm_tile_C(Gn, t):
                    xt = xgr.next()
                    tok0 = 2048 + Gn * 512 + t * 128
                    P.dma("sp", xt[:], x[tok0:tok0 + 128, :], w=[xt], sem=f"S_xC{xc_cnt[0] % 2}")
                    xc_cnt[0] += 1
                    return norm_transpose(xt, gpre, xss.next(), sss.next(), tmps.next(), rsts.next(),
                                          hTg[:, :, t * 128:(t + 1) * 128], hTg, "act", defer=True)

                for G in range(4):
                    m0 = G * 512
                    if G == 0:
                        pend = None
                        for t in range(4):
                            st2 = norm_tile_C(0, t)
                            if pend is not None:
                                pend()
                            pend = st2
                        pend()
                    if STOP == 42 and G == 0:
                        return True
                    for c in range(4):
                        bank = pfr.next()
                        for k in range(8):
                            P.mm(bank[:, :], wq[:, k, c * 128:(c + 1) * 128], hTg[:, k, :], k == 0, k == 7, [wq, hTg], [bank])
                        P.cp("act" if c % 2 == 0 else "dve", qT[:, c, :], bank[:, :], [bank], [qT])
                    proj_kv(lambda k, c0, n: hTg[:, k, c0:c0 + n], 4, hTg, 128, 1)
                    if STOP == 43 and G == 0:
                        return True
                    def S1(u):
                        t, hh = divmod(u, 2)
                        p = u % 2
                        for hl in range(4):
                            h = hh * 4 + hl
                            bank = pf[2 * p + (hl % 2)]
                            half = hl // 2
                            hs = slice(64 * (hl % 2), 64 * (hl % 2) + 64)
                            P.mm(bank[:, half * 256:half * 256 + 256], qT[hs, h // 2, t * 128:(t + 1) * 128],
                                 kT[hs, hh, t * 128:t * 128 + 256], True, True, [qT, kT], [bank], signal=(hl >= 2))

                    def S2(u):
                        t, hh = divmod(u, 2)
                        p = u % 2
                        first = (G == 0 and t == 0)
                        slog, Pm, mx, nmx, rs, es_ = slog2[p], Pm2[p], mx2[p], nmx2[p], rs2[u % 4], es2[u % 4]
                        for par in range(2):
                            bank = pf[2 * p + par]
                            bv = bank[:, :].rearrange("p (h j) -> p h j", h=2)
                            lsl = slice(par, par + 3, 2)
                            gsl = slice(hh * 4 + par, hh * 4 + par + 3, 2)
                            if first:
                                P.stt("dve", slog[:, lsl, 0:128], bv[:, :, 0:128], 0.125, Tb0[:, gsl, :],
                                      ALU.mult, ALU.add, [bank, Tb0], [slog])
                                P.stt("dve", slog[:, lsl, 128:256], bv[:, :, 128:256], 0.125, Tb[:, gsl, 128:256],
                                      ALU.mult, ALU.add, [bank, Tb], [slog])
                            else:
                                P.stt("dve", slog[:, lsl, :], bv, 0.125, Tb[:, gsl, :], ALU.mult, ALU.add, [bank, Tb], [slog])
                        sk = sinks[:, hh * 4:hh * 4 + 4]
                        P.rmax(mx[:], slog[:], [slog], [mx])
                        P.tt("dve", mx[:], mx[:], sk, ALU.max, [mx, sinks], [mx])
                        P.ts("dve", nmx[:], mx[:], -1.0, None, ALU.mult, None, [mx], [nmx])
                        P.tt("dve", es_[:], sk, mx[:], ALU.subtract, [sinks, mx], [es_])
                        for hl in range(4):
                            P.act(Pm[:, hl, :], slog[:, hl, :], AF.Exp, [slog, nmx], [Pm, rs], bias=nmx[:, hl:hl + 1], accum=rs[:, hl:hl + 1])
                        P.act(es_[:], es_[:], AF.Exp, [es_], [es_])

                    def S3(u):
                        p = u % 2
                        bank = pb[p]
                        for hl in range(4):
                            for kc in range(2):
                                P.tp(bank[:, (hl * 2 + kc) * 128:(hl * 2 + kc + 1) * 128], Pm2[p][:, hl, kc * 128:(kc + 1) * 128], ident[:],
                                     [Pm2[p], ident], [bank], signal=(hl == 3 and kc == 1))
                        P.cp("act", PTs2[p][:].rearrange("p h k q -> p (h k q)"), bank[:, :], [bank], [PTs2[p]])

                    def S4(u):
                        t, hh = divmod(u, 2)
                        p = u % 2
                        po = pf[4 + p]
                        dn, rs, es_ = dn2[u % 4], rs2[u % 4], es2[u % 4]
                        P.tt("dve", dn[:], rs[:], es_[:], ALU.add, [rs, es_], [dn])
                        P.recip(dn[:], dn[:], [dn], [dn])
                        for hl in range(4):
                            for kc in range(2):
                                P.mm(po[:, hl * 64:(hl + 1) * 64], PTs2[p][:, hl, kc, :], vtok[:, t + kc, hh * 64:hh * 64 + 64],
                                     kc == 0, kc == 1, [PTs2[p], vtok], [po], signal=(hl == 3 and kc == 1))
                        at = attn2[t % 2]
                        P.tt("dve", at[:, hh * 256:(hh + 1) * 256].rearrange("p (h d) -> p h d", h=4),
                             po[:, 0:256].rearrange("p (h d) -> p h d", h=4),
                             dn2[u % 4][:].unsqueeze(2).to_broadcast([128, 4, 64]), ALU.mult, [po, dn2[u % 4]], [at])
                        if hh == 1:
                            bank = pb[p]
                            for c in range(4):
                                P.tp(bank[:, c * 128:(c + 1) * 128], at[:, c * 128:(c + 1) * 128], ident[:], [at, ident], [bank], signal=(c == 3))
                            P.cp("act", attT[:, :, t * 128:(t + 1) * 128], bank[:, 0:512].rearrange("p (c j) -> p c j", c=4), [bank], [attT])

                    NU = 8
                    for i in range(NU + 3):
                        if i < NU:
                            S1(i)
                        if 0 <= i - 1 < NU:
                            S2(i - 1)
                        if 0 <= i - 2 < NU:
                            S3(i - 2)
                        if 0 <= i - 3 < NU:
                            S4(i - 3)
                    if STOP == 44 and G == 0:
                        return True
                    P.cp("dve", kT[:, :, 0:128], kT[:, :, 512:640], [kT], [kT])
                    P.cp("dve", vtok[:, 0, :], vtok[:, 4, :], [vtok], [vtok])
                    if STOP == 45 and G == 0:
                        return True
                    for co in range(4):
                        bank = pfr.next()
                        for c in range(4):
                            P.mm(bank[:, :], wglu[:, c, co * 128:(co + 1) * 128], zT[:, c, m0:m0 + 512], c == 0, c == 3, [wglu, zT], [bank])
                        sg = sgt.next()
                        P.act(sg[:], bank[:, :], AF.Sigmoid, [bank], [sg])
                        P.tt("dve", zg[:, co, :], zT[:, co, m0:m0 + 512], sg[:], ALU.mult, [zT, sg], [zg])
                    if STOP == 46 and G == 0:
                        return True
                    for fo in range(8):
                        fs = slice(fo * 128, (fo + 1) * 128)
                        bga = pfr.next()
                        for k in range(8):
                            P.mm(bga[:, :], wg[:, k, fo * 128:(fo + 1) * 128], hTg[:, k, :], k == 0, k == 7, [wg, hTg], [bga])
                        sga = sgt.next()
                        P.act(sga[:], bga[:, :], AF.Sigmoid, [bga], [sga])
                        bgs = pfr.next()
                        for k in range(8):
                            P.mm(bgs[:, :], wg[:, k, 1024 + fo * 128:1024 + (fo + 1) * 128], hTg[:, k, :], k == 0, k == 7, [wg, hTg], [bgs])
                        sgs = sgt.next()
                        P.act(sgs[:], bgs[:, :], AF.Sigmoid, [bgs], [sgs])
                        ba = pfr.next()
                        for c in range(4):
                            P.mm(ba[:, :], wab[:, c, fs], attT[:, c, :], c == 0, c == 3, [wab, attT], [ba])
                        t1 = t1s.next()
                        P.tt("dve", t1[:], ba[:, :], sga[:], ALU.mult, [ba, sga], [t1])
                        bs = pfr.next()
                        for c in range(4):
                            P.mm(bs[:, :], wsb_[:, c, fs], zg[:, c, :], c == 0, c == 3, [wsb_, zg], [bs])
                        t2 = t2s.next()
                        P.tt("dve", t2[:], bs[:, :], sgs[:], ALU.mult, [bs, sgs], [t2])
                        P.tt("dve", mT[:, fo, :], t1[:], t2[:], ALU.add, [t1, t2], [mT])
                    if STOP == 47 and G == 0:
                        return True
                    xrs, xr_sem = {}, {}

                    def load_xr(t):
                        idx = xr_i[0] % 3
                        xr_i[0] += 1
                        xr = xrr[idx]
                        tok0 = 2048 + m0 + t * 128
                        P.dma("sp", xr[:], x[tok0:tok0 + 128, :], w=[xr], sem=f"S_xR{idx}")
                        xrs[t] = xr
                        xr_sem[t] = idx

                    load_xr(0)
                    load_xr(1)
                    pendn = None
                    for t in range(4):
                        if G + 1 < 4:
                            st2n = norm_tile_C(G + 1, t)
                            if pendn is not None:
                                pendn()
                            pendn = st2n
                        b0, b1 = pfr.next(), pfr.next()
                        for half, bank in ((0, b0), (1, b1)):
                            for k in range(8):
                                P.mm(bank[:, :], mT[:, k, t * 128:(t + 1) * 128], wout[:, k, half * 512:(half + 1) * 512], k == 0, k == 7,
                                     [mT, wout], [bank])
                        sp_ = ssp.next()
                        j0, j1 = sgt.next(), sgt.next()
                        P.act(j0[:], b0[:, :], AF.Square, [b0], [j0, sp_], accum=sp_[:, 0:1])
                        P.act(j1[:], b1[:, :], AF.Square, [b1], [j1, sp_], accum=sp_[:, 1:2])
                        s1 = ss1.next()
                        P.tt("dve", s1[:], sp_[:, 0:1], sp_[:, 1:2], ALU.add, [sp_], [s1])
                        tm, rsd = tmps.next(), rsts.next()
                        rstd_from_ss(s1[:], s1, rsd, tm)
                        xo = x1t.next()
                        for half, bank in ((0, b0), (1, b1)):
                            hs_ = slice(half * 512, (half + 1) * 512)
                            P.stt("dve", xo[:, hs_], bank[:, :], rsd[:], gpost[:, hs_], ALU.mult, ALU.mult, [bank, rsd, gpost], [xo])
                        xr = xrs[t]
                        P.tt("dve", xr[:], xo[:], xr[:], ALU.add, [xo, xr], [xr])
                        r0 = m0 + t * 128
                        P.dma("pool", x1scr[r0:r0 + 128, :], xr[:], r=[xr], w=[Buf()], sem=f"S_x1w{xr_sem[t]}")
                        if t + 2 < 4:
                            load_xr(t + 2)
                    if pendn is not None:
                        pendn()
                P.barrier()
                if STOP == 5:
                    return True

            scABC.close()
            with ExitStack() as scD:
                wffi = sb(scD, "wffi", [128, 8, 4096], BF16)
                wffo = sb(scD, "wffo", [128, 32, D], BF16)
                wffi_bv = wffi_b.rearrange("(k p) n -> p k n", p=128)
                wffo_bv = wffo_b.rearrange("(k p) n -> p k n", p=128)
                wffi_q = [Buf() for _ in range(4)]
                for q4 in range(4):
                    P.dma("act", wffi[:, :, q4 * 1024:(q4 + 1) * 1024], wffi_bv[:, :, q4 * 1024:(q4 + 1) * 1024],
                          r=[cvb_i], w=[wffi_q[q4]], sem=f"S_wffi{q4}")
                wffo_p = [Buf() for _ in range(8)]
                depw = Buf()
                depw.w = dict(wffi_q[3].w)
                for k4 in range(8):
                    P.dma("pool", wffo[:, 4 * k4:4 * k4 + 4, :], wffo_bv[:, 4 * k4:4 * k4 + 4, :], r=[cvb_o, depw], w=[wffo_p[k4]],
                          sem=f"S_wffo{k4}")
                g2pre = sb(scD, "g2pre", [128, D], F32)
                g2post = sb(scD, "g2post", [128, D], F32)
                P.dma("sp", g2pre[:], gains[2, :, :], w=[g2pre], sem="S_c3")
                P.dma("sp", g2post[:], gains[3, :, :], w=[g2post], sem="S_c3")
                P.seal("S_c3", [g2pre, g2post])
                x1g = Ring([sb(scD, f"x1g{i}", [128, D], F32) for i in range(4)])
                xss = Ring([sb(scD, f"xsD{i}", [128, D], BF16) for i in range(2)])
                sss = Ring([sb(scD, f"ssD{i}", [128, 1], F32) for i in range(6)])
                tmps = Ring([sb(scD, f"tmD{i}", [128, 1], F32) for i in range(4)])
                rsts = Ring([sb(scD, f"rsD{i}", [128, 1], F32) for i in range(4)])
                h2T = sb(scD, "h2T", [128, 8, 256], BF16)
                ffT = sb(scD, "ffT", [128, 32, 256], BF16)
                rl = Ring([sb(scD, f"rl{i}", [128, 512], BF16) for i in range(2)])
                ot = Ring([sb(scD, f"ot{i}", [128, D], F32) for i in range(2)])
                ssp = Ring([sb(scD, f"sspD{i}", [128, 2], F32) for i in range(2)])
                ss1 = Ring([sb(scD, f"ss1D{i}", [128, 1], F32) for i in range(2)])
                outb = Buf()
                h2Tb = sb(scD, "h2Tb", [128, 8, 256], BF16)
                h2T2 = [h2T, h2Tb]
                xtiles = {}

                def norm_group(Gn, defer):
                    st2s = []
                    tl = []
                    for t in range(2):
                        xt = x1g.next()
                        r0 = Gn * 256 + t * 128
                        P.dma("sp", xt[:], x1scr[r0:r0 + 128, :], w=[xt], sem=f"S_xD{(Gn * 2 + t) % 4}")
                        tl.append(xt)
                        hdst = h2T2[Gn % 2]
                        st2s.append(norm_transpose(xt, g2pre, xss.next(), sss.next(), tmps.next(), rsts.next(),
                                                   hdst[:, :, t * 128:(t + 1) * 128], hdst, "act", defer=True))
                    xtiles[Gn] = tl
                    if defer:
                        return st2s
                    for f in st2s:
                        f()
                    return []

                norm_group(0, False)
                for G in range(8):
                    m0 = G * 256
                    xs_g = xtiles[G]
                    hcur = h2T2[G % 2]
                    for fp in range(16):
                        bank = pfr.next()
                        for j in range(2):
                            fc = fp * 2 + j
                            for k in range(8):
                                P.mm(bank[:, j * 256:(j + 1) * 256], wffi[:, k, fc * 128:(fc + 1) * 128], hcur[:, k, :], k == 0, k == 7,
                                     [wffi_q[fc // 8], hcur], [bank], signal=(j == 1 and k == 7))
                        r_ = rl.next()
                        P.act(r_[:], bank[:, :], AF.Relu, [bank], [r_])
                        P.tt("dve", ffT[:, fp * 2:fp * 2 + 2, :], r_[:].rearrange("p (a n) -> p a n", a=2),
                             r_[:].rearrange("p (a n) -> p a n", a=2), ALU.mult, [r_], [ffT])
                    nxt = norm_group(G + 1, True) if G + 1 < 8 else []
                    for t in range(2):
                        b0, b1 = pfr.next(), pfr.next()
                        for half, bank in ((0, b0), (1, b1)):
                            for fc in range(32):
                                P.mm(bank[:, :], ffT[:, fc, t * 128:(t + 1) * 128], wffo[:, fc, half * 512:(half + 1) * 512], fc == 0, fc == 31,
                                     [ffT, wffo_p[fc // 4]], [bank])
                        if t == 0:
                            for f in nxt:
                                f()
                        sp_ = ssp.next()
                        j0, j1 = rl.next(), rl.next()
                        P.act(j0[:], b0[:, :], AF.Square, [b0], [j0, sp_], accum=sp_[:, 0:1])
                        P.act(j1[:], b1[:, :], AF.Square, [b1], [j1, sp_], accum=sp_[:, 1:2])
                        s1 = ss1.next()
                        P.tt("dve", s1[:], sp_[:, 0:1], sp_[:, 1:2], ALU.add, [sp_], [s1])
                        tm, rsd = tmps.next(), rsts.next()
                        rstd_from_ss(s1[:], s1, rsd, tm)
                        xo = ot.next()
                        for half, bank in ((0, b0), (1, b1)):
                            hs_ = slice(half * 512, (half + 1) * 512)
                            P.stt("dve", xo[:, hs_], bank[:, :], rsd[:], g2post[:, hs_], ALU.mult, ALU.mult, [bank, rsd, g2post], [xo])
                        P.tt("dve", xo[:], xo[:], xs_g[t][:], ALU.add, [xo, xs_g[t]], [xo])
                        r0 = m0 + t * 128
                        P.dma("pool", out_d[r0:r0 + 128, :], xo[:], r=[xo], w=[outb], sem=f"S_ow{(G * 2 + t) % 2}")
                P.barrier()


            return False

        if body():
            scS.close()
            scABC.close()
            P.barrier()

        block = es.enter_context(nc.Block())
        P.emit(block)
    return nc


def _bucket(d):
    d = np.asarray(d)
    df = np.maximum(d, 1).astype(np.float32)
    large = 16 + (np.log(df / np.float32(16)) / np.float32(math.log(128 / 16)) * np.float32(16)).astype(np.int32)
    large = np.minimum(large, 31)
    return np.where(d < 16, d, large)


def _constants():
    c = {}
    c["ident"] = np.eye(128, dtype=np.float32)
    oh = np.zeros((33, 384), np.float32)
    for m in range(383):
        d = 255 - m
        if 0 <= d < 128:
            oh[int(_bucket(d)), m] = 1.0
        else:
            oh[32, m] = 1.0
    c["oh"] = oh
    cm = np.zeros((128, 2, 256), np.float32)
    iq = np.zeros((128, 2, 256), np.float32)
    for p in range(128):
        sl, cc = divmod(p, 16)
        for q in range(2):
            s = 8 * q + sl
            for s2 in range(16):
                if s2 >= s:
                    cm[p, q, s2 * 16:(s2 + 1) * 16] = 1.0
            iq[p, q, s * 16 + cc] = 1.0
    c["cmask"], c["identq"] = cm, iq
    ph = np.zeros((128, 8), np.float32)
    top, bot = slice(0, 64), slice(64, 128)
    ph[top, 0], ph[bot, 0] = PI / 2, 0.0
    ph[top, 1], ph[bot, 1] = PI, PI / 2
    ph[top, 2], ph[bot, 2] = PI / 2, PI
    ph[top, 3], ph[bot, 3] = PI, 3 * PI / 2
    ph[top, 4], ph[bot, 4] = 0.0, PI
    ph[top, 5], ph[bot, 5] = 1.0, -1.0
    c["ph"] = ph
    c["tauN"] = np.tile(-np.arange(16, dtype=np.float32), (128, 1))
    c["tauP"] = np.tile(np.arange(17, dtype=np.float32), (128, 1))
    c["Jv"] = np.tile(np.arange(256, dtype=np.float32), (128, 1))
    return c


def _prep_inputs(inp):
    f = lambda a: np.ascontiguousarray(np.asarray(a, dtype=np.float32))
    shared = dict(_constants())
    shared["w_in"] = f(inp["w_in"][0])
    shared["w_glu"] = f(inp["w_glu"][0])
    shared["w_ab"] = f(inp["w_attn_branch"][0])
    shared["w_sb"] = f(inp["w_ssm_branch"][0])
    shared["w_out"] = f(inp["w_out"][0])
    shared["w_ffi"] = f(inp["w_ff_in"][0])
    shared["w_ffo"] = f(inp["w_ff_out"][0])
    gains = np.stack([inp["norm_mix_pre"][0], inp["norm_mix_post"][0], inp["norm_mlp_pre"][0], inp["norm_mlp_post"][0]])
    shared["gains"] = f(np.broadcast_to(np.asarray(gains, np.float32)[:, None, :], (4, 128, D)))
    relb = np.empty((33, 8, 128), np.float32)
    relb[:32] = np.asarray(inp["rel_bias"], np.float32)[:, :, None]
    relb[32] = NEG
    shared["relb"] = relb
    shared["sinks"] = f(np.broadcast_to(np.asarray(inp["sinks"][0], np.float32)[None, :], (128, 8)))
    dup = lambda a: f(np.concatenate([a, a], axis=0))
    shared["lamr"] = dup(np.asarray(inp["lam_re"][0], np.float32).T)
    shared["lami"] = dup(np.asarray(inp["lam_im"][0], np.float32).T)
    shared["ldt"] = f(np.broadcast_to(np.asarray(inp["log_dt"][0], np.float32)[None, :], (128, 32)))
    shared["bre"] = dup(np.transpose(np.asarray(inp["b_re"][0], np.float32), (1, 0, 2)))
    shared["bim"] = dup(np.transpose(np.asarray(inp["b_im"][0], np.float32), (1, 0, 2)))
    shared["cre"] = dup(np.transpose(np.asarray(inp["c_re"][0], np.float32), (2, 0, 1)))
    shared["cim"] = dup(np.transpose(np.asarray(inp["c_im"][0], np.float32), (2, 0, 1)))
    dsk = np.asarray(inp["d_skip"][0], np.float32).reshape(32, 16)
    ddiag = np.zeros((16, 32, 16), np.float32)
    for c_ in range(16):
        ddiag[c_, :, c_] = dsk[:, c_]
    shared["ddiag"] = ddiag
    shared["dcol"] = f(np.tile(dsk.T, (8, 1)))
    xs = np.asarray(inp["x"], np.float32)
    in_maps = []
    for core in range(8):
        b, half = divmod(core, 2)
        m = dict(shared)
        if half == 0:
            xc = np.concatenate([np.zeros((2048, D), np.float32), xs[b, :2048]], axis=0)
            halo = np.full((128, 128), NEG, np.float32)
        else:
            xc = xs[b]
            halo = np.zeros((128, 128), np.float32)
        m["x"] = np.ascontiguousarray(xc)
        m["halo"] = halo
        in_maps.append(m)
    return in_maps


_NC_CACHE = {}


def kernel(**inputs):
    in_maps = _prep_inputs(inputs)
    if "nc" not in _NC_CACHE:
        _NC_CACHE["nc"] = build()
    nc = _NC_CACHE["nc"]
    res = run_bass_kernel_spmd(nc, in_maps, core_ids=list(range(8)))
    out = np.empty((4, 4096, D), np.float32)
    for core in range(8):
        b, half = divmod(core, 2)
        out[b, half * 2048:(half + 1) * 2048] = np.asarray(res.results[core]["out"], np.float32)
    return out
```

```python
import math
from contextlib import ExitStack

import numpy as np
import concourse.bass as bass
import concourse.mybir as mybir
from concourse.bass_utils import run_bass_kernel_spmd

F32 = mybir.dt.float32
BF16 = mybir.dt.bfloat16
AF = mybir.ActivationFunctionType
ALU = mybir.AluOpType
AX = mybir.AxisListType

NEG = -30000.0
EPS = 1e-6
PI = math.pi
TWO_PI = 2.0 * math.pi
MAGIC = 12582912.0
D = 1024
NTOK = 2048
DBG = False
STOP = 99


class _Stop(Exception):
    pass


class Buf:
    __slots__ = ("w", "r")

    def __init__(self):
        self.w = {}
        self.r = {}


class Tl:
    def __init__(self, t):
        self.t = t
        self.b = Buf()

    def __getitem__(self, k):
        return self.t[k]


class Prog:
    ENGS = ("pe", "act", "dve", "pool", "sp")

    def __init__(self, nc, es):
        self.nc, self.es = nc, es
        self.ops = {e: [] for e in self.ENGS}
        self.sem, self.cnt = {}, {}
        self.waited = {e: {} for e in self.ENGS}
        for e in ("pe", "act", "dve", "pool"):
            self.mksem("E_" + e)

    def mksem(self, name):
        if name not in self.sem:
            self.sem[name] = self.es.enter_context(self.nc.semaphore(name))
            self.cnt[name] = 0
        return name

    def _waits(self, eng, r, w, is_dma):
        own = None if is_dma else "E_" + eng
        need = {}

        def add(sem, val):
            if val > need.get(sem, 0):
                need[sem] = val

        for b in r:
            for sem, val in b.w.items():
                if sem == own and eng == "pe":
                    continue
                add(sem, val)
        for b in w:
            for sem, val in b.w.items():
                if sem != own or eng != "pe":
                    add(sem, val)
            for sem, val in b.r.items():
                if sem != own or eng != "pe":
                    add(sem, val)
        out = []
        wd = self.waited[eng]
        for sem, val in need.items():
            if wd.get(sem, 0) < val:
                out.append((sem, val))
                wd[sem] = val
        return out

    @staticmethod
    def _bufs(xs):
        return [x.b if isinstance(x, Tl) else x for x in xs]

    def op(self, eng, fn, r=(), w=(), signal=True):
        r, w = self._bufs(r), self._bufs(w)
        waits = self._waits(eng, r, w, False)
        name = "E_" + eng
        if signal:
            self.cnt[name] += 1
            val = self.cnt[name]
            inc = (name, 1)
        else:
            val = self.cnt[name] + 1
            inc = None
        self.ops[eng].append((waits, fn, inc))
        for b in r:
            b.r[name] = max(b.r.get(name, 0), val)
        for b in w:
            b.w = {name: val}
            b.r = {}

    def dma(self, q, out, in_, r=(), w=(), sem=None, **kw):
        r, w = self._bufs(r), self._bufs(w)
        self.mksem(sem)
        waits = [(sm, v) for (sm, v) in self._waits(q, r, w, True) if sm != sem]
        self.cnt[sem] += 16
        val = self.cnt[sem]
        self.ops[q].append((waits, lambda e: e.dma_start(out=out, in_=in_, **kw), (sem, 16)))
        for b in r:
            b.r[sem] = max(b.r.get(sem, 0), val)
        for b in w:
            b.w = {sem: val}
            b.r = {}

    def seal(self, sem, bufs):
        for b in self._bufs(bufs):
            b.w = {sem: self.cnt[sem]}

    def barrier(self):
        for eng in self.ENGS:
            waits = []
            for sem, c in self.cnt.items():
                if sem == "E_" + eng:
                    continue
                if c > self.waited[eng].get(sem, 0):
                    waits.append((sem, c))
                    self.waited[eng][sem] = c
            self.ops[eng].append((waits, None, None))

    def emit(self, block):
        def mk(eng):
            def f(e):
                for waits, fn, inc in self.ops[eng]:
                    for sem, val in waits:
                        e.wait_ge(self.sem[sem], val)
                    if fn is not None:
                        ins = fn(e)
                        if inc is not None:
                            ins.then_inc(self.sem[inc[0]], inc[1])

            return f

        block.tensor(mk("pe"))
        block.scalar(mk("act"))
        block.vector(mk("dve"))
        block.gpsimd(mk("pool"))
        block.sync(mk("sp"))

    def mm(self, out, lhsT, rhs, start, stop, r, w, signal=None):
        if signal is None:
            signal = stop
        self.op("pe", lambda e: e.matmul(out, lhsT=lhsT, rhs=rhs, start=start, stop=stop), r, w, signal)

    def tp(self, out, in_, ident, r, w, signal=True):
        self.op("pe", lambda e: e.transpose(out=out, in_=in_, identity=ident), r, w, signal)

    def act(self, out, in_, func, r, w, bias=None, scale=None, accum=None):
        kw = {}
        if bias is not None:
            kw["bias"] = bias
        if scale is not None:
            kw["scale"] = scale
        if accum is not None:
            kw["accum_out"] = accum
        self.op("act", lambda e: e.activation(out=out, in_=in_, func=func, **kw), r, w)

    def tt(self, eng, out, in0, in1, op, r, w):
        self.op(eng, lambda e: e.tensor_tensor(out=out, in0=in0, in1=in1, op=op), r, w)

    def ts(self, eng, out, in0, s1, s2, op0, op1, r, w):
        if s2 is None:
            self.op(eng, lambda e: e.tensor_scalar(out=out, in0=in0, scalar1=s1, scalar2=None, op0=op0), r, w)
        else:
            self.op(eng, lambda e: e.tensor_scalar(out=out, in0=in0, scalar1=s1, scalar2=s2, op0=op0, op1=op1), r, w)

    def stt(self, eng, out, in0, scalar, in1, op0, op1, r, w):
        self.op(eng, lambda e: e.scalar_tensor_tensor(out=out, in0=in0, scalar=scalar, in1=in1, op0=op0, op1=op1), r, w)

    def cp(self, eng, out, in_, r, w):
        if eng == "act":
            self.op(eng, lambda e: e.copy(out=out, in_=in_), r, w)
        else:
            self.op(eng, lambda e: e.tensor_copy(out=out, in_=in_), r, w)

    def rmax(self, out, in_, r, w):
        self.op("dve", lambda e: e.tensor_reduce(out=out, in_=in_, axis=AX.X, op=ALU.max), r, w)

    def recip(self, out, in_, r, w):
        self.op("dve", lambda e: e.reciprocal(out=out, in_=in_), r, w)

    def scan(self, out, d0, d1, r, w):
        self.op("dve", lambda e: e.tensor_tensor_scan(out=out, data0=d0, data1=d1, initial=0.0, op0=ALU.mult, op1=ALU.add), r, w)


class Ring:
    def __init__(self, items):
        self.items, self.i = items, 0

    def next(self):
        x = self.items[self.i % len(self.items)]
        self.i += 1
        return x


def build():
    nc = bass.Bass("TRN2", target_bir_lowering=False)

    def din(name, shape, dt=F32):
        return nc.dram_tensor(name, list(shape), dt, kind="ExternalInput").ap()

    x = din("x", [4096, D])
    w_in = din("w_in", [D, 3328])
    w_glu = din("w_glu", [512, 512])
    w_ab = din("w_ab", [512, D])
    w_sb = din("w_sb", [512, D])
    w_out = din("w_out", [D, D])
    w_ffi = din("w_ffi", [D, 4096])
    w_ffo = din("w_ffo", [4096, D])
    gains = din("gains", [4, 128, D])
    relb = din("relb", [33, 8, 128])
    sinks_d = din("sinks", [128, 8])
    oh_d = din("oh", [33, 384])
    halo_d = din("halo", [128, 128])
    ident_d = din("ident", [128, 128])
    cmask_d = din("cmask", [128, 2, 256])
    identq_d = din("identq", [128, 2, 256])
    ph_d = din("ph", [128, 8])
    tauN_d = din("tauN", [128, 16])
    tauP_d = din("tauP", [128, 17])
    Jv_d = din("Jv", [128, 256])
    lamr_d = din("lamr", [128, 32])
    lami_d = din("lami", [128, 32])
    ldt_d = din("ldt", [128, 32])
    bre_d = din("bre", [128, 32, 16])
    bim_d = din("bim", [128, 32, 16])
    cre_d = din("cre", [128, 32, 16])
    cim_d = din("cim", [128, 32, 16])
    dcol_d = din("dcol", [128, 32])
    ddiag_d = din("ddiag", [16, 32, 16])
    out_d = nc.dram_tensor("out", [NTOK, D], F32, kind="ExternalOutput").ap()
    x1scr = nc.dram_tensor("x1scr", [NTOK, D], F32).ap()
    wffi_b = nc.dram_tensor("wffi_b", [D, 4096], BF16).ap()
    wffo_b = nc.dram_tensor("wffo_b", [4096, D], BF16).ap()
    tbscr_t = nc.dram_tensor("tbscr", [8, 128 * 383], F32)
    tbscr = tbscr_t.ap()
    if DBG:
        dbg_zT = nc.dram_tensor("dbg_zT", [128, 4, NTOK], BF16, kind="ExternalOutput").ap()
        dbg_X = nc.dram_tensor("dbg_X", [128, 2 * 32 * 16 * 16], BF16, kind="ExternalOutput").ap()
        dbg_tb = nc.dram_tensor("dbg_tb", [128, 8 * 256], F32, kind="ExternalOutput").ap()

    with ExitStack() as es:
        P = Prog(nc, es)
        global _LASTP
        _LASTP = P

        def sb(scope, name, shape, dt):
            return Tl(scope.enter_context(nc.sbuf_tensor("sb_" + name, list(shape), dt)))

        def psum(name, shape, dt):
            return Tl(es.enter_context(nc.psum_tensor(name, list(shape), dt)))

        pf = [psum(f"pf{i}", [128, 512], F32) for i in range(6)]
        pb = [psum(f"pb{i}", [128, 1024], BF16) for i in range(2)]
        pfr = Ring(pf)
        pfr5 = Ring(pf[0:5])
        pbr = Ring(pb)

        ident = sb(es, "ident", [128, 128], BF16)
        epsc = sb(es, "epsc", [128, 1], F32)
        halfpi = sb(es, "halfpi", [128, 1], F32)
        scABC = es.enter_context(ExitStack())
        scS = ExitStack()
        ident_f = sb(scABC, "ident_f", [128, 128], F32)
        Tb = sb(scABC, "Tb", [128, 8, 256], F32)
        Tb0 = sb(scABC, "Tb0", [128, 8, 128], F32)
        sinks = sb(scABC, "sinks", [128, 8], F32)
        gpre = sb(scABC, "gpre", [128, D], F32)
        gpost = sb(scABC, "gpost", [128, D], F32)
        zT = sb(scABC, "zT", [128, 4, NTOK], BF16)
        wab = sb(scABC, "wab", [128, 4, D], BF16)
        wsb_ = sb(scABC, "wsb", [128, 4, D], BF16)
        wglu = sb(scABC, "wglu", [128, 4, 512], BF16)
        wout = sb(scABC, "wout", [128, 8, D], BF16)

        cvb_i, cvb_o = Buf(), Buf()

        def body():
            P.dma("sp", ident_f[:], ident_d[:, :], w=[ident_f], sem="S_c0")
            P.dma("sp", sinks[:], sinks_d[:, :], w=[sinks], sem="S_c0")
            P.dma("sp", gpre[:], gains[0, :, :], w=[gpre], sem="S_c0")
            P.dma("sp", gpost[:], gains[1, :, :], w=[gpost], sem="S_c0")
            P.seal("S_c0", [ident_f, sinks, gpre, gpost])
            P.cp("dve", ident[:], ident_f[:], [ident_f], [ident])
            P.op("dve", lambda e: e.memset(epsc[:], EPS), [], [epsc])
            P.op("dve", lambda e: e.memset(halfpi[:], PI / 2), [], [halfpi])

            def wload(dst_tl, dst_ap_fn, src, nk, sem):
                srcv = src.rearrange("(k p) n -> p k n", p=128)
                for k in range(nk):
                    P.dma("pool", dst_ap_fn(k), srcv[:, k, :], w=[dst_tl], sem=sem)
                P.seal(sem, [dst_tl])

            def rstd_from_ss(ss_ap, ss_tl, rstd_tl, tmp_tl):
                P.act(tmp_tl[:], ss_ap, AF.Ln, [ss_tl, epsc], [tmp_tl], bias=epsc[:], scale=1.0 / D)
                P.act(rstd_tl[:], tmp_tl[:], AF.Exp, [tmp_tl], [rstd_tl], scale=-0.5)

            def norm_transpose(xt, g_tl, xs, ss, tmp, rstd, dst_aps, dst_tl, evac_eng, defer=False):
                P.act(xs[:], xt[:], AF.Square, [xt], [xs, ss], accum=ss[:])
                rstd_from_ss(ss[:], ss, rstd, tmp)
                P.stt("dve", xs[:], xt[:], rstd[:], g_tl[:], ALU.mult, ALU.mult, [xt, rstd, g_tl], [xs])

                def stage2():
                    bank = pbr.next()
                    for k in range(8):
                        P.tp(bank[:, k * 128:(k + 1) * 128], xs[:, k * 128:(k + 1) * 128], ident[:], [xs, ident], [bank], signal=(k == 7))
                    P.cp(evac_eng, dst_aps, bank[:, :].rearrange("p (k j) -> p k j", k=8), [bank], [dst_tl])

                if defer:
                    return stage2
                stage2()

            with ExitStack() as scAB:
                def small(name, shape, dt=F32):
                    return sb(scAB, name, shape, dt)

                ph = small("ph", [128, 8]); Jv = small("Jv", [128, 256])
                Y1 = small("Y1", [128, 32, 16]); Y2 = small("Y2", [128, 32, 16])
                dcol = small("dcol", [128, 32])
                ddiag = small("ddiag", [16, 32, 16])
                phir = small("phir", [128, 32]); th15r = small("th15r", [128, 32])
                mag15 = small("mag15", [128, 32]); r16 = small("r16", [128, 32])
                X1 = small("X1", [128, 32, 16]); X2 = small("X2", [128, 32, 16])
                E1n = small("E1n", [128, 32, 16]); E2n = small("E2n", [128, 32, 16])
                F1 = small("F1", [128, 32, 17]); F2 = small("F2", [128, 32, 17])
                X = small("X", [128, 2, 32, 16, 16], BF16)

                scS.__enter__()

                def smallt(name, shape, dt=F32):
                    return sb(scS, name, shape, dt)

                lr = smallt("lr", [128, 32]); li = smallt("li", [128, 32]); ldt = smallt("ldt", [128, 32])
                tauN = smallt("tauN", [128, 16]); tauP = smallt("tauP", [128, 17])
                Br = smallt("Br", [128, 32, 16]); Bi = smallt("Bi", [128, 32, 16])
                dt_ = smallt("dt_", [128, 32]); lrdt = smallt("lrdt", [128, 32]); th = smallt("th", [128, 32])
                thr = smallt("thr", [128, 32]); t32a = smallt("t32a", [128, 32]); t32b = smallt("t32b", [128, 32])
                t32c = smallt("t32c", [128, 32])
                mag1 = smallt("mag1", [128, 32]); cth = smallt("cth", [128, 32]); sth = smallt("sth", [128, 32])
                ar = smallt("ar", [128, 32]); ai = smallt("ai", [128, 32]); nr = smallt("nr", [128, 32])
                den = smallt("den", [128, 32]); fre = smallt("fre", [128, 32]); fim = smallt("fim", [128, 32])
                magP = smallt("magP", [128, 32, 17]); angP = smallt("angP", [128, 32, 17])
                tb17a = smallt("tb17a", [128, 32, 17]); tb17b = smallt("tb17b", [128, 32, 17])
                relb_s = smallt("relb_s", [33, 8, 128])
                oh_s = smallt("oh_s", [33, 384])
                halo_s = smallt("halo_s", [128, 128])
                rrow = [smallt(f"rrow{i}", [128, 383]) for i in range(2)]

                def reduce_pm_pi(dst_ap, dst_tl, src_ap, tmp_ap, tmp_tl, rlist, eng="dve"):
                    P.ts(eng, tmp_ap, src_ap, 1.0 / TWO_PI, MAGIC, ALU.mult, ALU.add, rlist, [tmp_tl])
                    P.ts(eng, tmp_ap, tmp_ap, -MAGIC, None, ALU.add, None, [tmp_tl], [tmp_tl])
                    P.stt(eng, dst_ap, tmp_ap, -TWO_PI, src_ap, ALU.mult, ALU.add, [tmp_tl] + rlist, [dst_tl])
                    P.ts(eng, dst_ap, dst_ap, 3.14159, -3.14159, ALU.min, ALU.max, [dst_tl], [dst_tl])

                def sin_of(dst, ang_ap, ang_tl, phase, tA, tB, sl=None):
                    sl = sl if sl is not None else (slice(None),) * 3
                    rl = [ang_tl] + ([ph] if not isinstance(phase, float) else [])
                    P.ts("dve", tA[sl], ang_ap, phase, None, ALU.add, None, rl, [tA])
                    reduce_pm_pi(tB[sl], tB, tA[sl], dst[sl], dst, [tA])
                    yield
                    P.act(dst[sl], tB[sl], AF.Sin, [tB], [dst])
                    yield

                def background():
                    P.dma("sp", relb_s[:], relb[:, :, :], w=[relb_s], sem="S_c1")
                    P.dma("sp", oh_s[:], oh_d[:, :], w=[oh_s], sem="S_c1")
                    P.dma("sp", halo_s[:], halo_d[:, :], w=[halo_s], sem="S_c1")
                    P.seal("S_c1", [relb_s, oh_s, halo_s])
                    for tl, src in ((lr, lamr_d), (li, lami_d), (ldt, ldt_d), (ph, ph_d), (tauN, tauN_d), (tauP, tauP_d),
                                    (Jv, Jv_d), (dcol, dcol_d)):
                        P.dma("sp", tl[:], src[:, :], w=[tl], sem="S_c2")
                    for tl, src in ((Br, bre_d), (Bi, bim_d), (Y1, cre_d), (Y2, cim_d), (ddiag, ddiag_d)):
                        P.dma("sp", tl[:], src[:, :, :], w=[tl], sem="S_c2")
                    P.seal("S_c2", [lr, li, ldt, ph, tauN, tauP, Jv, dcol, Br, Bi, Y1, Y2, ddiag])
                    yield
                    scrb = [Buf() for _ in range(8)]
                    for h in range(8):
                        bank = pf[5]
                        P.mm(bank[:, 0:383], relb_s[:, h, :], oh_s[:, 0:383], True, True, [relb_s, oh_s], [bank])
                        rr = rrow[h % 2]
                        P.cp("dve", rr[:], bank[:, 0:383], [bank], [rr])
                        dst = bass.AP(tbscr_t, h * 128 * 383, [[383, 128], [1, 383]])
                        P.dma("pool", dst, rr[:], r=[rr], w=[scrb[h]], sem=f"S_tbw{h % 2}")
                        yield
                    for h in range(8):
                        src = bass.AP(tbscr_t, h * 128 * 383 + 127, [[382, 128], [1, 256]])
                        P.dma("pool", Tb[:, h, :], src, r=[scrb[h]], w=[Tb], sem="S_tbr")
                    P.seal("S_tbr", [Tb])
                    yield
                    P.act(dt_[:], ldt[:], AF.Exp, [ldt], [dt_])
                    yield
                    P.tt("dve", lrdt[:], lr[:], dt_[:], ALU.mult, [lr, dt_], [lrdt])
                    P.tt("dve", th[:], li[:], dt_[:], ALU.mult, [li, dt_], [th])
                    reduce_pm_pi(thr[:], thr, th[:], t32a[:], t32a, [th])
                    yield
                    P.act(mag1[:], lrdt[:], AF.Exp, [lrdt], [mag1])
                    P.act(mag15[:], lrdt[:], AF.Exp, [lrdt], [mag15], scale=15.0)
                    P.act(r16[:], lrdt[:], AF.Exp, [lrdt], [r16], scale=16.0)
                    s2 = (slice(None), slice(None))
                    yield from sin_of(cth, thr[:], thr, PI / 2, t32a, t32b, s2)
                    yield
                    yield from sin_of(sth, thr[:], thr, 0.0, t32a, t32b, s2)
                    yield
                    P.tt("dve", ar[:], mag1[:], cth[:], ALU.mult, [mag1, cth], [ar])
                    P.tt("dve", ai[:], mag1[:], sth[:], ALU.mult, [mag1, sth], [ai])
                    P.ts("dve", nr[:], ar[:], -1.0, None, ALU.add, None, [ar], [nr])
                    P.tt("dve", den[:], lr[:], lr[:], ALU.mult, [lr], [den])
                    yield
                    P.tt("dve", t32a[:], li[:], li[:], ALU.mult, [li], [t32a])
                    P.tt("dve", den[:], den[:], t32a[:], ALU.add, [den, t32a], [den])
                    P.recip(den[:], den[:], [den], [den])
                    yield
                    P.tt("dve", t32a[:], nr[:], lr[:], ALU.mult, [nr, lr], [t32a])
                    P.tt("dve", t32b[:], ai[:], li[:], ALU.mult, [ai, li], [t32b])
                    P.tt("dve", t32a[:], t32a[:], t32b[:], ALU.add, [t32a, t32b], [t32a])
                    P.tt("dve", fre[:], t32a[:], den[:], ALU.mult, [t32a, den], [fre])
                    yield
                    P.tt("dve", t32b[:], ai[:], lr[:], ALU.mult, [ai, lr], [t32b])
                    P.tt("dve", t32c[:], nr[:], li[:], ALU.mult, [nr, li], [t32c])
                    P.tt("dve", t32b[:], t32b[:], t32c[:], ALU.subtract, [t32b, t32c], [t32b])
                    P.tt("dve", fim[:], t32b[:], den[:], ALU.mult, [t32b, den], [fim])
                    yield
                    s16 = (slice(None), slice(None), slice(0, 16))
                    fre_b = fre[:].unsqueeze(2).to_broadcast([128, 32, 16])
                    fim_b = fim[:].unsqueeze(2).to_broadcast([128, 32, 16])
                    P.tt("dve", tb17a[s16], Br[:], fre_b, ALU.mult, [Br, fre], [tb17a])
                    P.tt("dve", tb17b[s16], Bi[:], fim_b, ALU.mult, [Bi, fim], [tb17b])
                    P.tt("dve", X1[:], tb17a[s16], tb17b[s16], ALU.subtract, [tb17a, tb17b], [X1])
                    yield
                    P.tt("dve", tb17a[s16], Bi[:], fre_b, ALU.mult, [Bi, fre], [tb17a])
                    P.tt("dve", tb17b[s16], Br[:], fim_b, ALU.mult, [Br, fim], [tb17b])
                    P.tt("dve", X2[:], tb17a[s16], tb17b[s16], ALU.add, [tb17a, tb17b], [X2])
                    yield
                    thr_b16 = thr[:].unsqueeze(2).to_broadcast([128, 32, 16])
                    lrdt_b16 = lrdt[:].unsqueeze(2).to_broadcast([128, 32, 16])
                    tauN_b = tauN[:].unsqueeze(1).to_broadcast([128, 32, 16])
                    P.tt("dve", angP[s16], thr_b16, tauN_b, ALU.mult, [thr, tauN], [angP])
                    P.tt("dve", tb17a[s16], lrdt_b16, tauN_b, ALU.mult, [lrdt, tauN], [tb17a])
                    yield
                    P.act(magP[s16], tb17a[s16], AF.Exp, [tb17a], [magP])
                    yield
                    yield from sin_of(E1n, angP[s16], angP, ph[:, 0:1], tb17a, tb17b, s16)
                    P.tt("dve", E1n[:], E1n[:], magP[s16], ALU.mult, [E1n, magP], [E1n])
                    yield
                    yield from sin_of(E2n, angP[s16], angP, ph[:, 1:2], tb17a, tb17b, s16)
                    P.tt("dve", E2n[:], E2n[:], magP[s16], ALU.mult, [E2n, magP], [E2n])
                    yield
                    thr_b17 = thr[:].unsqueeze(2).to_broadcast([128, 32, 17])
                    lrdt_b17 = lrdt[:].unsqueeze(2).to_broadcast([128, 32, 17])
                    tauP_b = tauP[:].unsqueeze(1).to_broadcast([128, 32, 17])
                    P.tt("dve", angP[:], thr_b17, tauP_b, ALU.mult, [thr, tauP], [angP])
                    P.tt("dve", tb17a[:], lrdt_b17, tauP_b, ALU.mult, [lrdt, tauP], [tb17a])
                    yield
                    P.act(magP[:], tb17a[:], AF.Exp, [tb17a], [magP])
                    yield
                    yield from sin_of(F1, angP[:], angP, ph[:, 2:3], tb17a, tb17b)
                    P.tt("dve", F1[:], F1[:], magP[:], ALU.mult, [F1, magP], [F1])
                    yield
                    yield from sin_of(F2, angP[:], angP, ph[:, 3:4], tb17a, tb17b)
                    P.tt("dve", F2[:], F2[:], magP[:], ALU.mult, [F2, magP], [F2])
                    yield
                    P.ts("dve", t32c[:], thr[:], 16.0, None, ALU.mult, None, [thr], [t32c])
                    reduce_pm_pi(phir[:], phir, t32c[:], t32a[:], t32a, [t32c])
                    yield
                    P.ts("dve", t32c[:], thr[:], 15.0, None, ALU.mult, None, [thr], [t32c])
                    reduce_pm_pi(th15r[:], th15r, t32c[:], t32a[:], t32a, [t32c])
                    yield
                    P.tt("dve", Tb0[:], Tb[:, :, 0:128], halo_s[:].unsqueeze(1).to_broadcast([128, 8, 128]), ALU.add, [Tb, halo_s], [Tb0])
                    if DBG:
                        P.dma("sp", dbg_tb, Tb[:].rearrange("p h j -> p (h j)"), r=[Tb], w=[Buf()], sem="S_dbg")

                bg = background()

                with ExitStack() as scA:
                    wu = sb(scA, "wu", [128, 8, 512], BF16)
                    srcu = w_in.rearrange("(k p) n -> p k n", p=128)
                    def load_wu(dep):
                        for k2 in range(2):
                            P.dma("pool", wu[:, 4 * k2:4 * k2 + 4, :], srcu[:, 4 * k2:4 * k2 + 4, 768:1280], r=[dep], w=[wu], sem="S_wu")
                        P.seal("S_wu", [wu])
                    hT2 = [sb(scA, f"hT{i}", [128, 8, 1024], BF16) for i in range(2)]
                    xts = Ring([sb(scA, f"xtA{i}", [128, D], F32) for i in range(3)])
                    xss = Ring([sb(scA, f"xsA{i}", [128, D], BF16) for i in range(2)])
                    sss = Ring([sb(scA, f"ssA{i}", [128, 1], F32) for i in range(4)])
                    tmps = Ring([sb(scA, f"tmA{i}", [128, 1], F32) for i in range(4)])
                    rsts = Ring([sb(scA, f"rsA{i}", [128, 1], F32) for i in range(4)])

                    def proj_steps(hb):
                        rb, jh = divmod(hb, 2)
                        hTc = hT2[hb % 2]
                        for s in range(16):
                            bank = pfr5.next()
                            for k in range(8):
                                P.mm(bank[0:64, :], hTc[:, k, s:1024:16], wu[:, k, :], k == 0, k == 7, [hTc, wu], [bank])
                            P.cp("act" if s % 2 == 0 else "dve", X[jh * 64:(jh + 1) * 64, rb, :, s, :],
                                 bank[0:64, :].rearrange("p (g c) -> p g c", g=32), [bank], [X])
                            yield

                    prev = None
                    ntile = 0
                    for hb in range(4):
                        pend = None
                        hTc = hT2[hb % 2]
                        for i in range(8):
                            xt = xts.next()
                            tok0 = hb * 1024 + i * 128
                            P.dma("sp", xt[:], x[tok0:tok0 + 128, :], w=[xt], sem=f"S_xA{ntile % 3}")
                            ntile += 1
                            next(bg, None)
                            if ntile == 1:
                                next(bg, None)
                                depu = Buf()
                                depu.w = dict(xt.b.w)
                                load_wu(depu)
                            st2 = norm_transpose(xt, gpre, xss.next(), sss.next(), tmps.next(), rsts.next(),
                                                 hTc[:, :, i * 128:(i + 1) * 128], hTc, "act", defer=True)
                            if pend is not None:
                                pend()
                            pend = st2
                            if prev is not None:
                                next(prev, None)
                                next(prev, None)
                            next(bg, None)
                        pend()
                        if prev is not None:
                            for _ in prev:
                                pass
                        prev = proj_steps(hb)
                        if hb == 1:
                            dep = Buf()
                            dep.w = dict(hTc.b.w)

                            def wload_late(dst_tl, src, nk, sem):
                                srcv = src.rearrange("(k p) n -> p k n", p=128)
                                for k in range(nk):
                                    P.dma("pool", dst_tl[:, k, :], srcv[:, k, :], r=[dep], w=[dst_tl], sem=sem)
                                P.seal(sem, [dst_tl])
                            wload_late(wab, w_ab, 4, "S_wab")
                            wload_late(wsb_, w_sb, 4, "S_wsb")
                            wload_late(wglu, w_glu, 4, "S_wglu")
                            wload_late(wout, w_out, 8, "S_wout")
                    for _ in prev:
                        pass
                    for _ in bg:
                        pass
                    if DBG:
                        P.dma("sp", dbg_X, X[:].rearrange("p a g s c -> p (a g s c)"), r=[X], w=[Buf()], sem="S_dbg")
                    P.barrier()
                scS.close()
                if STOP == 3:
                    return True
                conv_jobs = []
                for k in range(8):
                    conv_jobs.append((wffi_b[k * 128:(k + 1) * 128, :], w_ffi[k * 128:(k + 1) * 128, :], cvb_i, "S_cvi"))
                for k in range(8):
                    conv_jobs.append((wffo_b[k * 512:(k + 1) * 512, :], w_ffo[k * 512:(k + 1) * 512, :], cvb_o, "S_cvo"))

                with ExitStack() as scB:
                    zc2 = sb(scB, "zc2", [128, 16, 128], BF16)
                    tmA = sb(scB, "tmA", [128, 4, 17, 16], F32)
                    tmB = sb(scB, "tmB", [128, 4, 17, 16], F32)
                    tmC = sb(scB, "tmC", [128, 4, 16, 16], F32)
                    tmD = sb(scB, "tmD", [128, 4, 16, 16], F32)
                    Pst = sb(scB, "Pst", [128, 4, 16, 16], BF16)
                    Qst = sb(scB, "Qst", [128, 4, 17, 16], BF16)
                    Toep = sb(scB, "Toep", [128, 4, 512], BF16)
                    Kt = sb(scB, "Kt", [16, 4, 256], BF16)
                    PT = sb(scB, "PT", [128, 4, 2, 128], BF16)
                    UT = [sb(scB, f"UT{rb}", [128, 4, 2, 128], BF16) for rb in range(2)]
                    psi = sb(scB, "psi", [128, 4, 256], F32)
                    C1p = sb(scB, "C1p", [128, 4, 256], F32)
                    S1p = sb(scB, "S1p", [128, 4, 256], F32)
                    tbA = sb(scB, "tbA", [128, 4, 256], F32)
                    tbB = sb(scB, "tbB", [128, 4, 256], F32)
                    C1 = sb(scB, "C1", [128, 4, 128], F32)
                    S1 = sb(scB, "S1", [128, 4, 128], F32)
                    Zt = sb(scB, "Zt", [128, 4, 256], F32)
                    Zts = sb(scB, "Zts", [128, 4, 256], F32)
                    Sts = sb(scB, "Sts", [128, 4, 256], F32)
                    Sb = sb(scB, "Sb", [128, 4, 128], BF16)
                    St = psi
                    rtab = C1p
                    pz = [pf[1], pf[2]]
                    pzs = [pf[3], pf[4]]
                    Jpos = sb(scB, "Jpos", [128, 256], F32)
                    P.ts("dve", Jpos[:], Jv[:], 1.0, None, ALU.min, None, [Jv], [Jpos])
                    sgn = ph[:, 5:6]
                    P.op("dve", lambda e: e.memset(Toep[:], 0.0), [], [Toep])

                    def sincos(cos_tl, cos_ap, sin_tl, sin_ap, ang_ap, ang_tl, tA_tl, tB_tl, tA_ap, tB_ap):
                        reduce_pm_pi(tB_ap, tB_tl, ang_ap, tA_ap, tA_tl, [ang_tl])
                        P.act(sin_ap, tB_ap, AF.Sin, [tB_tl], [sin_tl])
                        P.act(tA_ap, tB_ap, AF.Abs, [tB_tl], [tA_tl])
                        P.act(cos_ap, tA_ap, AF.Sin, [tA_tl], [cos_tl], bias=halfpi[:], scale=-1.0)

                    for gb in range(8):
                        g0 = gb * 4
                        gs = slice(g0, g0 + 4)
                        P.tt("dve", tmA[:], Y1[:, gs, :].unsqueeze(2).to_broadcast([128, 4, 17, 16]),
                             F1[:, gs, :].unsqueeze(3).to_broadcast([128, 4, 17, 16]), ALU.mult, [Y1, F1], [tmA])
                        P.tt("dve", tmB[:], Y2[:, gs, :].unsqueeze(2).to_broadcast([128, 4, 17, 16]),
                             F2[:, gs, :].unsqueeze(3).to_broadcast([128, 4, 17, 16]), ALU.mult, [Y2, F2], [tmB])
                        P.tt("dve", Qst[:], tmA[:], tmB[:], ALU.add, [tmA, tmB], [Qst])
                        P.tt("dve", tmC[:], X1[:, gs, :].unsqueeze(2).to_broadcast([128, 4, 16, 16]),
                             E1n[:, gs, :].unsqueeze(3).to_broadcast([128, 4, 16, 16]), ALU.mult, [X1, E1n], [tmC])
                        P.tt("dve", tmD[:], X2[:, gs, :].unsqueeze(2).to_broadcast([128, 4, 16, 16]),
                             E2n[:, gs, :].unsqueeze(3).to_broadcast([128, 4, 16, 16]), ALU.mult, [X2, E2n], [tmD])
                        P.tt("dve", Pst[:], tmC[:], tmD[:], ALU.add, [tmC, tmD], [Pst])
                        for gp in range(2):
                            bank = pf[0] if gp == 0 else pf[5]
                            for gl in range(2):
                                g = gp * 2 + gl
                                P.mm(bank[0:16, gl * 256:(gl + 1) * 256], Pst[:, g, 0, :],
                                     Qst[:, g, 0:16, :].rearrange("p a b -> p (a b)"), True, True, [Pst, Qst], [bank], signal=(gl == 1))
                            P.cp("act", Kt[:, gp * 2:gp * 2 + 2, :].rearrange("p g n -> p (g n)"), bank[0:16, :], [bank], [Kt])
                        bank = pb[0]
                        for g in range(4):
                            for q in range(2):
                                P.tp(bank[:, (g * 2 + q) * 128:(g * 2 + q + 1) * 128],
                                     Pst[:, g, 8 * q:8 * q + 8, :].rearrange("p a b -> p (a b)"), ident[:], [Pst, ident], [bank],
                                     signal=(g == 3 and q == 1))
                        P.cp("act", PT[:].rearrange("p g q m -> p (g q m)"), bank[:, :], [bank], [PT])
                        for rb in range(2):
                            bank = pb[1]
                            for g in range(4):
                                for q in range(2):
                                    P.tp(bank[:, (g * 2 + q) * 128:(g * 2 + q + 1) * 128],
                                         X[:, rb, g0 + g, 8 * q:8 * q + 8, :].rearrange("p a b -> p (a b)"), ident[:], [X, ident], [bank],
                                         signal=(g == 3 and q == 1))
                            P.cp("act", UT[rb][:].rearrange("p g q m -> p (g q m)"), bank[:, :], [bank], [UT[rb]])
                        P.tt("dve", psi[:], phir[:, gs].unsqueeze(2).to_broadcast([128, 4, 256]),
                             Jv[:].unsqueeze(1).to_broadcast([128, 4, 256]), ALU.mult, [phir, Jv], [psi])
                        sincos(C1, C1[:], S1, S1[:], psi[:, :, 127:255], psi, Zt, Zts, Zt[:, :, 0:128], Zts[:, :, 0:128])
                        P.tt("dve", Kt[:, :, 0:16], Kt[:, :, 0:16], ddiag[:, gs, :], ALU.add, [Kt, ddiag], [Kt])
                        for q in range(2):
                            for sl in range(8):
                                sft = 8 * q + sl
                                P.dma("sp", Toep[16 * sl:16 * sl + 16, :, q * 256 + 16 * sft:(q + 1) * 256],
                                      Kt[:, :, 0:256 - 16 * sft], r=[Kt], w=[Toep], sem="S_toep")
                        P.seal("S_toep", [Toep])
                        P.tt("dve", psi[:], psi[:], th15r[:, gs].unsqueeze(2).to_broadcast([128, 4, 256]), ALU.subtract, [psi, th15r], [psi])
                        sincos(C1p, C1p[:], S1p, S1p[:], psi[:], psi, tbA, tbB, tbA[:], tbB[:])
                        for rb in range(2):
                            for g in range(4):
                                cs = slice(g * 128, (g + 1) * 128)
                                for q in range(2):
                                    P.mm(pz[rb][:, cs], PT[:, g, q, :], UT[rb][:, g, q, :], q == 0, q == 1, [PT, UT[rb]], [pz[rb]],
                                         signal=(q == 1 and g == 3))
                            for g in range(4):
                                cs = slice(g * 128, (g + 1) * 128)
                                for q in range(2):
                                    P.mm(pzs[rb][0:64, cs], PT[:, g, q, 64:128], UT[rb][:, g, q, :], q == 0, q == 1, [PT, UT[rb]], [pzs[rb]],
                                         signal=False)
                                for q in range(2):
                                    P.mm(pzs[rb][64:128, cs], PT[:, g, q, 0:64], UT[rb][:, g, q, :], q == 0, q == 1, [PT, UT[rb]], [pzs[rb]],
                                         signal=(q == 1 and g == 3))
                        m15b = mag15[:, gs].unsqueeze(2).to_broadcast([128, 4, 128])
                        P.tt("dve", C1[:], C1[:], m15b, ALU.mult, [C1, mag15], [C1])
                        P.tt("dve", S1[:], S1[:], m15b, ALU.mult, [S1, mag15], [S1])
                        for rb in range(2):
                            js = slice(rb * 128, (rb + 1) * 128)
                            zv = pz[rb][:, :].rearrange("p (g j) -> p g j", g=4)
                            zsv = pzs[rb][:, :].rearrange("p (g j) -> p g j", g=4)
                            P.tt("dve", tbA[:, :, js], zv, C1p[:, :, js], ALU.mult, [pz[rb], C1p], [tbA])
                            P.stt("dve", tbB[:, :, js], zsv, sgn, S1p[:, :, js], ALU.mult, ALU.mult, [pzs[rb], S1p, ph], [tbB])
                            P.tt("dve", Zt[:, :, js], tbA[:, :, js], tbB[:, :, js], ALU.add, [tbA, tbB], [Zt])
                            P.tt("dve", tbA[:, :, js], zsv, C1p[:, :, js], ALU.mult, [pzs[rb], C1p], [tbA])
                            P.stt("dve", tbB[:, :, js], zv, sgn, S1p[:, :, js], ALU.mult, ALU.mult, [pz[rb], S1p, ph], [tbB])
                            P.tt("dve", Zts[:, :, js], tbA[:, :, js], tbB[:, :, js], ALU.subtract, [tbA, tbB], [Zts])
                        P.tt("dve", rtab[:], r16[:, gs].unsqueeze(2).to_broadcast([128, 4, 256]),
                             Jpos[:].unsqueeze(1).to_broadcast([128, 4, 256]), ALU.mult, [r16, Jpos], [rtab])
                        P.scan(St[:].rearrange("p g j -> p (g j)"), rtab[:].rearrange("p g j -> p (g j)"),
                               Zt[:].rearrange("p g j -> p (g j)"), [rtab, Zt], [St])
                        P.scan(Sts[:].rearrange("p g j -> p (g j)"), rtab[:].rearrange("p g j -> p (g j)"),
                               Zts[:].rearrange("p g j -> p (g j)"), [rtab, Zts], [Sts])
                        P.tt("dve", tbA[:, :, 0:128], St[:, :, 127:255], C1[:], ALU.mult, [St, C1], [tbA])
                        P.stt("dve", tbB[:, :, 0:128], Sts[:, :, 127:255], sgn, S1[:], ALU.mult, ALU.mult, [Sts, S1, ph], [tbB])
                        P.tt("dve", Sb[:], tbA[:, :, 0:128], tbB[:, :, 0:128], ALU.subtract, [tbA, tbB], [Sb])
                        depc = Buf()
                        depc.w = dict(Sb.b.w)
                        for _ in range(2):
                            o_, i_, cb_, sm_ = conv_jobs.pop(0)
                            P.dma("pool", o_, i_, r=[depc], w=[cb_], sem=sm_)
                        if gb == 7:
                            P.seal("S_cvi", [cvb_i])
                            P.seal("S_cvo", [cvb_o])
                        for g in range(4):
                            bank = pf[0] if g % 2 == 0 else pf[5]
                            cs = slice(0, 256)
                            P.mm(bank[:, cs], UT[1][:, g, 0, :], Toep[:, g, 0:256], True, False, [UT[1], Toep], [bank], signal=False)
                            P.mm(bank[:, cs], UT[1][:, g, 1, :], Toep[:, g, 256:512], False, False, [UT[1], Toep], [bank], signal=False)
                            P.mm(bank[:, cs], Sb[:, g, :], Qst[:, g, 1:17, :].rearrange("p a b -> p (a b)"), False, True, [Sb, Qst], [bank])
                            gl = (g0 + g) % 8
                            P.act(zc2[:, :, gl * 16:(gl + 1) * 16], bank[:, cs].rearrange("p (s c) -> p s c", s=16),
                                  AF.Gelu_apprx_tanh, [bank], [zc2])
                        if gb % 2 == 1:
                            cc = gb // 2
                            for sh in range(2):
                                bank = pb[sh]
                                for s8 in range(8):
                                    s = sh * 8 + s8
                                    P.tp(bank[:, s8 * 128:(s8 + 1) * 128], zc2[:, s, :], ident[:], [zc2, ident], [bank], signal=(s8 == 7))
                                P.cp("act",
                                     zT[:, cc, :].rearrange("p (j s) -> p s j", s=16)[:, sh * 8:(sh + 1) * 8, :],
                                     bank[:, :].rearrange("p (s j) -> p s j", s=8), [bank], [zT])
                    if DBG:
                        P.dma("sp", dbg_zT, zT[:], r=[zT], w=[Buf()], sem="S_dbg")
                    P.barrier()
                    if STOP == 4:
                        return True

            with ExitStack() as scC:
                wq = sb(scC, "wq", [128, 8, 512], BF16)
                wk = sb(scC, "wk", [128, 8, 2, 128], BF16)
                wv = sb(scC, "wv", [128, 8, 128], BF16)
                wg = sb(scC, "wg", [128, 8, 2048], BF16)
                srcw = w_in.rearrange("(k p) n -> p k n", p=128)
                P.dma("pool", wv[:, :, :], srcw[:, :, 640:768], w=[wv], sem="S_wv")
                for kv in range(2):
                    for hh in range(2):
                        P.dma("pool", wk[:, :, kv, hh * 64:(hh + 1) * 64], srcw[:, :, 512 + kv * 64:512 + (kv + 1) * 64], w=[wk], sem="S_wk")
                P.seal("S_wv", [wv])
                P.seal("S_wk", [wk])
                for k2 in range(2):
                    P.dma("pool", wq[:, 4 * k2:4 * k2 + 4, :], srcw[:, 4 * k2:4 * k2 + 4, 0:512], w=[wq], sem="S_wq")
                P.seal("S_wq", [wq])
                depq = Buf()
                depq.w = dict(wq.b.w)
                for k2 in range(4):
                    P.dma("pool", wg[:, 2 * k2:2 * k2 + 2, :], srcw[:, 2 * k2:2 * k2 + 2, 1280:3328], r=[depq], w=[wg], sem="S_wg")
                P.seal("S_wg", [wg])

                xg = [sb(scC, f"xg{i}", [128, D], F32) for i in range(2)]
                xgr = Ring(xg)
                xrr = [sb(scC, f"xr{i}", [128, D], F32) for i in range(3)]
                xr_i = [0]
                xss = Ring([sb(scC, f"xsC{i}", [128, D], BF16) for i in range(2)])
                sss = Ring([sb(scC, f"ssC{i}", [128, 1], F32) for i in range(4)])
                tmps = Ring([sb(scC, f"tmC{i}", [128, 1], F32) for i in range(4)])
                rsts = Ring([sb(scC, f"rsC{i}", [128, 1], F32) for i in range(4)])
                hTg = sb(scC, "hTg", [128, 8, 512], BF16)
                qT = sb(scC, "qT", [128, 4, 512], BF16)
                kT = sb(scC, "kT", [128, 2, 640], BF16)
                vtok = sb(scC, "vtok", [128, 5, 128], BF16)
                attT = sb(scC, "attT", [128, 4, 512], BF16)
                zg = sb(scC, "zg", [128, 4, 512], BF16)
                sgt = Ring([sb(scC, f"sgt{i}", [128, 512], BF16) for i in range(3)])
                mT = sb(scC, "mT", [128, 8, 512], BF16)
                slog2 = [sb(scC, f"slog{i}", [128, 4, 256], F32) for i in range(2)]
                Pm2 = [sb(scC, f"Pm{i}", [128, 4, 256], BF16) for i in range(2)]
                PTs2 = [sb(scC, f"PTs{i}", [128, 4, 2, 128], BF16) for i in range(2)]
                attn2 = [sb(scC, f"attn{i}", [128, 512], BF16) for i in range(2)]
                mx2 = [sb(scC, f"mx{i}", [128, 4], F32) for i in range(2)]
                nmx2 = [sb(scC, f"nmx{i}", [128, 4], F32) for i in range(2)]
                rs2 = [sb(scC, f"rs{i}", [128, 4], F32) for i in range(4)]
                es2 = [sb(scC, f"es{i}", [128, 4], F32) for i in range(4)]
                dn2 = [sb(scC, f"dn{i}", [128, 4], F32) for i in range(4)]
                t1s = Ring([sb(scC, f"t1s{i}", [128, 512], F32) for i in range(2)])
                t2s = Ring([sb(scC, f"t2s{i}", [128, 512], F32) for i in range(1)])
                x1t = Ring([sb(scC, f"x1t{i}", [128, D], F32) for i in range(1)])
                ssp = Ring([sb(scC, f"ssp{i}", [128, 2], F32) for i in range(2)])
                ss1 = Ring([sb(scC, f"ss1{i}", [128, 1], F32) for i in range(2)])

                def proj_kv(src_hT_ap_fn, ntiles, hT_tl, kcol0, vt0):
                    n = ntiles * 128
                    for kv in range(2):
                        bank = pfr.next()
                        for k in range(8):
                            P.mm(bank[:, 0:n], wk[:, k, kv, :], src_hT_ap_fn(k, 0, n), k == 0, k == 7, [wk, hT_tl], [bank])
                        P.cp("act", kT[:, kv, kcol0:kcol0 + n], bank[:, 0:n], [bank], [kT])
                    for t in range(ntiles):
                        bank = pfr.next()
                        for k in range(8):
                            P.mm(bank[:, 0:128], src_hT_ap_fn(k, t * 128, 128), wv[:, k, :], k == 0, k == 7, [hT_tl, wv], [bank])
                        P.cp("dve", vtok[:, vt0 + t, :], bank[:, 0:128], [bank], [vtok])

                xt = xgr.next()
                P.dma("sp", xt[:], x[1920:2048, :], w=[xt], sem="S_xC0")
                norm_transpose(xt, gpre, xss.next(), sss.next(), tmps.next(), rsts.next(),
                               hTg[:, :, 0:128], hTg, "act")
                proj_kv(lambda k, c0, n: hTg[:, k, c0:c0 + n], 1, hTg, 0, 0)

                if STOP == 41:
                    return True
                xc_cnt = [1]

                def norm_tile_C(Gn, t):
                    xt = xgr.next()
                    tok0 = 2048 + Gn * 512 + t * 128
                    P.dma("sp", xt[:], x[tok0:tok0 + 128, :], w=[xt], sem=f"S_xC{xc_cnt[0] % 2}")
                    xc_cnt[0] += 1
                    return norm_transpose(xt, gpre, xss.next(), sss.next(), tmps.next(), rsts.next(),
                                          hTg[:, :, t * 128:(t + 1) * 128], hTg, "act", defer=True)

                for G in range(4):
                    m0 = G * 512
                    if G == 0:
                        pend = None
                        for t in range(4):
                            st2 = norm_tile_C(0, t)
                            if pend is not None:
                                pend()
                            pend = st2
                        pend()
                    if STOP == 42 and G == 0:
                        return True
                    for c in range(4):
                        bank = pfr.next()
                        for k in range(8):
                            P.mm(bank[:, :], wq[:, k, c * 128:(c + 1) * 128], hTg[:, k, :], k == 0, k == 7, [wq, hTg], [bank])
                        P.cp("act" if c % 2 == 0 else "dve", qT[:, c, :], bank[:, :], [bank], [qT])
                    proj_kv(lambda k, c0, n: hTg[:, k, c0:c0 + n], 4, hTg, 128, 1)
                    if STOP == 43 and G == 0:
                        return True
                    def S1(u):
                        t, hh = divmod(u, 2)
                        p = u % 2
                        for hl in range(4):
                            h = hh * 4 + hl
                            bank = pf[2 * p + (hl % 2)]
                            half = hl // 2
                            hs = slice(64 * (hl % 2), 64 * (hl % 2) + 64)
                            P.mm(bank[:, half * 256:half * 256 + 256], qT[hs, h // 2, t * 128:(t + 1) * 128],
                                 kT[hs, hh, t * 128:t * 128 + 256], True, True, [qT, kT], [bank], signal=(hl >= 2))

                    def S2(u):
                        t, hh = divmod(u, 2)
                        p = u % 2
                        first = (G == 0 and t == 0)
                        slog, Pm, mx, nmx, rs, es_ = slog2[p], Pm2[p], mx2[p], nmx2[p], rs2[u % 4], es2[u % 4]
                        for par in range(2):
                            bank = pf[2 * p + par]
                            bv = bank[:, :].rearrange("p (h j) -> p h j", h=2)
                            lsl = slice(par, par + 3, 2)
                            gsl = slice(hh * 4 + par, hh * 4 + par + 3, 2)
                            if first:
                                P.stt("dve", slog[:, lsl, 0:128], bv[:, :, 0:128], 0.125, Tb0[:, gsl, :],
                                      ALU.mult, ALU.add, [bank, Tb0], [slog])
                                P.stt("dve", slog[:, lsl, 128:256], bv[:, :, 128:256], 0.125, Tb[:, gsl, 128:256],
                                      ALU.mult, ALU.add, [bank, Tb], [slog])
                            else:
                                P.stt("dve", slog[:, lsl, :], bv, 0.125, Tb[:, gsl, :], ALU.mult, ALU.add, [bank, Tb], [slog])
                        sk = sinks[:, hh * 4:hh * 4 + 4]
                        P.rmax(mx[:], slog[:], [slog], [mx])
                        P.tt("dve", mx[:], mx[:], sk, ALU.max, [mx, sinks], [mx])
                        P.ts("dve", nmx[:], mx[:], -1.0, None, ALU.mult, None, [mx], [nmx])
                        P.tt("dve", es_[:], sk, mx[:], ALU.subtract, [sinks, mx], [es_])
                        for hl in range(4):
                            P.act(Pm[:, hl, :], slog[:, hl, :], AF.Exp, [slog, nmx], [Pm, rs], bias=nmx[:, hl:hl + 1], accum=rs[:, hl:hl + 1])
                        P.act(es_[:], es_[:], AF.Exp, [es_], [es_])

                    def S3(u):
                        p = u % 2
                        bank = pb[p]
                        for hl in range(4):
                            for kc in range(2):
                                P.tp(bank[:, (hl * 2 + kc) * 128:(hl * 2 + kc + 1) * 128], Pm2[p][:, hl, kc * 128:(kc + 1) * 128], ident[:],
                                     [Pm2[p], ident], [bank], signal=(hl == 3 and kc == 1))
                        P.cp("act", PTs2[p][:].rearrange("p h k q -> p (h k q)"), bank[:, :], [bank], [PTs2[p]])

                    def S4(u):
                        t, hh = divmod(u, 2)
                        p = u % 2
                        po = pf[4 + p]
                        dn, rs, es_ = dn2[u % 4], rs2[u % 4], es2[u % 4]
                        P.tt("dve", dn[:], rs[:], es_[:], ALU.add, [rs, es_], [dn])
                        P.recip(dn[:], dn[:], [dn], [dn])
                        for hl in range(4):
                            for kc in range(2):
                                P.mm(po[:, hl * 64:(hl + 1) * 64], PTs2[p][:, hl, kc, :], vtok[:, t + kc, hh * 64:hh * 64 + 64],
                                     kc == 0, kc == 1, [PTs2[p], vtok], [po], signal=(hl == 3 and kc == 1))
                        at = attn2[t % 2]
                        P.tt("dve", at[:, hh * 256:(hh + 1) * 256].rearrange("p (h d) -> p h d", h=4),
                             po[:, 0:256].rearrange("p (h d) -> p h d", h=4),
                             dn2[u % 4][:].unsqueeze(2).to_broadcast([128, 4, 64]), ALU.mult, [po, dn2[u % 4]], [at])
                        if hh == 1:
                            bank = pb[p]
                            for c in range(4):
                                P.tp(bank[:, c * 128:(c + 1) * 128], at[:, c * 128:(c + 1) * 128], ident[:], [at, ident], [bank], signal=(c == 3))
                            P.cp("act", attT[:, :, t * 128:(t + 1) * 128], bank[:, 0:512].rearrange("p (c j) -> p c j", c=4), [bank], [attT])

                    NU = 8
                    for i in range(NU + 3):
                        if i < NU:
                            S1(i)
                        if 0 <= i - 1 < NU:
                            S2(i - 1)
                        if 0 <= i - 2 < NU:
                            S3(i - 2)
                        if 0 <= i - 3 < NU:
                            S4(i - 3)
                    if STOP == 44 and G == 0:
                        return True
                    P.cp("dve", kT[:, :, 0:128], kT[:, :, 512:640], [kT], [kT])
                    P.cp("dve", vtok[:, 0, :], vtok[:, 4, :], [vtok], [vtok])
                    if STOP == 45 and G == 0:
                        return True
                    for co in range(4):
                        bank = pfr.next()
                        for c in range(4):
                            P.mm(bank[:, :], wglu[:, c, co * 128:(co + 1) * 128], zT[:, c, m0:m0 + 512], c == 0, c == 3, [wglu, zT], [bank])
                        sg = sgt.next()
                        P.act(sg[:], bank[:, :], AF.Sigmoid, [bank], [sg])
                        P.tt("dve", zg[:, co, :], zT[:, co, m0:m0 + 512], sg[:], ALU.mult, [zT, sg], [zg])
                    if STOP == 46 and G == 0:
                        return True
                    for fo in range(8):
                        fs = slice(fo * 128, (fo + 1) * 128)
                        bga = pfr.next()
                        for k in range(8):
                            P.mm(bga[:, :], wg[:, k, fo * 128:(fo + 1) * 128], hTg[:, k, :], k == 0, k == 7, [wg, hTg], [bga])
                        sga = sgt.next()
                        P.act(sga[:], bga[:, :], AF.Sigmoid, [bga], [sga])
                        bgs = pfr.next()
                        for k in range(8):
                            P.mm(bgs[:, :], wg[:, k, 1024 + fo * 128:1024 + (fo + 1) * 128], hTg[:, k, :], k == 0, k == 7, [wg, hTg], [bgs])
                        sgs = sgt.next()
                        P.act(sgs[:], bgs[:, :], AF.Sigmoid, [bgs], [sgs])
                        ba = pfr.next()
                        for c in range(4):
                            P.mm(ba[:, :], wab[:, c, fs], attT[:, c, :], c == 0, c == 3, [wab, attT], [ba])
                        t1 = t1s.next()
                        P.tt("dve", t1[:], ba[:, :], sga[:], ALU.mult, [ba, sga], [t1])
                        bs = pfr.next()
                        for c in range(4):
                            P.mm(bs[:, :], wsb_[:, c, fs], zg[:, c, :], c == 0, c == 3, [wsb_, zg], [bs])
                        t2 = t2s.next()
                        P.tt("dve", t2[:], bs[:, :], sgs[:], ALU.mult, [bs, sgs], [t2])
                        P.tt("dve", mT[:, fo, :], t1[:], t2[:], ALU.add, [t1, t2], [mT])
                    if STOP == 47 and G == 0:
                        return True
                    xrs, xr_sem = {}, {}

                    def load_xr(t):
                        idx = xr_i[0] % 3
                        xr_i[0] += 1
                        xr = xrr[idx]
                        tok0 = 2048 + m0 + t * 128
                        P.dma("sp", xr[:], x[tok0:tok0 + 128, :], w=[xr], sem=f"S_xR{idx}")
                        xrs[t] = xr
                        xr_sem[t] = idx

                    load_xr(0)
                    load_xr(1)
                    pendn = None
                    for t in range(4):
                        if G + 1 < 4:
                            st2n = norm_tile_C(G + 1, t)
                            if pendn is not None:
                                pendn()
                            pendn = st2n
                        b0, b1 = pfr.next(), pfr.next()
                        for half, bank in ((0, b0), (1, b1)):
                            for k in range(8):
                                P.mm(bank[:, :], mT[:, k, t * 128:(t + 1) * 128], wout[:, k, half * 512:(half + 1) * 512], k == 0, k == 7,
                                     [mT, wout], [bank])
                        sp_ = ssp.next()
                        j0, j1 = sgt.next(), sgt.next()
                        P.act(j0[:], b0[:, :], AF.Square, [b0], [j0, sp_], accum=sp_[:, 0:1])
                        P.act(j1[:], b1[:, :], AF.Square, [b1], [j1, sp_], accum=sp_[:, 1:2])
                        s1 = ss1.next()
                        P.tt("dve", s1[:], sp_[:, 0:1], sp_[:, 1:2], ALU.add, [sp_], [s1])
                        tm, rsd = tmps.next(), rsts.next()
                        rstd_from_ss(s1[:], s1, rsd, tm)
                        xo = x1t.next()
                        for half, bank in ((0, b0), (1, b1)):
                            hs_ = slice(half * 512, (half + 1) * 512)
                            P.stt("dve", xo[:, hs_], bank[:, :], rsd[:], gpost[:, hs_], ALU.mult, ALU.mult, [bank, rsd, gpost], [xo])
                        xr = xrs[t]
                        P.tt("dve", xr[:], xo[:], xr[:], ALU.add, [xo, xr], [xr])
                        r0 = m0 + t * 128
                        P.dma("pool", x1scr[r0:r0 + 128, :], xr[:], r=[xr], w=[Buf()], sem=f"S_x1w{xr_sem[t]}")
                        if t + 2 < 4:
                            load_xr(t + 2)
                    if pendn is not None:
                        pendn()
                P.barrier()
                if STOP == 5:
                    return True

            scABC.close()
            with ExitStack() as scD:
                wffi = sb(scD, "wffi", [128, 8, 4096], BF16)
                wffo = sb(scD, "wffo", [128, 32, D], BF16)
                wffi_bv = wffi_b.rearrange("(k p) n -> p k n", p=128)
                wffo_bv = wffo_b.rearrange("(k p) n -> p k n", p=128)
                wffi_q = [Buf() for _ in range(4)]
                for q4 in range(4):
                    P.dma("act", wffi[:, :, q4 * 1024:(q4 + 1) * 1024], wffi_bv[:, :, q4 * 1024:(q4 + 1) * 1024],
                          r=[cvb_i], w=[wffi_q[q4]], sem=f"S_wffi{q4}")
                wffo_p = [Buf() for _ in range(8)]
                depw = Buf()
                depw.w = dict(wffi_q[3].w)
                for k4 in range(8):
                    P.dma("pool", wffo[:, 4 * k4:4 * k4 + 4, :], wffo_bv[:, 4 * k4:4 * k4 + 4, :], r=[cvb_o, depw], w=[wffo_p[k4]],
                          sem=f"S_wffo{k4}")
                g2pre = sb(scD, "g2pre", [128, D], F32)
                g2post = sb(scD, "g2post", [128, D], F32)
                P.dma("sp", g2pre[:], gains[2, :, :], w=[g2pre], sem="S_c3")
                P.dma("sp", g2post[:], gains[3, :, :], w=[g2post], sem="S_c3")
                P.seal("S_c3", [g2pre, g2post])
                x1g = Ring([sb(scD, f"x1g{i}", [128, D], F32) for i in range(4)])
                xss = Ring([sb(scD, f"xsD{i}", [128, D], BF16) for i in range(2)])
                sss = Ring([sb(scD, f"ssD{i}", [128, 1], F32) for i in range(6)])
                tmps = Ring([sb(scD, f"tmD{i}", [128, 1], F32) for i in range(4)])
                rsts = Ring([sb(scD, f"rsD{i}", [128, 1], F32) for i in range(4)])
                h2T = sb(scD, "h2T", [128, 8, 256], BF16)
                ffT = sb(scD, "ffT", [128, 32, 256], BF16)
                rl = Ring([sb(scD, f"rl{i}", [128, 512], BF16) for i in range(2)])
                ot = Ring([sb(scD, f"ot{i}", [128, D], F32) for i in range(2)])
                ssp = Ring([sb(scD, f"sspD{i}", [128, 2], F32) for i in range(2)])
                ss1 = Ring([sb(scD, f"ss1D{i}", [128, 1], F32) for i in range(2)])
                outb = Buf()
                h2Tb = sb(scD, "h2Tb", [128, 8, 256], BF16)
                h2T2 = [h2T, h2Tb]
                xtiles = {}

                def norm_group(Gn, defer):
                    st2s = []
                    tl = []
                    for t in range(2):
                        xt = x1g.next()
                        r0 = Gn * 256 + t * 128
                        P.dma("sp", xt[:], x1scr[r0:r0 + 128, :], w=[xt], sem=f"S_xD{(Gn * 2 + t) % 4}")
                        tl.append(xt)
                        hdst = h2T2[Gn % 2]
                        st2s.append(norm_transpose(xt, g2pre, xss.next(), sss.next(), tmps.next(), rsts.next(),
                                                   hdst[:, :, t * 128:(t + 1) * 128], hdst, "act", defer=True))
                    xtiles[Gn] = tl
                    if defer:
                        return st2s
                    for f in st2s:
                        f()
                    return []

                norm_group(0, False)
                for G in range(8):
                    m0 = G * 256
                    xs_g = xtiles[G]
                    hcur = h2T2[G % 2]
                    for fp in range(16):
                        bank = pfr.next()
                        for j in range(2):
                            fc = fp * 2 + j
                            for k in range(8):
                                P.mm(bank[:, j * 256:(j + 1) * 256], wffi[:, k, fc * 128:(fc + 1) * 128], hcur[:, k, :], k == 0, k == 7,
                                     [wffi_q[fc // 8], hcur], [bank], signal=(j == 1 and k == 7))
                        r_ = rl.next()
                        P.act(r_[:], bank[:, :], AF.Relu, [bank], [r_])
                        P.tt("dve", ffT[:, fp * 2:fp * 2 + 2, :], r_[:].rearrange("p (a n) -> p a n", a=2),
                             r_[:].rearrange("p (a n) -> p a n", a=2), ALU.mult, [r_], [ffT])
                    nxt = norm_group(G + 1, True) if G + 1 < 8 else []
                    for t in range(2):
                        b0, b1 = pfr.next(), pfr.next()
                        for half, bank in ((0, b0), (1, b1)):
                            for fc in range(32):
                                P.mm(bank[:, :], ffT[:, fc, t * 128:(t + 1) * 128], wffo[:, fc, half * 512:(half + 1) * 512], fc == 0, fc == 31,
                                     [ffT, wffo_p[fc // 4]], [bank])
                        if t == 0:
                            for f in nxt:
                                f()
                        sp_ = ssp.next()
                        j0, j1 = rl.next(), rl.next()
                        P.act(j0[:], b0[:, :], AF.Square, [b0], [j0, sp_], accum=sp_[:, 0:1])
                        P.act(j1[:], b1[:, :], AF.Square, [b1], [j1, sp_], accum=sp_[:, 1:2])
                        s1 = ss1.next()
                        P.tt("dve", s1[:], sp_[:, 0:1], sp_[:, 1:2], ALU.add, [sp_], [s1])
                        tm, rsd = tmps.next(), rsts.next()
                        rstd_from_ss(s1[:], s1, rsd, tm)
                        xo = ot.next()
                        for half, bank in ((0, b0), (1, b1)):
                            hs_ = slice(half * 512, (half + 1) * 512)
                            P.stt("dve", xo[:, hs_], bank[:, :], rsd[:], g2post[:, hs_], ALU.mult, ALU.mult, [bank, rsd, g2post], [xo])
                        P.tt("dve", xo[:], xo[:], xs_g[t][:], ALU.add, [xo, xs_g[t]], [xo])
                        r0 = m0 + t * 128
                        P.dma("pool", out_d[r0:r0 + 128, :], xo[:], r=[xo], w=[outb], sem=f"S_ow{(G * 2 + t) % 2}")
                P.barrier()


            return False

        if body():
            scS.close()
            scABC.close()
            P.barrier()

        block = es.enter_context(nc.Block())
        P.emit(block)
    return nc


def _bucket(d):
    d = np.asarray(d)
    df = np.maximum(d, 1).astype(np.float32)
    large = 16 + (np.log(df / np.float32(16)) / np.float32(math.log(128 / 16)) * np.float32(16)).astype(np.int32)
    large = np.minimum(large, 31)
    return np.where(d < 16, d, large)


def _constants():
    c = {}
    c["ident"] = np.eye(128, dtype=np.float32)
    oh = np.zeros((33, 384), np.float32)
    for m in range(383):
        d = 255 - m
        if 0 <= d < 128:
            oh[int(_bucket(d)), m] = 1.0
        else:
            oh[32, m] = 1.0
    c["oh"] = oh
    cm = np.zeros((128, 2, 256), np.float32)
    iq = np.zeros((128, 2, 256), np.float32)
    for p in range(128):
        sl, cc = divmod(p, 16)
        for q in range(2):
            s = 8 * q + sl
            for s2 in range(16):
                if s2 >= s:
                    cm[p, q, s2 * 16:(s2 + 1) * 16] = 1.0
            iq[p, q, s * 16 + cc] = 1.0
    c["cmask"], c["identq"] = cm, iq
    ph = np.zeros((128, 8), np.float32)
    top, bot = slice(0, 64), slice(64, 128)
    ph[top, 0], ph[bot, 0] = PI / 2, 0.0
    ph[top, 1], ph[bot, 1] = PI, PI / 2
    ph[top, 2], ph[bot, 2] = PI / 2, PI
    ph[top, 3], ph[bot, 3] = PI, 3 * PI / 2
    ph[top, 4], ph[bot, 4] = 0.0, PI
    ph[top, 5], ph[bot, 5] = 1.0, -1.0
    c["ph"] = ph
    c["tauN"] = np.tile(-np.arange(16, dtype=np.float32), (128, 1))
    c["tauP"] = np.tile(np.arange(17, dtype=np.float32), (128, 1))
    c["Jv"] = np.tile(np.arange(256, dtype=np.float32), (128, 1))
    return c


def _prep_inputs(inp):
    f = lambda a: np.ascontiguousarray(np.asarray(a, dtype=np.float32))
    shared = dict(_constants())
    shared["w_in"] = f(inp["w_in"][0])
    shared["w_glu"] = f(inp["w_glu"][0])
    shared["w_ab"] = f(inp["w_attn_branch"][0])
    shared["w_sb"] = f(inp["w_ssm_branch"][0])
    shared["w_out"] = f(inp["w_out"][0])
    shared["w_ffi"] = f(inp["w_ff_in"][0])
    shared["w_ffo"] = f(inp["w_ff_out"][0])
    gains = np.stack([inp["norm_mix_pre"][0], inp["norm_mix_post"][0], inp["norm_mlp_pre"][0], inp["norm_mlp_post"][0]])
    shared["gains"] = f(np.broadcast_to(np.asarray(gains, np.float32)[:, None, :], (4, 128, D)))
    relb = np.empty((33, 8, 128), np.float32)
    relb[:32] = np.asarray(inp["rel_bias"], np.float32)[:, :, None]
    relb[32] = NEG
    shared["relb"] = relb
    shared["sinks"] = f(np.broadcast_to(np.asarray(inp["sinks"][0], np.float32)[None, :], (128, 8)))
    dup = lambda a: f(np.concatenate([a, a], axis=0))
    shared["lamr"] = dup(np.asarray(inp["lam_re"][0], np.float32).T)
    shared["lami"] = dup(np.asarray(inp["lam_im"][0], np.float32).T)
    shared["ldt"] = f(np.broadcast_to(np.asarray(inp["log_dt"][0], np.float32)[None, :], (128, 32)))
    shared["bre"] = dup(np.transpose(np.asarray(inp["b_re"][0], np.float32), (1, 0, 2)))
    shared["bim"] = dup(np.transpose(np.asarray(inp["b_im"][0], np.float32), (1, 0, 2)))
    shared["cre"] = dup(np.transpose(np.asarray(inp["c_re"][0], np.float32), (2, 0, 1)))
    shared["cim"] = dup(np.transpose(np.asarray(inp["c_im"][0], np.float32), (2, 0, 1)))
    dsk = np.asarray(inp["d_skip"][0], np.float32).reshape(32, 16)
    ddiag = np.zeros((16, 32, 16), np.float32)
    for c_ in range(16):
        ddiag[c_, :, c_] = dsk[:, c_]
    shared["ddiag"] = ddiag
    shared["dcol"] = f(np.tile(dsk.T, (8, 1)))
    xs = np.asarray(inp["x"], np.float32)
    in_maps = []
    for core in range(8):
        b, half = divmod(core, 2)
        m = dict(shared)
        if half == 0:
            xc = np.concatenate([np.zeros((2048, D), np.float32), xs[b, :2048]], axis=0)
            halo = np.full((128, 128), NEG, np.float32)
        else:
            xc = xs[b]
            halo = np.zeros((128, 128), np.float32)
        m["x"] = np.ascontiguousarray(xc)
        m["halo"] = halo
        in_maps.append(m)
    return in_maps


_NC_CACHE = {}


def kernel(**inputs):
    in_maps = _prep_inputs(inputs)
    if "nc" not in _NC_CACHE:
        _NC_CACHE["nc"] = build()
    nc = _NC_CACHE["nc"]
    res = run_bass_kernel_spmd(nc, in_maps, core_ids=list(range(8)))
    out = np.empty((4, 4096, D), np.float32)
    for core in range(8):
        b, half = divmod(core, 2)
        out[b, half * 2048:(half + 1) * 2048] = np.asarray(res.results[core]["out"], np.float32)
    return out
```

```python
import math
from contextlib import ExitStack

import numpy as np
import concourse.bass as bass
import concourse.mybir as mybir
from concourse.bass_utils import run_bass_kernel_spmd

F32 = mybir.dt.float32
BF16 = mybir.dt.bfloat16
AF = mybir.ActivationFunctionType
ALU = mybir.AluOpType
AX = mybir.AxisListType

NEG = -30000.0
EPS = 1e-6
PI = math.pi
TWO_PI = 2.0 * math.pi
MAGIC = 12582912.0
D = 1024
NTOK = 2048
DBG = False
STOP = 99


class _Stop(Exception):
    pass


class Buf:
    __slots__ = ("w", "r")

    def __init__(self):
        self.w = {}
        self.r = {}


class Tl:
    def __init__(self, t):
        self.t = t
        self.b = Buf()

    def __getitem__(self, k):
        return self.t[k]


class Prog:
    ENGS = ("pe", "act", "dve", "pool", "sp")

    def __init__(self, nc, es):
        self.nc, self.es = nc, es
        self.ops = {e: [] for e in self.ENGS}
        self.sem, self.cnt = {}, {}
        self.waited = {e: {} for e in self.ENGS}
        for e in ("pe", "act", "dve", "pool"):
            self.mksem("E_" + e)

    def mksem(self, name):
        if name not in self.sem:
            self.sem[name] = self.es.enter_context(self.nc.semaphore(name))
            self.cnt[name] = 0
        return name

    def _waits(self, eng, r, w, is_dma):
        own = None if is_dma else "E_" + eng
        need = {}

        def add(sem, val):
            if val > need.get(sem, 0):
                need[sem] = val

        for b in r:
            for sem, val in b.w.items():
                if sem == own and eng == "pe":
                    continue
                add(sem, val)
        for b in w:
            for sem, val in b.w.items():
                if sem != own or eng != "pe":
                    add(sem, val)
            for sem, val in b.r.items():
                if sem != own or eng != "pe":
                    add(sem, val)
        out = []
        wd = self.waited[eng]
        for sem, val in need.items():
            if wd.get(sem, 0) < val:
                out.append((sem, val))
                wd[sem] = val
        return out

    @staticmethod
    def _bufs(xs):
        return [x.b if isinstance(x, Tl) else x for x in xs]

    def op(self, eng, fn, r=(), w=(), signal=True):
        r, w = self._bufs(r), self._bufs(w)
        waits = self._waits(eng, r, w, False)
        name = "E_" + eng
        if signal:
            self.cnt[name] += 1
            val = self.cnt[name]
            inc = (name, 1)
        else:
            val = self.cnt[name] + 1
            inc = None
        self.ops[eng].append((waits, fn, inc))
        for b in r:
            b.r[name] = max(b.r.get(name, 0), val)
        for b in w:
            b.w = {name: val}
            b.r = {}

    def dma(self, q, out, in_, r=(), w=(), sem=None, **kw):
        r, w = self._bufs(r), self._bufs(w)
        self.mksem(sem)
        waits = [(sm, v) for (sm, v) in self._waits(q, r, w, True) if sm != sem]
        self.cnt[sem] += 16
        val = self.cnt[sem]
        self.ops[q].append((waits, lambda e: e.dma_start(out=out, in_=in_, **kw), (sem, 16)))
        for b in r:
            b.r[sem] = max(b.r.get(sem, 0), val)
        for b in w:
            b.w = {sem: val}
            b.r = {}

    def seal(self, sem, bufs):
        for b in self._bufs(bufs):
            b.w = {sem: self.cnt[sem]}

    def barrier(self):
        for eng in self.ENGS:
            waits = []
            for sem, c in self.cnt.items():
                if sem == "E_" + eng:
                    continue
                if c > self.waited[eng].get(sem, 0):
                    waits.append((sem, c))
                    self.waited[eng][sem] = c
            self.ops[eng].append((waits, None, None))

    def emit(self, block):
        def mk(eng):
            def f(e):
                for waits, fn, inc in self.ops[eng]:
                    for sem, val in waits:
                        e.wait_ge(self.sem[sem], val)
                    if fn is not None:
                        ins = fn(e)
                        if inc is not None:
                            ins.then_inc(self.sem[inc[0]], inc[1])

            return f

        block.tensor(mk("pe"))
        block.scalar(mk("act"))
        block.vector(mk("dve"))
        block.gpsimd(mk("pool"))
        block.sync(mk("sp"))

    def mm(self, out, lhsT, rhs, start, stop, r, w, signal=None):
        if signal is None:
            signal = stop
        self.op("pe", lambda e: e.matmul(out, lhsT=lhsT, rhs=rhs, start=start, stop=stop), r, w, signal)

    def tp(self, out, in_, ident, r, w, signal=True):
        self.op("pe", lambda e: e.transpose(out=out, in_=in_, identity=ident), r, w, signal)

    def act(self, out, in_, func, r, w, bias=None, scale=None, accum=None):
        kw = {}
        if bias is not None:
            kw["bias"] = bias
        if scale is not None:
            kw["scale"] = scale
        if accum is not None:
            kw["accum_out"] = accum
        self.op("act", lambda e: e.activation(out=out, in_=in_, func=func, **kw), r, w)

    def tt(self, eng, out, in0, in1, op, r, w):
        self.op(eng, lambda e: e.tensor_tensor(out=out, in0=in0, in1=in1, op=op), r, w)

    def ts(self, eng, out, in0, s1, s2, op0, op1, r, w):
        if s2 is None:
            self.op(eng, lambda e: e.tensor_scalar(out=out, in0=in0, scalar1=s1, scalar2=None, op0=op0), r, w)
        else:
            self.op(eng, lambda e: e.tensor_scalar(out=out, in0=in0, scalar1=s1, scalar2=s2, op0=op0, op1=op1), r, w)

    def stt(self, eng, out, in0, scalar, in1, op0, op1, r, w):
        self.op(eng, lambda e: e.scalar_tensor_tensor(out=out, in0=in0, scalar=scalar, in1=in1, op0=op0, op1=op1), r, w)

    def cp(self, eng, out, in_, r, w):
        if eng == "act":
            self.op(eng, lambda e: e.copy(out=out, in_=in_), r, w)
        else:
            self.op(eng, lambda e: e.tensor_copy(out=out, in_=in_), r, w)

    def rmax(self, out, in_, r, w):
        self.op("dve", lambda e: e.tensor_reduce(out=out, in_=in_, axis=AX.X, op=ALU.max), r, w)

    def recip(self, out, in_, r, w):
        self.op("dve", lambda e: e.reciprocal(out=out, in_=in_), r, w)

    def scan(self, out, d0, d1, r, w):
        self.op("dve", lambda e: e.tensor_tensor_scan(out=out, data0=d0, data1=d1, initial=0.0, op0=ALU.mult, op1=ALU.add), r, w)


class Ring:
    def __init__(self, items):
        self.items, self.i = items, 0

    def next(self):
        x = self.items[self.i % len(self.items)]
        self.i += 1
        return x


def build():
    nc = bass.Bass("TRN2", target_bir_lowering=False)

    def din(name, shape, dt=F32):
        return nc.dram_tensor(name, list(shape), dt, kind="ExternalInput").ap()

    x = din("x", [4096, D])
    w_in = din("w_in", [D, 3328])
    w_glu = din("w_glu", [512, 512])
    w_ab = din("w_ab", [512, D])
    w_sb = din("w_sb", [512, D])
    w_out = din("w_out", [D, D])
    w_ffi = din("w_ffi", [D, 4096])
    w_ffo = din("w_ffo", [4096, D])
    gains = din("gains", [4, 128, D])
    relb = din("relb", [33, 8, 128])
    sinks_d = din("sinks", [128, 8])
    oh_d = din("oh", [33, 384])
    halo_d = din("halo", [128, 128])
    ident_d = din("ident", [128, 128])
    cmask_d = din("cmask", [128, 2, 256])
    identq_d = din("identq", [128, 2, 256])
    ph_d = din("ph", [128, 8])
    tauN_d = din("tauN", [128, 16])
    tauP_d = din("tauP", [128, 17])
    Jv_d = din("Jv", [128, 256])
    lamr_d = din("lamr", [128, 32])
    lami_d = din("lami", [128, 32])
    ldt_d = din("ldt", [128, 32])
    bre_d = din("bre", [128, 32, 16])
    bim_d = din("bim", [128, 32, 16])
    cre_d = din("cre", [128, 32, 16])
    cim_d = din("cim", [128, 32, 16])
    dcol_d = din("dcol", [128, 32])
    ddiag_d = din("ddiag", [16, 32, 16])
    out_d = nc.dram_tensor("out", [NTOK, D], F32, kind="ExternalOutput").ap()
    x1scr = nc.dram_tensor("x1scr", [NTOK, D], F32).ap()
    wffi_b = nc.dram_tensor("wffi_b", [D, 4096], BF16).ap()
    wffo_b = nc.dram_tensor("wffo_b", [4096, D], BF16).ap()
    tbscr_t = nc.dram_tensor("tbscr", [8, 128 * 383], F32)
    tbscr = tbscr_t.ap()
    if DBG:
        dbg_zT = nc.dram_tensor("dbg_zT", [128, 4, NTOK], BF16, kind="ExternalOutput").ap()
        dbg_X = nc.dram_tensor("dbg_X", [128, 2 * 32 * 16 * 16], BF16, kind="ExternalOutput").ap()
        dbg_tb = nc.dram_tensor("dbg_tb", [128, 8 * 256], F32, kind="ExternalOutput").ap()

    with ExitStack() as es:
        P = Prog(nc, es)
        global _LASTP
        _LASTP = P

        def sb(scope, name, shape, dt):
            return Tl(scope.enter_context(nc.sbuf_tensor("sb_" + name, list(shape), dt)))

        def psum(name, shape, dt):
            return Tl(es.enter_context(nc.psum_tensor(name, list(shape), dt)))

        pf = [psum(f"pf{i}", [128, 512], F32) for i in range(6)]
        pb = [psum(f"pb{i}", [128, 1024], BF16) for i in range(2)]
        pfr = Ring(pf)
        pfr5 = Ring(pf[0:5])
        pbr = Ring(pb)

        ident = sb(es, "ident", [128, 128], BF16)
        epsc = sb(es, "epsc", [128, 1], F32)
        halfpi = sb(es, "halfpi", [128, 1], F32)
        scABC = es.enter_context(ExitStack())
        scS = ExitStack()
        ident_f = sb(scABC, "ident_f", [128, 128], F32)
        Tb = sb(scABC, "Tb", [128, 8, 256], F32)
        Tb0 = sb(scABC, "Tb0", [128, 8, 128], F32)
        sinks = sb(scABC, "sinks", [128, 8], F32)
        gpre = sb(scABC, "gpre", [128, D], F32)
        gpost = sb(scABC, "gpost", [128, D], F32)
        zT = sb(scABC, "zT", [128, 4, NTOK], BF16)
        wab = sb(scABC, "wab", [128, 4, D], BF16)
        wsb_ = sb(scABC, "wsb", [128, 4, D], BF16)
        wglu = sb(scABC, "wglu", [128, 4, 512], BF16)
        wout = sb(scABC, "wout", [128, 8, D], BF16)

        cvb_i, cvb_o = Buf(), Buf()

        def body():
            P.dma("sp", ident_f[:], ident_d[:, :], w=[ident_f], sem="S_c0")
            P.dma("sp", sinks[:], sinks_d[:, :], w=[sinks], sem="S_c0")
            P.dma("sp", gpre[:], gains[0, :, :], w=[gpre], sem="S_c0")
            P.dma("sp", gpost[:], gains[1, :, :], w=[gpost], sem="S_c0")
            P.seal("S_c0", [ident_f, sinks, gpre, gpost])
            P.cp("dve", ident[:], ident_f[:], [ident_f], [ident])
            P.op("dve", lambda e: e.memset(epsc[:], EPS), [], [epsc])
            P.op("dve", lambda e: e.memset(halfpi[:], PI / 2), [], [halfpi])

            def wload(dst_tl, dst_ap_fn, src, nk, sem):
                srcv = src.rearrange("(k p) n -> p k n", p=128)
                for k in range(nk):
                    P.dma("pool", dst_ap_fn(k), srcv[:, k, :], w=[dst_tl], sem=sem)
                P.seal(sem, [dst_tl])

            def rstd_from_ss(ss_ap, ss_tl, rstd_tl, tmp_tl):
                P.act(tmp_tl[:], ss_ap, AF.Ln, [ss_tl, epsc], [tmp_tl], bias=epsc[:], scale=1.0 / D)
                P.act(rstd_tl[:], tmp_tl[:], AF.Exp, [tmp_tl], [rstd_tl], scale=-0.5)

            def norm_transpose(xt, g_tl, xs, ss, tmp, rstd, dst_aps, dst_tl, evac_eng, defer=False):
                P.act(xs[:], xt[:], AF.Square, [xt], [xs, ss], accum=ss[:])
                rstd_from_ss(ss[:], ss, rstd, tmp)
                P.stt("dve", xs[:], xt[:], rstd[:], g_tl[:], ALU.mult, ALU.mult, [xt, rstd, g_tl], [xs])

                def stage2():
                    bank = pbr.next()
                    for k in range(8):
                        P.tp(bank[:, k * 128:(k + 1) * 128], xs[:, k * 128:(k + 1) * 128], ident[:], [xs, ident], [bank], signal=(k == 7))
                    P.cp(evac_eng, dst_aps, bank[:, :].rearrange("p (k j) -> p k j", k=8), [bank], [dst_tl])

                if defer:
                    return stage2
                stage2()

            with ExitStack() as scAB:
                def small(name, shape, dt=F32):
                    return sb(scAB, name, shape, dt)

                ph = small("ph", [128, 8]); Jv = small("Jv", [128, 256])
                Y1 = small("Y1", [128, 32, 16]); Y2 = small("Y2", [128, 32, 16])
                dcol = small("dcol", [128, 32])
                ddiag = small("ddiag", [16, 32, 16])
                phir = small("phir", [128, 32]); th15r = small("th15r", [128, 32])
                mag15 = small("mag15", [128, 32]); r16 = small("r16", [128, 32])
                X1 = small("X1", [128, 32, 16]); X2 = small("X2", [128, 32, 16])
                E1n = small("E1n", [128, 32, 16]); E2n = small("E2n", [128, 32, 16])
                F1 = small("F1", [128, 32, 17]); F2 = small("F2", [128, 32, 17])
                X = small("X", [128, 2, 32, 16, 16], BF16)

                scS.__enter__()

                def smallt(name, shape, dt=F32):
                    return sb(scS, name, shape, dt)

                lr = smallt("lr", [128, 32]); li = smallt("li", [128, 32]); ldt = smallt("ldt", [128, 32])
                tauN = smallt("tauN", [128, 16]); tauP = smallt("tauP", [128, 17])
                Br = smallt("Br", [128, 32, 16]); Bi = smallt("Bi", [128, 32, 16])
                dt_ = smallt("dt_", [128, 32]); lrdt = smallt("lrdt", [128, 32]); th = smallt("th", [128, 32])
                thr = smallt("thr", [128, 32]); t32a = smallt("t32a", [128, 32]); t32b = smallt("t32b", [128, 32])
                t32c = smallt("t32c", [128, 32])
                mag1 = smallt("mag1", [128, 32]); cth = smallt("cth", [128, 32]); sth = smallt("sth", [128, 32])
                ar = smallt("ar", [128, 32]); ai = smallt("ai", [128, 32]); nr = smallt("nr", [128, 32])
                den = smallt("den", [128, 32]); fre = smallt("fre", [128, 32]); fim = smallt("fim", [128, 32])
                magP = smallt("magP", [128, 32, 17]); angP = smallt("angP", [128, 32, 17])
                tb17a = smallt("tb17a", [128, 32, 17]); tb17b = smallt("tb17b", [128, 32, 17])
                relb_s = smallt("relb_s", [33, 8, 128])
                oh_s = smallt("oh_s", [33, 384])
                halo_s = smallt("halo_s", [128, 128])
                rrow = [smallt(f"rrow{i}", [128, 383]) for i in range(2)]

                def reduce_pm_pi(dst_ap, dst_tl, src_ap, tmp_ap, tmp_tl, rlist, eng="dve"):
                    P.ts(eng, tmp_ap, src_ap, 1.0 / TWO_PI, MAGIC, ALU.mult, ALU.add, rlist, [tmp_tl])
                    P.ts(eng, tmp_ap, tmp_ap, -MAGIC, None, ALU.add, None, [tmp_tl], [tmp_tl])
                    P.stt(eng, dst_ap, tmp_ap, -TWO_PI, src_ap, ALU.mult, ALU.add, [tmp_tl] + rlist, [dst_tl])
                    P.ts(eng, dst_ap, dst_ap, 3.14159, -3.14159, ALU.min, ALU.max, [dst_tl], [dst_tl])

                def sin_of(dst, ang_ap, ang_tl, phase, tA, tB, sl=None):
                    sl = sl if sl is not None else (slice(None),) * 3
                    rl = [ang_tl] + ([ph] if not isinstance(phase, float) else [])
                    P.ts("dve", tA[sl], ang_ap, phase, None, ALU.add, None, rl, [tA])
                    reduce_pm_pi(tB[sl], tB, tA[sl], dst[sl], dst, [tA])
                    yield
                    P.act(dst[sl], tB[sl], AF.Sin, [tB], [dst])
                    yield

                def background():
                    P.dma("sp", relb_s[:], relb[:, :, :], w=[relb_s], sem="S_c1")
                    P.dma("sp", oh_s[:], oh_d[:, :], w=[oh_s], sem="S_c1")
                    P.dma("sp", halo_s[:], halo_d[:, :], w=[halo_s], sem="S_c1")
                    P.seal("S_c1", [relb_s, oh_s, halo_s])
                    for tl, src in ((lr, lamr_d), (li, lami_d), (ldt, ldt_d), (ph, ph_d), (tauN, tauN_d), (tauP, tauP_d),
                                    (Jv, Jv_d), (dcol, dcol_d)):
                        P.dma("sp", tl[:], src[:, :], w=[tl], sem="S_c2")
                    for tl, src in ((Br, bre_d), (Bi, bim_d), (Y1, cre_d), (Y2, cim_d), (ddiag, ddiag_d)):
                        P.dma("sp", tl[:], src[:, :, :], w=[tl], sem="S_c2")
                    P.seal("S_c2", [lr, li, ldt, ph, tauN, tauP, Jv, dcol, Br, Bi, Y1, Y2, ddiag])
                    yield
                    scrb = [Buf() for _ in range(8)]
                    for h in range(8):
                        bank = pf[5]
                        P.mm(bank[:, 0:383], relb_s[:, h, :], oh_s[:, 0:383], True, True, [relb_s, oh_s], [bank])
                        rr = rrow[h % 2]
                        P.cp("dve", rr[:], bank[:, 0:383], [bank], [rr])
                        dst = bass.AP(tbscr_t, h * 128 * 383, [[383, 128], [1, 383]])
                        P.dma("pool", dst, rr[:], r=[rr], w=[scrb[h]], sem=f"S_tbw{h % 2}")
                        yield
                    for h in range(8):
                        src = bass.AP(tbscr_t, h * 128 * 383 + 127, [[382, 128], [1, 256]])
                        P.dma("pool", Tb[:, h, :], src, r=[scrb[h]], w=[Tb], sem="S_tbr")
                    P.seal("S_tbr", [Tb])
                    yield
                    P.act(dt_[:], ldt[:], AF.Exp, [ldt], [dt_])
                    yield
                    P.tt("dve", lrdt[:], lr[:], dt_[:], ALU.mult, [lr, dt_], [lrdt])
                    P.tt("dve", th[:], li[:], dt_[:], ALU.mult, [li, dt_], [th])
                    reduce_pm_pi(thr[:], thr, th[:], t32a[:], t32a, [th])
                    yield
                    P.act(mag1[:], lrdt[:], AF.Exp, [lrdt], [mag1])
                    P.act(mag15[:], lrdt[:], AF.Exp, [lrdt], [mag15], scale=15.0)
                    P.act(r16[:], lrdt[:], AF.Exp, [lrdt], [r16], scale=16.0)
                    s2 = (slice(None), slice(None))
                    yield from sin_of(cth, thr[:], thr, PI / 2, t32a, t32b, s2)
                    yield
                    yield from sin_of(sth, thr[:], thr, 0.0, t32a, t32b, s2)
                    yield
                    P.tt("dve", ar[:], mag1[:], cth[:], ALU.mult, [mag1, cth], [ar])
                    P.tt("dve", ai[:], mag1[:], sth[:], ALU.mult, [mag1, sth], [ai])
                    P.ts("dve", nr[:], ar[:], -1.0, None, ALU.add, None, [ar], [nr])
                    P.tt("dve", den[:], lr[:], lr[:], ALU.mult, [lr], [den])
                    yield
                    P.tt("dve", t32a[:], li[:], li[:], ALU.mult, [li], [t32a])
                    P.tt("dve", den[:], den[:], t32a[:], ALU.add, [den, t32a], [den])
                    P.recip(den[:], den[:], [den], [den])
                    yield
                    P.tt("dve", t32a[:], nr[:], lr[:], ALU.mult, [nr, lr], [t32a])
                    P.tt("dve", t32b[:], ai[:], li[:], ALU.mult, [ai, li], [t32b])
                    P.tt("dve", t32a[:], t32a[:], t32b[:], ALU.add, [t32a, t32b], [t32a])
                    P.tt("dve", fre[:], t32a[:], den[:], ALU.mult, [t32a, den], [fre])
                    yield
                    P.tt("dve", t32b[:], ai[:], lr[:], ALU.mult, [ai, lr], [t32b])
                    P.tt("dve", t32c[:], nr[:], li[:], ALU.mult, [nr, li], [t32c])
                    P.tt("dve", t32b[:], t32b[:], t32c[:], ALU.subtract, [t32b, t32c], [t32b])
                    P.tt("dve", fim[:], t32b[:], den[:], ALU.mult, [t32b, den], [fim])
                    yield
                    s16 = (slice(None), slice(None), slice(0, 16))
                    fre_b = fre[:].unsqueeze(2).to_broadcast([128, 32, 16])
                    fim_b = fim[:].unsqueeze(2).to_broadcast([128, 32, 16])
                    P.tt("dve", tb17a[s16], Br[:], fre_b, ALU.mult, [Br, fre], [tb17a])
                    P.tt("dve", tb17b[s16], Bi[:], fim_b, ALU.mult, [Bi, fim], [tb17b])
                    P.tt("dve", X1[:], tb17a[s16], tb17b[s16], ALU.subtract, [tb17a, tb17b], [X1])
                    yield
                    P.tt("dve", tb17a[s16], Bi[:], fre_b, ALU.mult, [Bi, fre], [tb17a])
                    P.tt("dve", tb17b[s16], Br[:], fim_b, ALU.mult, [Br, fim], [tb17b])
                    P.tt("dve", X2[:], tb17a[s16], tb17b[s16], ALU.add, [tb17a, tb17b], [X2])
                    yield
                    thr_b16 = thr[:].unsqueeze(2).to_broadcast([128, 32, 16])
                    lrdt_b16 = lrdt[:].unsqueeze(2).to_broadcast([128, 32, 16])
                    tauN_b = tauN[:].unsqueeze(1).to_broadcast([128, 32, 16])
                    P.tt("dve", angP[s16], thr_b16, tauN_b, ALU.mult, [thr, tauN], [angP])
                    P.tt("dve", tb17a[s16], lrdt_b16, tauN_b, ALU.mult, [lrdt, tauN], [tb17a])
                    yield
                    P.act(magP[s16], tb17a[s16], AF.Exp, [tb17a], [magP])
                    yield
                    yield from sin_of(E1n, angP[s16], angP, ph[:, 0:1], tb17a, tb17b, s16)
                    P.tt("dve", E1n[:], E1n[:], magP[s16], ALU.mult, [E1n, magP], [E1n])
                    yield
                    yield from sin_of(E2n, angP[s16], angP, ph[:, 1:2], tb17a, tb17b, s16)
                    P.tt("dve", E2n[:], E2n[:], magP[s16], ALU.mult, [E2n, magP], [E2n])
                    yield
                    thr_b17 = thr[:].unsqueeze(2).to_broadcast([128, 32, 17])
                    lrdt_b17 = lrdt[:].unsqueeze(2).to_broadcast([128, 32, 17])
                    tauP_b = tauP[:].unsqueeze(1).to_broadcast([128, 32, 17])
                    P.tt("dve", angP[:], thr_b17, tauP_b, ALU.mult, [thr, tauP], [angP])
                    P.tt("dve", tb17a[:], lrdt_b17, tauP_b, ALU.mult, [lrdt, tauP], [tb17a])
                    yield
                    P.act(magP[:], tb17a[:], AF.Exp, [tb17a], [magP])
                    yield
                    yield from sin_of(F1, angP[:], angP, ph[:, 2:3], tb17a, tb17b)
                    P.tt("dve", F1[:], F1[:], magP[:], ALU.mult, [F1, magP], [F1])
                    yield
                    yield from sin_of(F2, angP[:], angP, ph[:, 3:4], tb17a, tb17b)
                    P.tt("dve", F2[:], F2[:], magP[:], ALU.mult, [F2, magP], [F2])
                    yield
                    P.ts("dve", t32c[:], thr[:], 16.0, None, ALU.mult, None, [thr], [t32c])
                    reduce_pm_pi(phir[:], phir, t32c[:], t32a[:], t32a, [t32c])
                    yield
                    P.ts("dve", t32c[:], thr[:], 15.0, None, ALU.mult, None, [thr], [t32c])
                    reduce_pm_pi(th15r[:], th15r, t32c[:], t32a[:], t32a, [t32c])
                    yield
                    P.tt("dve", Tb0[:], Tb[:, :, 0:128], halo_s[:].unsqueeze(1).to_broadcast([128, 8, 128]), ALU.add, [Tb, halo_s], [Tb0])
                    if DBG:
                        P.dma("sp", dbg_tb, Tb[:].rearrange("p h j -> p (h j)"), r=[Tb], w=[Buf()], sem="S_dbg")

                bg = background()

                with ExitStack() as scA:
                    wu = sb(scA, "wu", [128, 8, 512], BF16)
                    srcu = w_in.rearrange("(k p) n -> p k n", p=128)
                    for k2 in range(2):
                        P.dma("pool", wu[:, 4 * k2:4 * k2 + 4, :], srcu[:, 4 * k2:4 * k2 + 4, 768:1280], w=[wu], sem="S_wu")
                    P.seal("S_wu", [wu])
                    hT2 = [sb(scA, f"hT{i}", [128, 8, 1024], BF16) for i in range(2)]
                    xts = Ring([sb(scA, f"xtA{i}", [128, D], F32) for i in range(3)])
                    xss = Ring([sb(scA, f"xsA{i}", [128, D], BF16) for i in range(2)])
                    sss = Ring([sb(scA, f"ssA{i}", [128, 1], F32) for i in range(4)])
                    tmps = Ring([sb(scA, f"tmA{i}", [128, 1], F32) for i in range(4)])
                    rsts = Ring([sb(scA, f"rsA{i}", [128, 1], F32) for i in range(4)])

                    def proj_steps(hb):
                        rb, jh = divmod(hb, 2)
                        hTc = hT2[hb % 2]
                        for s in range(16):
                            bank = pfr5.next()
                            for k in range(8):
                                P.mm(bank[0:64, :], hTc[:, k, s:1024:16], wu[:, k, :], k == 0, k == 7, [hTc, wu], [bank])
                            P.cp("act" if s % 2 == 0 else "dve", X[jh * 64:(jh + 1) * 64, rb, :, s, :],
                                 bank[0:64, :].rearrange("p (g c) -> p g c", g=32), [bank], [X])
                            yield

                    prev = None
                    ntile = 0
                    for hb in range(4):
                        pend = None
                        hTc = hT2[hb % 2]
                        for i in range(8):
                            xt = xts.next()
                            tok0 = hb * 1024 + i * 128
                            P.dma("sp", xt[:], x[tok0:tok0 + 128, :], w=[xt], sem=f"S_xA{ntile % 3}")
                            ntile += 1
                            next(bg, None)
                            if ntile == 1:
                                next(bg, None)
                            st2 = norm_transpose(xt, gpre, xss.next(), sss.next(), tmps.next(), rsts.next(),
                                                 hTc[:, :, i * 128:(i + 1) * 128], hTc, "act", defer=True)
                            if pend is not None:
                                pend()
                            pend = st2
                            if prev is not None:
                                next(prev, None)
                                next(prev, None)
                            next(bg, None)
                        pend()
                        if prev is not None:
                            for _ in prev:
                                pass
                        prev = proj_steps(hb)
                        if hb == 1:
                            dep = Buf()
                            dep.w = dict(hTc.b.w)

                            def wload_late(dst_tl, src, nk, sem):
                                srcv = src.rearrange("(k p) n -> p k n", p=128)
                                for k in range(nk):
                                    P.dma("pool", dst_tl[:, k, :], srcv[:, k, :], r=[dep], w=[dst_tl], sem=sem)
                                P.seal(sem, [dst_tl])
                            wload_late(wab, w_ab, 4, "S_wab")
                            wload_late(wsb_, w_sb, 4, "S_wsb")
                            wload_late(wglu, w_glu, 4, "S_wglu")
                            wload_late(wout, w_out, 8, "S_wout")
                    for _ in prev:
                        pass
                    for _ in bg:
                        pass
                    if DBG:
                        P.dma("sp", dbg_X, X[:].rearrange("p a g s c -> p (a g s c)"), r=[X], w=[Buf()], sem="S_dbg")
                    P.barrier()
                scS.close()
                if STOP == 3:
                    return True
                conv_jobs = []
                for k in range(8):
                    conv_jobs.append((wffi_b[k * 128:(k + 1) * 128, :], w_ffi[k * 128:(k + 1) * 128, :], cvb_i, "S_cvi"))
                for k in range(8):
                    conv_jobs.append((wffo_b[k * 512:(k + 1) * 512, :], w_ffo[k * 512:(k + 1) * 512, :], cvb_o, "S_cvo"))

                with ExitStack() as scB:
                    zc2 = sb(scB, "zc2", [128, 16, 128], BF16)
                    tmA = sb(scB, "tmA", [128, 4, 17, 16], BF16)
                    tmB = sb(scB, "tmB", [128, 4, 17, 16], BF16)
                    tmC = sb(scB, "tmC", [128, 4, 16, 16], BF16)
                    tmD = sb(scB, "tmD", [128, 4, 16, 16], BF16)
                    Pst = sb(scB, "Pst", [128, 4, 16, 16], BF16)
                    Qst = sb(scB, "Qst", [128, 4, 17, 16], BF16)
                    Toep = sb(scB, "Toep", [128, 4, 512], BF16)
                    Kt = sb(scB, "Kt", [16, 4, 256], BF16)
                    PT = sb(scB, "PT", [128, 4, 2, 128], BF16)
                    UT = [sb(scB, f"UT{rb}", [128, 4, 2, 128], BF16) for rb in range(2)]
                    psi = sb(scB, "psi", [128, 4, 256], F32)
                    C1p = sb(scB, "C1p", [128, 4, 256], F32)
                    S1p = sb(scB, "S1p", [128, 4, 256], F32)
                    tbA = sb(scB, "tbA", [128, 4, 256], F32)
                    tbB = sb(scB, "tbB", [128, 4, 256], F32)
                    C1 = sb(scB, "C1", [128, 4, 128], F32)
                    S1 = sb(scB, "S1", [128, 4, 128], F32)
                    Zt = sb(scB, "Zt", [128, 4, 256], F32)
                    Zts = sb(scB, "Zts", [128, 4, 256], F32)
                    Sts = sb(scB, "Sts", [128, 4, 256], F32)
                    Sb = sb(scB, "Sb", [128, 4, 128], BF16)
                    St = psi
                    rtab = C1p
                    pz = [pf[1], pf[2]]
                    pzs = [pf[3], pf[4]]
                    Jpos = sb(scB, "Jpos", [128, 256], F32)
                    P.ts("dve", Jpos[:], Jv[:], 1.0, None, ALU.min, None, [Jv], [Jpos])
                    sgn = ph[:, 5:6]
                    P.op("dve", lambda e: e.memset(Toep[:], 0.0), [], [Toep])

                    def sincos(cos_tl, cos_ap, sin_tl, sin_ap, ang_ap, ang_tl, tA_tl, tB_tl, tA_ap, tB_ap):
                        reduce_pm_pi(tB_ap, tB_tl, ang_ap, tA_ap, tA_tl, [ang_tl])
                        P.act(sin_ap, tB_ap, AF.Sin, [tB_tl], [sin_tl])
                        P.act(tA_ap, tB_ap, AF.Abs, [tB_tl], [tA_tl])
                        P.act(cos_ap, tA_ap, AF.Sin, [tA_tl], [cos_tl], bias=halfpi[:], scale=-1.0)

                    for gb in range(8):
                        g0 = gb * 4
                        gs = slice(g0, g0 + 4)
                        P.tt("dve", tmA[:], Y1[:, gs, :].unsqueeze(2).to_broadcast([128, 4, 17, 16]),
                             F1[:, gs, :].unsqueeze(3).to_broadcast([128, 4, 17, 16]), ALU.mult, [Y1, F1], [tmA])
                        P.tt("dve", tmB[:], Y2[:, gs, :].unsqueeze(2).to_broadcast([128, 4, 17, 16]),
                             F2[:, gs, :].unsqueeze(3).to_broadcast([128, 4, 17, 16]), ALU.mult, [Y2, F2], [tmB])
                        P.tt("dve", Qst[:], tmA[:], tmB[:], ALU.add, [tmA, tmB], [Qst])
                        P.tt("dve", tmC[:], X1[:, gs, :].unsqueeze(2).to_broadcast([128, 4, 16, 16]),
                             E1n[:, gs, :].unsqueeze(3).to_broadcast([128, 4, 16, 16]), ALU.mult, [X1, E1n], [tmC])
                        P.tt("dve", tmD[:], X2[:, gs, :].unsqueeze(2).to_broadcast([128, 4, 16, 16]),
                             E2n[:, gs, :].unsqueeze(3).to_broadcast([128, 4, 16, 16]), ALU.mult, [X2, E2n], [tmD])
                        P.tt("dve", Pst[:], tmC[:], tmD[:], ALU.add, [tmC, tmD], [Pst])
                        for gp in range(2):
                            bank = pf[0] if gp == 0 else pf[5]
                            for gl in range(2):
                                g = gp * 2 + gl
                                P.mm(bank[0:16, gl * 256:(gl + 1) * 256], Pst[:, g, 0, :],
                                     Qst[:, g, 0:16, :].rearrange("p a b -> p (a b)"), True, True, [Pst, Qst], [bank], signal=(gl == 1))
                            P.cp("act", Kt[:, gp * 2:gp * 2 + 2, :].rearrange("p g n -> p (g n)"), bank[0:16, :], [bank], [Kt])
                        bank = pb[0]
                        for g in range(4):
                            for q in range(2):
                                P.tp(bank[:, (g * 2 + q) * 128:(g * 2 + q + 1) * 128],
                                     Pst[:, g, 8 * q:8 * q + 8, :].rearrange("p a b -> p (a b)"), ident[:], [Pst, ident], [bank],
                                     signal=(g == 3 and q == 1))
                        P.cp("act", PT[:].rearrange("p g q m -> p (g q m)"), bank[:, :], [bank], [PT])
                        for rb in range(2):
                            bank = pb[1]
                            for g in range(4):
                                for q in range(2):
                                    P.tp(bank[:, (g * 2 + q) * 128:(g * 2 + q + 1) * 128],
                                         X[:, rb, g0 + g, 8 * q:8 * q + 8, :].rearrange("p a b -> p (a b)"), ident[:], [X, ident], [bank],
                                         signal=(g == 3 and q == 1))
                            P.cp("act", UT[rb][:].rearrange("p g q m -> p (g q m)"), bank[:, :], [bank], [UT[rb]])
                        P.tt("dve", psi[:], phir[:, gs].unsqueeze(2).to_broadcast([128, 4, 256]),
                             Jv[:].unsqueeze(1).to_broadcast([128, 4, 256]), ALU.mult, [phir, Jv], [psi])
                        sincos(C1, C1[:], S1, S1[:], psi[:, :, 127:255], psi, Zt, Zts, Zt[:, :, 0:128], Zts[:, :, 0:128])
                        P.tt("dve", Kt[:, :, 0:16], Kt[:, :, 0:16], ddiag[:, gs, :], ALU.add, [Kt, ddiag], [Kt])
                        for q in range(2):
                            for sl in range(8):
                                sft = 8 * q + sl
                                P.dma("sp", Toep[16 * sl:16 * sl + 16, :, q * 256 + 16 * sft:(q + 1) * 256],
                                      Kt[:, :, 0:256 - 16 * sft], r=[Kt], w=[Toep], sem="S_toep")
                        P.seal("S_toep", [Toep])
                        P.tt("dve", psi[:], psi[:], th15r[:, gs].unsqueeze(2).to_broadcast([128, 4, 256]), ALU.subtract, [psi, th15r], [psi])
                        sincos(C1p, C1p[:], S1p, S1p[:], psi[:], psi, tbA, tbB, tbA[:], tbB[:])
                        for rb in range(2):
                            for g in range(4):
                                cs = slice(g * 128, (g + 1) * 128)
                                for q in range(2):
                                    P.mm(pz[rb][:, cs], PT[:, g, q, :], UT[rb][:, g, q, :], q == 0, q == 1, [PT, UT[rb]], [pz[rb]],
                                         signal=(q == 1 and g == 3))
                            for g in range(4):
                                cs = slice(g * 128, (g + 1) * 128)
                                for q in range(2):
                                    P.mm(pzs[rb][0:64, cs], PT[:, g, q, 64:128], UT[rb][:, g, q, :], q == 0, q == 1, [PT, UT[rb]], [pzs[rb]],
                                         signal=False)
                                for q in range(2):
                                    P.mm(pzs[rb][64:128, cs], PT[:, g, q, 0:64], UT[rb][:, g, q, :], q == 0, q == 1, [PT, UT[rb]], [pzs[rb]],
                                         signal=(q == 1 and g == 3))
                        m15b = mag15[:, gs].unsqueeze(2).to_broadcast([128, 4, 128])
                        P.tt("dve", C1[:], C1[:], m15b, ALU.mult, [C1, mag15], [C1])
                        P.tt("dve", S1[:], S1[:], m15b, ALU.mult, [S1, mag15], [S1])
                        for rb in range(2):
                            js = slice(rb * 128, (rb + 1) * 128)
                            zv = pz[rb][:, :].rearrange("p (g j) -> p g j", g=4)
                            zsv = pzs[rb][:, :].rearrange("p (g j) -> p g j", g=4)
                            P.tt("dve", tbA[:, :, js], zv, C1p[:, :, js], ALU.mult, [pz[rb], C1p], [tbA])
                            P.stt("dve", tbB[:, :, js], zsv, sgn, S1p[:, :, js], ALU.mult, ALU.mult, [pzs[rb], S1p, ph], [tbB])
                            P.tt("dve", Zt[:, :, js], tbA[:, :, js], tbB[:, :, js], ALU.add, [tbA, tbB], [Zt])
                            P.tt("dve", tbA[:, :, js], zsv, C1p[:, :, js], ALU.mult, [pzs[rb], C1p], [tbA])
                            P.stt("dve", tbB[:, :, js], zv, sgn, S1p[:, :, js], ALU.mult, ALU.mult, [pz[rb], S1p, ph], [tbB])
                            P.tt("dve", Zts[:, :, js], tbA[:, :, js], tbB[:, :, js], ALU.subtract, [tbA, tbB], [Zts])
                        P.tt("dve", rtab[:], r16[:, gs].unsqueeze(2).to_broadcast([128, 4, 256]),
                             Jpos[:].unsqueeze(1).to_broadcast([128, 4, 256]), ALU.mult, [r16, Jpos], [rtab])
                        P.scan(St[:].rearrange("p g j -> p (g j)"), rtab[:].rearrange("p g j -> p (g j)"),
                               Zt[:].rearrange("p g j -> p (g j)"), [rtab, Zt], [St])
                        P.scan(Sts[:].rearrange("p g j -> p (g j)"), rtab[:].rearrange("p g j -> p (g j)"),
                               Zts[:].rearrange("p g j -> p (g j)"), [rtab, Zts], [Sts])
                        P.tt("dve", tbA[:, :, 0:128], St[:, :, 127:255], C1[:], ALU.mult, [St, C1], [tbA])
                        P.stt("dve", tbB[:, :, 0:128], Sts[:, :, 127:255], sgn, S1[:], ALU.mult, ALU.mult, [Sts, S1, ph], [tbB])
                        P.tt("dve", Sb[:], tbA[:, :, 0:128], tbB[:, :, 0:128], ALU.subtract, [tbA, tbB], [Sb])
                        depc = Buf()
                        depc.w = dict(Sb.b.w)
                        for _ in range(2):
                            o_, i_, cb_, sm_ = conv_jobs.pop(0)
                            P.dma("pool", o_, i_, r=[depc], w=[cb_], sem=sm_)
                        if gb == 7:
                            P.seal("S_cvi", [cvb_i])
                            P.seal("S_cvo", [cvb_o])
                        for g in range(4):
                            bank = pf[0] if g % 2 == 0 else pf[5]
                            cs = slice(0, 256)
                            P.mm(bank[:, cs], UT[1][:, g, 0, :], Toep[:, g, 0:256], True, False, [UT[1], Toep], [bank], signal=False)
                            P.mm(bank[:, cs], UT[1][:, g, 1, :], Toep[:, g, 256:512], False, False, [UT[1], Toep], [bank], signal=False)
                            P.mm(bank[:, cs], Sb[:, g, :], Qst[:, g, 1:17, :].rearrange("p a b -> p (a b)"), False, True, [Sb, Qst], [bank])
                            gl = (g0 + g) % 8
                            P.act(zc2[:, :, gl * 16:(gl + 1) * 16], bank[:, cs].rearrange("p (s c) -> p s c", s=16),
                                  AF.Gelu_apprx_tanh, [bank], [zc2])
                        if gb % 2 == 1:
                            cc = gb // 2
                            for sh in range(2):
                                bank = pb[sh]
                                for s8 in range(8):
                                    s = sh * 8 + s8
                                    P.tp(bank[:, s8 * 128:(s8 + 1) * 128], zc2[:, s, :], ident[:], [zc2, ident], [bank], signal=(s8 == 7))
                                P.cp("act",
                                     zT[:, cc, :].rearrange("p (j s) -> p s j", s=16)[:, sh * 8:(sh + 1) * 8, :],
                                     bank[:, :].rearrange("p (s j) -> p s j", s=8), [bank], [zT])
                    if DBG:
                        P.dma("sp", dbg_zT, zT[:], r=[zT], w=[Buf()], sem="S_dbg")
                    P.barrier()
                    if STOP == 4:
                        return True

            with ExitStack() as scC:
                wq = sb(scC, "wq", [128, 8, 512], BF16)
                wk = sb(scC, "wk", [128, 8, 2, 128], BF16)
                wv = sb(scC, "wv", [128, 8, 128], BF16)
                wg = sb(scC, "wg", [128, 8, 2048], BF16)
                srcw = w_in.rearrange("(k p) n -> p k n", p=128)
                P.dma("pool", wv[:, :, :], srcw[:, :, 640:768], w=[wv], sem="S_wv")
                for kv in range(2):
                    for hh in range(2):
                        P.dma("pool", wk[:, :, kv, hh * 64:(hh + 1) * 64], srcw[:, :, 512 + kv * 64:512 + (kv + 1) * 64], w=[wk], sem="S_wk")
                P.seal("S_wv", [wv])
                P.seal("S_wk", [wk])
                for k2 in range(2):
                    P.dma("pool", wq[:, 4 * k2:4 * k2 + 4, :], srcw[:, 4 * k2:4 * k2 + 4, 0:512], w=[wq], sem="S_wq")
                P.seal("S_wq", [wq])
                depq = Buf()
                depq.w = dict(wq.b.w)
                for k2 in range(4):
                    P.dma("pool", wg[:, 2 * k2:2 * k2 + 2, :], srcw[:, 2 * k2:2 * k2 + 2, 1280:3328], r=[depq], w=[wg], sem="S_wg")
                P.seal("S_wg", [wg])

                xg = [sb(scC, f"xg{i}", [128, D], F32) for i in range(2)]
                xgr = Ring(xg)
                xrr = [sb(scC, f"xr{i}", [128, D], F32) for i in range(3)]
                xr_i = [0]
                xss = Ring([sb(scC, f"xsC{i}", [128, D], BF16) for i in range(2)])
                sss = Ring([sb(scC, f"ssC{i}", [128, 1], F32) for i in range(4)])
                tmps = Ring([sb(scC, f"tmC{i}", [128, 1], F32) for i in range(4)])
                rsts = Ring([sb(scC, f"rsC{i}", [128, 1], F32) for i in range(4)])
                hTg = sb(scC, "hTg", [128, 8, 512], BF16)
                qT = sb(scC, "qT", [128, 4, 512], BF16)
                kT = sb(scC, "kT", [128, 2, 640], BF16)
                vtok = sb(scC, "vtok", [128, 5, 128], BF16)
                attT = sb(scC, "attT", [128, 4, 512], BF16)
                zg = sb(scC, "zg", [128, 4, 512], BF16)
                sgt = Ring([sb(scC, f"sgt{i}", [128, 512], BF16) for i in range(3)])
                mT = sb(scC, "mT", [128, 8, 512], BF16)
                slog2 = [sb(scC, f"slog{i}", [128, 4, 256], F32) for i in range(2)]
                Pm2 = [sb(scC, f"Pm{i}", [128, 4, 256], BF16) for i in range(2)]
                PTs2 = [sb(scC, f"PTs{i}", [128, 4, 2, 128], BF16) for i in range(2)]
                attn2 = [sb(scC, f"attn{i}", [128, 512], BF16) for i in range(2)]
                mx2 = [sb(scC, f"mx{i}", [128, 4], F32) for i in range(2)]
                nmx2 = [sb(scC, f"nmx{i}", [128, 4], F32) for i in range(2)]
                rs2 = [sb(scC, f"rs{i}", [128, 4], F32) for i in range(4)]
                es2 = [sb(scC, f"es{i}", [128, 4], F32) for i in range(4)]
                dn2 = [sb(scC, f"dn{i}", [128, 4], F32) for i in range(4)]
                t1s = Ring([sb(scC, f"t1s{i}", [128, 512], F32) for i in range(2)])
                t2s = Ring([sb(scC, f"t2s{i}", [128, 512], F32) for i in range(1)])
                x1t = Ring([sb(scC, f"x1t{i}", [128, D], F32) for i in range(1)])
                ssp = Ring([sb(scC, f"ssp{i}", [128, 2], F32) for i in range(2)])
                ss1 = Ring([sb(scC, f"ss1{i}", [128, 1], F32) for i in range(2)])

                def proj_kv(src_hT_ap_fn, ntiles, hT_tl, kcol0, vt0):
                    n = ntiles * 128
                    for kv in range(2):
                        bank = pfr.next()
                        for k in range(8):
                            P.mm(bank[:, 0:n], wk[:, k, kv, :], src_hT_ap_fn(k, 0, n), k == 0, k == 7, [wk, hT_tl], [bank])
                        P.cp("act", kT[:, kv, kcol0:kcol0 + n], bank[:, 0:n], [bank], [kT])
                    for t in range(ntiles):
                        bank = pfr.next()
                        for k in range(8):
                            P.mm(bank[:, 0:128], src_hT_ap_fn(k, t * 128, 128), wv[:, k, :], k == 0, k == 7, [hT_tl, wv], [bank])
                        P.cp("dve", vtok[:, vt0 + t, :], bank[:, 0:128], [bank], [vtok])

                xt = xgr.next()
                P.dma("sp", xt[:], x[1920:2048, :], w=[xt], sem="S_xC0")
                norm_transpose(xt, gpre, xss.next(), sss.next(), tmps.next(), rsts.next(),
                               hTg[:, :, 0:128], hTg, "act")
                proj_kv(lambda k, c0, n: hTg[:, k, c0:c0 + n], 1, hTg, 0, 0)

                if STOP == 41:
                    return True
                xc_cnt = [1]

                def norm_tile_C(Gn, t):
                    xt = xgr.next()
                    tok0 = 2048 + Gn * 512 + t * 128
                    P.dma("sp", xt[:], x[tok0:tok0 + 128, :], w=[xt], sem=f"S_xC{xc_cnt[0] % 2}")
                    xc_cnt[0] += 1
                    return norm_transpose(xt, gpre, xss.next(), sss.next(), tmps.next(), rsts.next(),
                                          hTg[:, :, t * 128:(t + 1) * 128], hTg, "act", defer=True)

                for G in range(4):
                    m0 = G * 512
                    if G == 0:
                        pend = None
                        for t in range(4):
                            st2 = norm_tile_C(0, t)
                            if pend is not None:
                                pend()
                            pend = st2
                        pend()
                    if STOP == 42 and G == 0:
                        return True
                    for c in range(4):
                        bank = pfr.next()
                        for k in range(8):
                            P.mm(bank[:, :], wq[:, k, c * 128:(c + 1) * 128], hTg[:, k, :], k == 0, k == 7, [wq, hTg], [bank])
                        P.cp("act" if c % 2 == 0 else "dve", qT[:, c, :], bank[:, :], [bank], [qT])
                    proj_kv(lambda k, c0, n: hTg[:, k, c0:c0 + n], 4, hTg, 128, 1)
                    if STOP == 43 and G == 0:
                        return True
                    def S1(u):
                        t, hh = divmod(u, 2)
                        p = u % 2
                        for hl in range(4):
                            h = hh * 4 + hl
                            bank = pf[2 * p + (hl % 2)]
                            half = hl // 2
                            hs = slice(64 * (hl % 2), 64 * (hl % 2) + 64)
                            P.mm(bank[:, half * 256:half * 256 + 256], qT[hs, h // 2, t * 128:(t + 1) * 128],
                                 kT[hs, hh, t * 128:t * 128 + 256], True, True, [qT, kT], [bank], signal=(hl >= 2))

                    def S2(u):
                        t, hh = divmod(u, 2)
                        p = u % 2
                        first = (G == 0 and t == 0)
                        slog, Pm, mx, nmx, rs, es_ = slog2[p], Pm2[p], mx2[p], nmx2[p], rs2[u % 4], es2[u % 4]
                        for par in range(2):
                            bank = pf[2 * p + par]
                            bv = bank[:, :].rearrange("p (h j) -> p h j", h=2)
                            lsl = slice(par, par + 3, 2)
                            gsl = slice(hh * 4 + par, hh * 4 + par + 3, 2)
                            if first:
                                P.stt("dve", slog[:, lsl, 0:128], bv[:, :, 0:128], 0.125, Tb0[:, gsl, :],
                                      ALU.mult, ALU.add, [bank, Tb0], [slog])
                                P.stt("dve", slog[:, lsl, 128:256], bv[:, :, 128:256], 0.125, Tb[:, gsl, 128:256],
                                      ALU.mult, ALU.add, [bank, Tb], [slog])
                            else:
                                P.stt("dve", slog[:, lsl, :], bv, 0.125, Tb[:, gsl, :], ALU.mult, ALU.add, [bank, Tb], [slog])
                        sk = sinks[:, hh * 4:hh * 4 + 4]
                        P.rmax(mx[:], slog[:], [slog], [mx])
                        P.tt("dve", mx[:], mx[:], sk, ALU.max, [mx, sinks], [mx])
                        P.ts("dve", nmx[:], mx[:], -1.0, None, ALU.mult, None, [mx], [nmx])
                        P.tt("dve", es_[:], sk, mx[:], ALU.subtract, [sinks, mx], [es_])
                        for hl in range(4):
                            P.act(Pm[:, hl, :], slog[:, hl, :], AF.Exp, [slog, nmx], [Pm, rs], bias=nmx[:, hl:hl + 1], accum=rs[:, hl:hl + 1])
                        P.act(es_[:], es_[:], AF.Exp, [es_], [es_])

                    def S3(u):
                        p = u % 2
                        bank = pb[p]
                        for hl in range(4):
                            for kc in range(2):
                                P.tp(bank[:, (hl * 2 + kc) * 128:(hl * 2 + kc + 1) * 128], Pm2[p][:, hl, kc * 128:(kc + 1) * 128], ident[:],
                                     [Pm2[p], ident], [bank], signal=(hl == 3 and kc == 1))
                        P.cp("act", PTs2[p][:].rearrange("p h k q -> p (h k q)"), bank[:, :], [bank], [PTs2[p]])

                    def S4(u):
                        t, hh = divmod(u, 2)
                        p = u % 2
                        po = pf[4 + p]
                        dn, rs, es_ = dn2[u % 4], rs2[u % 4], es2[u % 4]
                        P.tt("dve", dn[:], rs[:], es_[:], ALU.add, [rs, es_], [dn])
                        P.recip(dn[:], dn[:], [dn], [dn])
                        for hl in range(4):
                            for kc in range(2):
                                P.mm(po[:, hl * 64:(hl + 1) * 64], PTs2[p][:, hl, kc, :], vtok[:, t + kc, hh * 64:hh * 64 + 64],
                                     kc == 0, kc == 1, [PTs2[p], vtok], [po], signal=(hl == 3 and kc == 1))
                        at = attn2[t % 2]
                        P.tt("dve", at[:, hh * 256:(hh + 1) * 256].rearrange("p (h d) -> p h d", h=4),
                             po[:, 0:256].rearrange("p (h d) -> p h d", h=4),
                             dn2[u % 4][:].unsqueeze(2).to_broadcast([128, 4, 64]), ALU.mult, [po, dn2[u % 4]], [at])
                        if hh == 1:
                            bank = pb[p]
                            for c in range(4):
                                P.tp(bank[:, c * 128:(c + 1) * 128], at[:, c * 128:(c + 1) * 128], ident[:], [at, ident], [bank], signal=(c == 3))
                            P.cp("act", attT[:, :, t * 128:(t + 1) * 128], bank[:, 0:512].rearrange("p (c j) -> p c j", c=4), [bank], [attT])

                    NU = 8
                    for i in range(NU + 3):
                        if i < NU:
                            S1(i)
                        if 0 <= i - 1 < NU:
                            S2(i - 1)
                        if 0 <= i - 2 < NU:
                            S3(i - 2)
                        if 0 <= i - 3 < NU:
                            S4(i - 3)
                    if STOP == 44 and G == 0:
                        return True
                    P.cp("dve", kT[:, :, 0:128], kT[:, :, 512:640], [kT], [kT])
                    P.cp("dve", vtok[:, 0, :], vtok[:, 4, :], [vtok], [vtok])
                    if STOP == 45 and G == 0:
                        return True
                    for co in range(4):
                        bank = pfr.next()
                        for c in range(4):
                            P.mm(bank[:, :], wglu[:, c, co * 128:(co + 1) * 128], zT[:, c, m0:m0 + 512], c == 0, c == 3, [wglu, zT], [bank])
                        sg = sgt.next()
                        P.act(sg[:], bank[:, :], AF.Sigmoid, [bank], [sg])
                        P.tt("dve", zg[:, co, :], zT[:, co, m0:m0 + 512], sg[:], ALU.mult, [zT, sg], [zg])
                    if STOP == 46 and G == 0:
                        return True
                    for fo in range(8):
                        fs = slice(fo * 128, (fo + 1) * 128)
                        bga = pfr.next()
                        for k in range(8):
                            P.mm(bga[:, :], wg[:, k, fo * 128:(fo + 1) * 128], hTg[:, k, :], k == 0, k == 7, [wg, hTg], [bga])
                        sga = sgt.next()
                        P.act(sga[:], bga[:, :], AF.Sigmoid, [bga], [sga])
                        bgs = pfr.next()
                        for k in range(8):
                            P.mm(bgs[:, :], wg[:, k, 1024 + fo * 128:1024 + (fo + 1) * 128], hTg[:, k, :], k == 0, k == 7, [wg, hTg], [bgs])
                        sgs = sgt.next()
                        P.act(sgs[:], bgs[:, :], AF.Sigmoid, [bgs], [sgs])
                        ba = pfr.next()
                        for c in range(4):
                            P.mm(ba[:, :], wab[:, c, fs], attT[:, c, :], c == 0, c == 3, [wab, attT], [ba])
                        t1 = t1s.next()
                        P.tt("dve", t1[:], ba[:, :], sga[:], ALU.mult, [ba, sga], [t1])
                        bs = pfr.next()
                        for c in range(4):
                            P.mm(bs[:, :], wsb_[:, c, fs], zg[:, c, :], c == 0, c == 3, [wsb_, zg], [bs])
                        t2 = t2s.next()
                        P.tt("dve", t2[:], bs[:, :], sgs[:], ALU.mult, [bs, sgs], [t2])
                        P.tt("dve", mT[:, fo, :], t1[:], t2[:], ALU.add, [t1, t2], [mT])
                    if STOP == 47 and G == 0:
                        return True
                    xrs, xr_sem = {}, {}

                    def load_xr(t):
                        idx = xr_i[0] % 3
                        xr_i[0] += 1
                        xr = xrr[idx]
                        tok0 = 2048 + m0 + t * 128
                        P.dma("sp", xr[:], x[tok0:tok0 + 128, :], w=[xr], sem=f"S_xR{idx}")
                        xrs[t] = xr
                        xr_sem[t] = idx

                    load_xr(0)
                    load_xr(1)
                    pendn = None
                    for t in range(4):
                        if G + 1 < 4:
                            st2n = norm_tile_C(G + 1, t)
                            if pendn is not None:
                                pendn()
                            pendn = st2n
                        b0, b1 = pfr.next(), pfr.next()
                        for half, bank in ((0, b0), (1, b1)):
                            for k in range(8):
                                P.mm(bank[:, :], mT[:, k, t * 128:(t + 1) * 128], wout[:, k, half * 512:(half + 1) * 512], k == 0, k == 7,
                                     [mT, wout], [bank])
                        sp_ = ssp.next()
                        j0, j1 = sgt.next(), sgt.next()
                        P.act(j0[:], b0[:, :], AF.Square, [b0], [j0, sp_], accum=sp_[:, 0:1])
                        P.act(j1[:], b1[:, :], AF.Square, [b1], [j1, sp_], accum=sp_[:, 1:2])
                        s1 = ss1.next()
                        P.tt("dve", s1[:], sp_[:, 0:1], sp_[:, 1:2], ALU.add, [sp_], [s1])
                        tm, rsd = tmps.next(), rsts.next()
                        rstd_from_ss(s1[:], s1, rsd, tm)
                        xo = x1t.next()
                        for half, bank in ((0, b0), (1, b1)):
                            hs_ = slice(half * 512, (half + 1) * 512)
                            P.stt("dve", xo[:, hs_], bank[:, :], rsd[:], gpost[:, hs_], ALU.mult, ALU.mult, [bank, rsd, gpost], [xo])
                        xr = xrs[t]
                        P.tt("dve", xr[:], xo[:], xr[:], ALU.add, [xo, xr], [xr])
                        r0 = m0 + t * 128
                        P.dma("pool", x1scr[r0:r0 + 128, :], xr[:], r=[xr], w=[Buf()], sem=f"S_x1w{xr_sem[t]}")
                        if t + 2 < 4:
                            load_xr(t + 2)
                    if pendn is not None:
                        pendn()
                P.barrier()
                if STOP == 5:
                    return True

            scABC.close()
            with ExitStack() as scD:
                wffi = sb(scD, "wffi", [128, 8, 4096], BF16)
                wffo = sb(scD, "wffo", [128, 32, D], BF16)
                wffi_bv = wffi_b.rearrange("(k p) n -> p k n", p=128)
                wffo_bv = wffo_b.rearrange("(k p) n -> p k n", p=128)
                wffi_q = [Buf() for _ in range(4)]
                for q4 in range(4):
                    P.dma("act", wffi[:, :, q4 * 1024:(q4 + 1) * 1024], wffi_bv[:, :, q4 * 1024:(q4 + 1) * 1024],
                          r=[cvb_i], w=[wffi_q[q4]], sem=f"S_wffi{q4}")
                wffo_p = [Buf() for _ in range(8)]
                depw = Buf()
                depw.w = dict(wffi_q[3].w)
                for k4 in range(8):
                    P.dma("pool", wffo[:, 4 * k4:4 * k4 + 4, :], wffo_bv[:, 4 * k4:4 * k4 + 4, :], r=[cvb_o, depw], w=[wffo_p[k4]],
                          sem=f"S_wffo{k4}")
                g2pre = sb(scD, "g2pre", [128, D], F32)
                g2post = sb(scD, "g2post", [128, D], F32)
                P.dma("sp", g2pre[:], gains[2, :, :], w=[g2pre], sem="S_c3")
                P.dma("sp", g2post[:], gains[3, :, :], w=[g2post], sem="S_c3")
                P.seal("S_c3", [g2pre, g2post])
                x1g = Ring([sb(scD, f"x1g{i}", [128, D], F32) for i in range(4)])
                xss = Ring([sb(scD, f"xsD{i}", [128, D], BF16) for i in range(2)])
                sss = Ring([sb(scD, f"ssD{i}", [128, 1], F32) for i in range(6)])
                tmps = Ring([sb(scD, f"tmD{i}", [128, 1], F32) for i in range(4)])
                rsts = Ring([sb(scD, f"rsD{i}", [128, 1], F32) for i in range(4)])
                h2T = sb(scD, "h2T", [128, 8, 256], BF16)
                ffT = sb(scD, "ffT", [128, 32, 256], BF16)
                rl = Ring([sb(scD, f"rl{i}", [128, 512], BF16) for i in range(2)])
                ot = Ring([sb(scD, f"ot{i}", [128, D], F32) for i in range(2)])
                ssp = Ring([sb(scD, f"sspD{i}", [128, 2], F32) for i in range(2)])
                ss1 = Ring([sb(scD, f"ss1D{i}", [128, 1], F32) for i in range(2)])
                outb = Buf()
                h2Tb = sb(scD, "h2Tb", [128, 8, 256], BF16)
                h2T2 = [h2T, h2Tb]
                xtiles = {}

                def norm_group(Gn, defer):
                    st2s = []
                    tl = []
                    for t in range(2):
                        xt = x1g.next()
                        r0 = Gn * 256 + t * 128
                        P.dma("sp", xt[:], x1scr[r0:r0 + 128, :], w=[xt], sem=f"S_xD{(Gn * 2 + t) % 4}")
                        tl.append(xt)
                        hdst = h2T2[Gn % 2]
                        st2s.append(norm_transpose(xt, g2pre, xss.next(), sss.next(), tmps.next(), rsts.next(),
                                                   hdst[:, :, t * 128:(t + 1) * 128], hdst, "act", defer=True))
                    xtiles[Gn] = tl
                    if defer:
                        return st2s
                    for f in st2s:
                        f()
                    return []

                norm_group(0, False)
                for G in range(8):
                    m0 = G * 256
                    xs_g = xtiles[G]
                    hcur = h2T2[G % 2]
                    for fp in range(16):
                        bank = pfr.next()
                        for j in range(2):
                            fc = fp * 2 + j
                            for k in range(8):
                                P.mm(bank[:, j * 256:(j + 1) * 256], wffi[:, k, fc * 128:(fc + 1) * 128], hcur[:, k, :], k == 0, k == 7,
                                     [wffi_q[fc // 8], hcur], [bank], signal=(j == 1 and k == 7))
                        r_ = rl.next()
                        P.act(r_[:], bank[:, :], AF.Relu, [bank], [r_])
                        P.tt("dve", ffT[:, fp * 2:fp * 2 + 2, :], r_[:].rearrange("p (a n) -> p a n", a=2),
                             r_[:].rearrange("p (a n) -> p a n", a=2), ALU.mult, [r_], [ffT])
                    nxt = norm_group(G + 1, True) if G + 1 < 8 else []
                    for t in range(2):
                        b0, b1 = pfr.next(), pfr.next()
                        for half, bank in ((0, b0), (1, b1)):
                            for fc in range(32):
                                P.mm(bank[:, :], ffT[:, fc, t * 128:(t + 1) * 128], wffo[:, fc, half * 512:(half + 1) * 512], fc == 0, fc == 31,
                                     [ffT, wffo_p[fc // 4]], [bank])
                        if t == 0:
                            for f in nxt:
                                f()
                        sp_ = ssp.next()
                        j0, j1 = rl.next(), rl.next()
                        P.act(j0[:], b0[:, :], AF.Square, [b0], [j0, sp_], accum=sp_[:, 0:1])
                        P.act(j1[:], b1[:, :], AF.Square, [b1], [j1, sp_], accum=sp_[:, 1:2])
                        s1 = ss1.next()
                        P.tt("dve", s1[:], sp_[:, 0:1], sp_[:, 1:2], ALU.add, [sp_], [s1])
                        tm, rsd = tmps.next(), rsts.next()
                        rstd_from_ss(s1[:], s1, rsd, tm)
                        xo = ot.next()
                        for half, bank in ((0, b0), (1, b1)):
                            hs_ = slice(half * 512, (half + 1) * 512)
                            P.stt("dve", xo[:, hs_], bank[:, :], rsd[:], g2post[:, hs_], ALU.mult, ALU.mult, [bank, rsd, g2post], [xo])
                        P.tt("dve", xo[:], xo[:], xs_g[t][:], ALU.add, [xo, xs_g[t]], [xo])
                        r0 = m0 + t * 128
                        P.dma("pool", out_d[r0:r0 + 128, :], xo[:], r=[xo], w=[outb], sem=f"S_ow{(G * 2 + t) % 2}")
                P.barrier()


            return False

        if body():
            scS.close()
            scABC.close()
            P.barrier()

        block = es.enter_context(nc.Block())
        P.emit(block)
    return nc


def _bucket(d):
    d = np.asarray(d)
    df = np.maximum(d, 1).astype(np.float32)
    large = 16 + (np.log(df / np.float32(16)) / np.float32(math.log(128 / 16)) * np.float32(16)).astype(np.int32)
    large = np.minimum(large, 31)
    return np.where(d < 16, d, large)


def _constants():
    c = {}
    c["ident"] = np.eye(128, dtype=np.float32)
    oh = np.zeros((33, 384), np.float32)
    for m in range(383):
        d = 255 - m
        if 0 <= d < 128:
            oh[int(_bucket(d)), m] = 1.0
        else:
            oh[32, m] = 1.0
    c["oh"] = oh
    cm = np.zeros((128, 2, 256), np.float32)
    iq = np.zeros((128, 2, 256), np.float32)
    for p in range(128):
        sl, cc = divmod(p, 16)
        for q in range(2):
            s = 8 * q + sl
            for s2 in range(16):
                if s2 >= s:
                    cm[p, q, s2 * 16:(s2 + 1) * 16] = 1.0
            iq[p, q, s * 16 + cc] = 1.0
    c["cmask"], c["identq"] = cm, iq
    ph = np.zeros((128, 8), np.float32)
    top, bot = slice(0, 64), slice(64, 128)
    ph[top, 0], ph[bot, 0] = PI / 2, 0.0
    ph[top, 1], ph[bot, 1] = PI, PI / 2
    ph[top, 2], ph[bot, 2] = PI / 2, PI
    ph[top, 3], ph[bot, 3] = PI, 3 * PI / 2
    ph[top, 4], ph[bot, 4] = 0.0, PI
    ph[top, 5], ph[bot, 5] = 1.0, -1.0
    c["ph"] = ph
    c["tauN"] = np.tile(-np.arange(16, dtype=np.float32), (128, 1))
    c["tauP"] = np.tile(np.arange(17, dtype=np.float32), (128, 1))
    c["Jv"] = np.tile(np.arange(256, dtype=np.float32), (128, 1))
    return c


def _prep_inputs(inp):
    f = lambda a: np.ascontiguousarray(np.asarray(a, dtype=np.float32))
    shared = dict(_constants())
    shared["w_in"] = f(inp["w_in"][0])
    shared["w_glu"] = f(inp["w_glu"][0])
    shared["w_ab"] = f(inp["w_attn_branch"][0])
    shared["w_sb"] = f(inp["w_ssm_branch"][0])
    shared["w_out"] = f(inp["w_out"][0])
    shared["w_ffi"] = f(inp["w_ff_in"][0])
    shared["w_ffo"] = f(inp["w_ff_out"][0])
    gains = np.stack([inp["norm_mix_pre"][0], inp["norm_mix_post"][0], inp["norm_mlp_pre"][0], inp["norm_mlp_post"][0]])
    shared["gains"] = f(np.broadcast_to(np.asarray(gains, np.float32)[:, None, :], (4, 128, D)))
    relb = np.empty((33, 8, 128), np.float32)
    relb[:32] = np.asarray(inp["rel_bias"], np.float32)[:, :, None]
    relb[32] = NEG
    shared["relb"] = relb
    shared["sinks"] = f(np.broadcast_to(np.asarray(inp["sinks"][0], np.float32)[None, :], (128, 8)))
    dup = lambda a: f(np.concatenate([a, a], axis=0))
    shared["lamr"] = dup(np.asarray(inp["lam_re"][0], np.float32).T)
    shared["lami"] = dup(np.asarray(inp["lam_im"][0], np.float32).T)
    shared["ldt"] = f(np.broadcast_to(np.asarray(inp["log_dt"][0], np.float32)[None, :], (128, 32)))
    shared["bre"] = dup(np.transpose(np.asarray(inp["b_re"][0], np.float32), (1, 0, 2)))
    shared["bim"] = dup(np.transpose(np.asarray(inp["b_im"][0], np.float32), (1, 0, 2)))
    shared["cre"] = dup(np.transpose(np.asarray(inp["c_re"][0], np.float32), (2, 0, 1)))
    shared["cim"] = dup(np.transpose(np.asarray(inp["c_im"][0], np.float32), (2, 0, 1)))
    dsk = np.asarray(inp["d_skip"][0], np.float32).reshape(32, 16)
    ddiag = np.zeros((16, 32, 16), np.float32)
    for c_ in range(16):
        ddiag[c_, :, c_] = dsk[:, c_]
    shared["ddiag"] = ddiag
    shared["dcol"] = f(np.tile(dsk.T, (8, 1)))
    xs = np.asarray(inp["x"], np.float32)
    in_maps = []
    for core in range(8):
        b, half = divmod(core, 2)
        m = dict(shared)
        if half == 0:
            xc = np.concatenate([np.zeros((2048, D), np.float32), xs[b, :2048]], axis=0)
            halo = np.full((128, 128), NEG, np.float32)
        else:
            xc = xs[b]
            halo = np.zeros((128, 128), np.float32)
        m["x"] = np.ascontiguousarray(xc)
        m["halo"] = halo
        in_maps.append(m)
    return in_maps


_NC_CACHE = {}


def kernel(**inputs):
    in_maps = _prep_inputs(inputs)
    if "nc" not in _NC_CACHE:
        _NC_CACHE["nc"] = build()
    nc = _NC_CACHE["nc"]
    res = run_bass_kernel_spmd(nc, in_maps, core_ids=list(range(8)))
    out = np.empty((4, 4096, D), np.float32)
    for core in range(8):
        b, half = divmod(core, 2)
        out[b, half * 2048:(half + 1) * 2048] = np.asarray(res.results[core]["out"], np.float32)
    return out
```

```python
import math
from contextlib import ExitStack

import numpy as np
import concourse.bass as bass
import concourse.mybir as mybir
from concourse.bass_utils import run_bass_kernel_spmd

F32 = mybir.dt.float32
BF16 = mybir.dt.bfloat16
AF = mybir.ActivationFunctionType
ALU = mybir.AluOpType
AX = mybir.AxisListType

NEG = -30000.0
EPS = 1e-6
PI = math.pi
TWO_PI = 2.0 * math.pi
MAGIC = 12582912.0
D = 1024
NTOK = 2048
DBG = False
STOP = 99


class _Stop(Exception):
    pass


class Buf:
    __slots__ = ("w", "r")

    def __init__(self):
        self.w = {}
        self.r = {}


class Tl:
    def __init__(self, t):
        self.t = t
        self.b = Buf()

    def __getitem__(self, k):
        return self.t[k]


class Prog:
    ENGS = ("pe", "act", "dve", "pool", "sp")

    def __init__(self, nc, es):
        self.nc, self.es = nc, es
        self.ops = {e: [] for e in self.ENGS}
        self.sem, self.cnt = {}, {}
        self.waited = {e: {} for e in self.ENGS}
        for e in ("pe", "act", "dve", "pool"):
            self.mksem("E_" + e)

    def mksem(self, name):
        if name not in self.sem:
            self.sem[name] = self.es.enter_context(self.nc.semaphore(name))
            self.cnt[name] = 0
        return name

    def _waits(self, eng, r, w, is_dma):
        own = None if is_dma else "E_" + eng
        need = {}

        def add(sem, val):
            if val > need.get(sem, 0):
                need[sem] = val

        for b in r:
            for sem, val in b.w.items():
                if sem == own and eng == "pe":
                    continue
                add(sem, val)
        for b in w:
            for sem, val in b.w.items():
                if sem != own or eng != "pe":
                    add(sem, val)
            for sem, val in b.r.items():
                if sem != own or eng != "pe":
                    add(sem, val)
        out = []
        wd = self.waited[eng]
        for sem, val in need.items():
            if wd.get(sem, 0) < val:
                out.append((sem, val))
                wd[sem] = val
        return out

    @staticmethod
    def _bufs(xs):
        return [x.b if isinstance(x, Tl) else x for x in xs]

    def op(self, eng, fn, r=(), w=(), signal=True):
        r, w = self._bufs(r), self._bufs(w)
        waits = self._waits(eng, r, w, False)
        name = "E_" + eng
        if signal:
            self.cnt[name] += 1
            val = self.cnt[name]
            inc = (name, 1)
        else:
            val = self.cnt[name] + 1
            inc = None
        self.ops[eng].append((waits, fn, inc))
        for b in r:
            b.r[name] = max(b.r.get(name, 0), val)
        for b in w:
            b.w = {name: val}
            b.r = {}

    def dma(self, q, out, in_, r=(), w=(), sem=None, **kw):
        r, w = self._bufs(r), self._bufs(w)
        self.mksem(sem)
        waits = [(sm, v) for (sm, v) in self._waits(q, r, w, True) if sm != sem]
        self.cnt[sem] += 16
        val = self.cnt[sem]
        self.ops[q].append((waits, lambda e: e.dma_start(out=out, in_=in_, **kw), (sem, 16)))
        for b in r:
            b.r[sem] = max(b.r.get(sem, 0), val)
        for b in w:
            b.w = {sem: val}
            b.r = {}

    def seal(self, sem, bufs):
        for b in self._bufs(bufs):
            b.w = {sem: self.cnt[sem]}

    def barrier(self):
        for eng in self.ENGS:
            waits = []
            for sem, c in self.cnt.items():
                if sem == "E_" + eng:
                    continue
                if c > self.waited[eng].get(sem, 0):
                    waits.append((sem, c))
                    self.waited[eng][sem] = c
            self.ops[eng].append((waits, None, None))

    def emit(self, block):
        def mk(eng):
            def f(e):
                for waits, fn, inc in self.ops[eng]:
                    for sem, val in waits:
                        e.wait_ge(self.sem[sem], val)
                    if fn is not None:
                        ins = fn(e)
                        if inc is not None:
                            ins.then_inc(self.sem[inc[0]], inc[1])

            return f

        block.tensor(mk("pe"))
        block.scalar(mk("act"))
        block.vector(mk("dve"))
        block.gpsimd(mk("pool"))
        block.sync(mk("sp"))

    def mm(self, out, lhsT, rhs, start, stop, r, w, signal=None):
        if signal is None:
            signal = stop
        self.op("pe", lambda e: e.matmul(out, lhsT=lhsT, rhs=rhs, start=start, stop=stop), r, w, signal)

    def tp(self, out, in_, ident, r, w, signal=True):
        self.op("pe", lambda e: e.transpose(out=out, in_=in_, identity=ident), r, w, signal)

    def act(self, out, in_, func, r, w, bias=None, scale=None, accum=None):
        kw = {}
        if bias is not None:
            kw["bias"] = bias
        if scale is not None:
            kw["scale"] = scale
        if accum is not None:
            kw["accum_out"] = accum
        self.op("act", lambda e: e.activation(out=out, in_=in_, func=func, **kw), r, w)

    def tt(self, eng, out, in0, in1, op, r, w):
        self.op(eng, lambda e: e.tensor_tensor(out=out, in0=in0, in1=in1, op=op), r, w)

    def ts(self, eng, out, in0, s1, s2, op0, op1, r, w):
        if s2 is None:
            self.op(eng, lambda e: e.tensor_scalar(out=out, in0=in0, scalar1=s1, scalar2=None, op0=op0), r, w)
        else:
            self.op(eng, lambda e: e.tensor_scalar(out=out, in0=in0, scalar1=s1, scalar2=s2, op0=op0, op1=op1), r, w)

    def stt(self, eng, out, in0, scalar, in1, op0, op1, r, w):
        self.op(eng, lambda e: e.scalar_tensor_tensor(out=out, in0=in0, scalar=scalar, in1=in1, op0=op0, op1=op1), r, w)

    def cp(self, eng, out, in_, r, w):
        if eng == "act":
            self.op(eng, lambda e: e.copy(out=out, in_=in_), r, w)
        else:
            self.op(eng, lambda e: e.tensor_copy(out=out, in_=in_), r, w)

    def rmax(self, out, in_, r, w):
        self.op("dve", lambda e: e.tensor_reduce(out=out, in_=in_, axis=AX.X, op=ALU.max), r, w)

    def recip(self, out, in_, r, w):
        self.op("dve", lambda e: e.reciprocal(out=out, in_=in_), r, w)

    def scan(self, out, d0, d1, r, w):
        self.op("dve", lambda e: e.tensor_tensor_scan(out=out, data0=d0, data1=d1, initial=0.0, op0=ALU.mult, op1=ALU.add), r, w)


class Ring:
    def __init__(self, items):
        self.items, self.i = items, 0

    def next(self):
        x = self.items[self.i % len(self.items)]
        self.i += 1
        return x


def build():
    nc = bass.Bass("TRN2", target_bir_lowering=False)

    def din(name, shape, dt=F32):
        return nc.dram_tensor(name, list(shape), dt, kind="ExternalInput").ap()

    x = din("x", [4096, D])
    w_in = din("w_in", [D, 3328])
    w_glu = din("w_glu", [512, 512])
    w_ab = din("w_ab", [512, D])
    w_sb = din("w_sb", [512, D])
    w_out = din("w_out", [D, D])
    w_ffi = din("w_ffi", [D, 4096])
    w_ffo = din("w_ffo", [4096, D])
    gains = din("gains", [4, 128, D])
    relb = din("relb", [33, 8, 128])
    sinks_d = din("sinks", [128, 8])
    oh_d = din("oh", [33, 384])
    halo_d = din("halo", [128, 128])
    ident_d = din("ident", [128, 128])
    cmask_d = din("cmask", [128, 2, 256])
    identq_d = din("identq", [128, 2, 256])
    ph_d = din("ph", [128, 8])
    tauN_d = din("tauN", [128, 16])
    tauP_d = din("tauP", [128, 17])
    Jv_d = din("Jv", [128, 256])
    lamr_d = din("lamr", [128, 32])
    lami_d = din("lami", [128, 32])
    ldt_d = din("ldt", [128, 32])
    bre_d = din("bre", [128, 32, 16])
    bim_d = din("bim", [128, 32, 16])
    cre_d = din("cre", [128, 32, 16])
    cim_d = din("cim", [128, 32, 16])
    dcol_d = din("dcol", [128, 32])
    ddiag_d = din("ddiag", [16, 32, 16])
    out_d = nc.dram_tensor("out", [NTOK, D], F32, kind="ExternalOutput").ap()
    x1scr = nc.dram_tensor("x1scr", [NTOK, D], F32).ap()
    wffi_b = nc.dram_tensor("wffi_b", [D, 4096], BF16).ap()
    wffo_b = nc.dram_tensor("wffo_b", [4096, D], BF16).ap()
    tbscr_t = nc.dram_tensor("tbscr", [8, 128 * 383], F32)
    tbscr = tbscr_t.ap()
    if DBG:
        dbg_zT = nc.dram_tensor("dbg_zT", [128, 4, NTOK], BF16, kind="ExternalOutput").ap()
        dbg_X = nc.dram_tensor("dbg_X", [128, 2 * 32 * 16 * 16], BF16, kind="ExternalOutput").ap()
        dbg_tb = nc.dram_tensor("dbg_tb", [128, 8 * 256], F32, kind="ExternalOutput").ap()

    with ExitStack() as es:
        P = Prog(nc, es)
        global _LASTP
        _LASTP = P

        def sb(scope, name, shape, dt):
            return Tl(scope.enter_context(nc.sbuf_tensor("sb_" + name, list(shape), dt)))

        def psum(name, shape, dt):
            return Tl(es.enter_context(nc.psum_tensor(name, list(shape), dt)))

        pf = [psum(f"pf{i}", [128, 512], F32) for i in range(6)]
        pb = [psum(f"pb{i}", [128, 1024], BF16) for i in range(2)]
        pfr = Ring(pf)
        pfr5 = Ring(pf[0:5])
        pbr = Ring(pb)

        ident = sb(es, "ident", [128, 128], BF16)
        epsc = sb(es, "epsc", [128, 1], F32)
        halfpi = sb(es, "halfpi", [128, 1], F32)
        scABC = es.enter_context(ExitStack())
        scS = ExitStack()
        ident_f = sb(scABC, "ident_f", [128, 128], F32)
        Tb = sb(scABC, "Tb", [128, 8, 256], F32)
        Tb0 = sb(scABC, "Tb0", [128, 8, 128], F32)
        sinks = sb(scABC, "sinks", [128, 8], F32)
        gpre = sb(scABC, "gpre", [128, D], F32)
        gpost = sb(scABC, "gpost", [128, D], F32)
        zT = sb(scABC, "zT", [128, 4, NTOK], BF16)
        wab = sb(scABC, "wab", [128, 4, D], BF16)
        wsb_ = sb(scABC, "wsb", [128, 4, D], BF16)
        wglu = sb(scABC, "wglu", [128, 4, 512], BF16)
        wout = sb(scABC, "wout", [128, 8, D], BF16)

        cvb_i, cvb_o = Buf(), Buf()

        def body():
            P.dma("sp", ident_f[:], ident_d[:, :], w=[ident_f], sem="S_c0")
            P.dma("sp", sinks[:], sinks_d[:, :], w=[sinks], sem="S_c0")
            P.dma("sp", gpre[:], gains[0, :, :], w=[gpre], sem="S_c0")
            P.dma("sp", gpost[:], gains[1, :, :], w=[gpost], sem="S_c0")
            P.seal("S_c0", [ident_f, sinks, gpre, gpost])
            P.cp("dve", ident[:], ident_f[:], [ident_f], [ident])
            P.op("dve", lambda e: e.memset(epsc[:], EPS), [], [epsc])
            P.op("dve", lambda e: e.memset(halfpi[:], PI / 2), [], [halfpi])

            def wload(dst_tl, dst_ap_fn, src, nk, sem):
                srcv = src.rearrange("(k p) n -> p k n", p=128)
                for k in range(nk):
                    P.dma("pool", dst_ap_fn(k), srcv[:, k, :], w=[dst_tl], sem=sem)
                P.seal(sem, [dst_tl])

            def rstd_from_ss(ss_ap, ss_tl, rstd_tl, tmp_tl):
                P.act(tmp_tl[:], ss_ap, AF.Ln, [ss_tl, epsc], [tmp_tl], bias=epsc[:], scale=1.0 / D)
                P.act(rstd_tl[:], tmp_tl[:], AF.Exp, [tmp_tl], [rstd_tl], scale=-0.5)

            def norm_transpose(xt, g_tl, xs, ss, tmp, rstd, dst_aps, dst_tl, evac_eng, defer=False):
                P.act(xs[:], xt[:], AF.Square, [xt], [xs, ss], accum=ss[:])
                rstd_from_ss(ss[:], ss, rstd, tmp)
                P.stt("dve", xs[:], xt[:], rstd[:], g_tl[:], ALU.mult, ALU.mult, [xt, rstd, g_tl], [xs])

                def stage2():
                    bank = pbr.next()
                    for k in range(8):
                        P.tp(bank[:, k * 128:(k + 1) * 128], xs[:, k * 128:(k + 1) * 128], ident[:], [xs, ident], [bank], signal=(k == 7))
                    P.cp(evac_eng, dst_aps, bank[:, :].rearrange("p (k j) -> p k j", k=8), [bank], [dst_tl])

                if defer:
                    return stage2
                stage2()

            with ExitStack() as scAB:
                def small(name, shape, dt=F32):
                    return sb(scAB, name, shape, dt)

                ph = small("ph", [128, 8]); Jv = small("Jv", [128, 256])
                Y1 = small("Y1", [128, 32, 16]); Y2 = small("Y2", [128, 32, 16])
                dcol = small("dcol", [128, 32])
                ddiag = small("ddiag", [16, 32, 16])
                phir = small("phir", [128, 32]); th15r = small("th15r", [128, 32])
                mag15 = small("mag15", [128, 32]); r16 = small("r16", [128, 32])
                X1 = small("X1", [128, 32, 16]); X2 = small("X2", [128, 32, 16])
                E1n = small("E1n", [128, 32, 16]); E2n = small("E2n", [128, 32, 16])
                F1 = small("F1", [128, 32, 17]); F2 = small("F2", [128, 32, 17])
                X = small("X", [128, 2, 32, 16, 16], BF16)

                scS.__enter__()

                def smallt(name, shape, dt=F32):
                    return sb(scS, name, shape, dt)

                lr = smallt("lr", [128, 32]); li = smallt("li", [128, 32]); ldt = smallt("ldt", [128, 32])
                tauN = smallt("tauN", [128, 16]); tauP = smallt("tauP", [128, 17])
                Br = smallt("Br", [128, 32, 16]); Bi = smallt("Bi", [128, 32, 16])
                dt_ = smallt("dt_", [128, 32]); lrdt = smallt("lrdt", [128, 32]); th = smallt("th", [128, 32])
                thr = smallt("thr", [128, 32]); t32a = smallt("t32a", [128, 32]); t32b = smallt("t32b", [128, 32])
                t32c = smallt("t32c", [128, 32])
                mag1 = smallt("mag1", [128, 32]); cth = smallt("cth", [128, 32]); sth = smallt("sth", [128, 32])
                ar = smallt("ar", [128, 32]); ai = smallt("ai", [128, 32]); nr = smallt("nr", [128, 32])
                den = smallt("den", [128, 32]); fre = smallt("fre", [128, 32]); fim = smallt("fim", [128, 32])
                magP = smallt("magP", [128, 32, 17]); angP = smallt("angP", [128, 32, 17])
                tb17a = smallt("tb17a", [128, 32, 17]); tb17b = smallt("tb17b", [128, 32, 17])
                relb_s = smallt("relb_s", [33, 8, 128])
                oh_s = smallt("oh_s", [33, 384])
                halo_s = smallt("halo_s", [128, 128])
                rrow = [smallt(f"rrow{i}", [128, 383]) for i in range(2)]

                def reduce_pm_pi(dst_ap, dst_tl, src_ap, tmp_ap, tmp_tl, rlist, eng="dve"):
                    P.ts(eng, tmp_ap, src_ap, 1.0 / TWO_PI, MAGIC, ALU.mult, ALU.add, rlist, [tmp_tl])
                    P.ts(eng, tmp_ap, tmp_ap, -MAGIC, None, ALU.add, None, [tmp_tl], [tmp_tl])
                    P.stt(eng, dst_ap, tmp_ap, -TWO_PI, src_ap, ALU.mult, ALU.add, [tmp_tl] + rlist, [dst_tl])
                    P.ts(eng, dst_ap, dst_ap, 3.14159, -3.14159, ALU.min, ALU.max, [dst_tl], [dst_tl])

                def sin_of(dst, ang_ap, ang_tl, phase, tA, tB, sl=None):
                    sl = sl if sl is not None else (slice(None),) * 3
                    rl = [ang_tl] + ([ph] if not isinstance(phase, float) else [])
                    P.ts("dve", tA[sl], ang_ap, phase, None, ALU.add, None, rl, [tA])
                    reduce_pm_pi(tB[sl], tB, tA[sl], dst[sl], dst, [tA])
                    yield
                    P.act(dst[sl], tB[sl], AF.Sin, [tB], [dst])
                    yield

                def background():
                    P.dma("sp", relb_s[:], relb[:, :, :], w=[relb_s], sem="S_c1")
                    P.dma("sp", oh_s[:], oh_d[:, :], w=[oh_s], sem="S_c1")
                    P.dma("sp", halo_s[:], halo_d[:, :], w=[halo_s], sem="S_c1")
                    P.seal("S_c1", [relb_s, oh_s, halo_s])
                    for tl, src in ((lr, lamr_d), (li, lami_d), (ldt, ldt_d), (ph, ph_d), (tauN, tauN_d), (tauP, tauP_d),
                                    (Jv, Jv_d), (dcol, dcol_d)):
                        P.dma("sp", tl[:], src[:, :], w=[tl], sem="S_c2")
                    for tl, src in ((Br, bre_d), (Bi, bim_d), (Y1, cre_d), (Y2, cim_d), (ddiag, ddiag_d)):
                        P.dma("sp", tl[:], src[:, :, :], w=[tl], sem="S_c2")
                    P.seal("S_c2", [lr, li, ldt, ph, tauN, tauP, Jv, dcol, Br, Bi, Y1, Y2, ddiag])
                    yield
                    scrb = [Buf() for _ in range(8)]
                    for h in range(8):
                        bank = pf[5]
                        P.mm(bank[:, 0:383], relb_s[:, h, :], oh_s[:, 0:383], True, True, [relb_s, oh_s], [bank])
                        rr = rrow[h % 2]
                        P.cp("dve", rr[:], bank[:, 0:383], [bank], [rr])
                        dst = bass.AP(tbscr_t, h * 128 * 383, [[383, 128], [1, 383]])
                        P.dma("pool", dst, rr[:], r=[rr], w=[scrb[h]], sem=f"S_tbw{h % 2}")
                        yield
                    for h in range(8):
                        src = bass.AP(tbscr_t, h * 128 * 383 + 127, [[382, 128], [1, 256]])
                        P.dma("pool", Tb[:, h, :], src, r=[scrb[h]], w=[Tb], sem="S_tbr")
                    P.seal("S_tbr", [Tb])
                    yield
                    P.act(dt_[:], ldt[:], AF.Exp, [ldt], [dt_])
                    yield
                    P.tt("dve", lrdt[:], lr[:], dt_[:], ALU.mult, [lr, dt_], [lrdt])
                    P.tt("dve", th[:], li[:], dt_[:], ALU.mult, [li, dt_], [th])
                    reduce_pm_pi(thr[:], thr, th[:], t32a[:], t32a, [th])
                    yield
                    P.act(mag1[:], lrdt[:], AF.Exp, [lrdt], [mag1])
                    P.act(mag15[:], lrdt[:], AF.Exp, [lrdt], [mag15], scale=15.0)
                    P.act(r16[:], lrdt[:], AF.Exp, [lrdt], [r16], scale=16.0)
                    s2 = (slice(None), slice(None))
                    yield from sin_of(cth, thr[:], thr, PI / 2, t32a, t32b, s2)
                    yield
                    yield from sin_of(sth, thr[:], thr, 0.0, t32a, t32b, s2)
                    yield
                    P.tt("dve", ar[:], mag1[:], cth[:], ALU.mult, [mag1, cth], [ar])
                    P.tt("dve", ai[:], mag1[:], sth[:], ALU.mult, [mag1, sth], [ai])
                    P.ts("dve", nr[:], ar[:], -1.0, None, ALU.add, None, [ar], [nr])
                    P.tt("dve", den[:], lr[:], lr[:], ALU.mult, [lr], [den])
                    yield
                    P.tt("dve", t32a[:], li[:], li[:], ALU.mult, [li], [t32a])
                    P.tt("dve", den[:], den[:], t32a[:], ALU.add, [den, t32a], [den])
                    P.recip(den[:], den[:], [den], [den])
                    yield
                    P.tt("dve", t32a[:], nr[:], lr[:], ALU.mult, [nr, lr], [t32a])
                    P.tt("dve", t32b[:], ai[:], li[:], ALU.mult, [ai, li], [t32b])
                    P.tt("dve", t32a[:], t32a[:], t32b[:], ALU.add, [t32a, t32b], [t32a])
                    P.tt("dve", fre[:], t32a[:], den[:], ALU.mult, [t32a, den], [fre])
                    yield
                    P.tt("dve", t32b[:], ai[:], lr[:], ALU.mult, [ai, lr], [t32b])
                    P.tt("dve", t32c[:], nr[:], li[:], ALU.mult, [nr, li], [t32c])
                    P.tt("dve", t32b[:], t32b[:], t32c[:], ALU.subtract, [t32b, t32c], [t32b])
                    P.tt("dve", fim[:], t32b[:], den[:], ALU.mult, [t32b, den], [fim])
                    yield
                    s16 = (slice(None), slice(None), slice(0, 16))
                    fre_b = fre[:].unsqueeze(2).to_broadcast([128, 32, 16])
                    fim_b = fim[:].unsqueeze(2).to_broadcast([128, 32, 16])
                    P.tt("dve", tb17a[s16], Br[:], fre_b, ALU.mult, [Br, fre], [tb17a])
                    P.tt("dve", tb17b[s16], Bi[:], fim_b, ALU.mult, [Bi, fim], [tb17b])
                    P.tt("dve", X1[:], tb17a[s16], tb17b[s16], ALU.subtract, [tb17a, tb17b], [X1])
                    yield
                    P.tt("dve", tb17a[s16], Bi[:], fre_b, ALU.mult, [Bi, fre], [tb17a])
                    P.tt("dve", tb17b[s16], Br[:], fim_b, ALU.mult, [Br, fim], [tb17b])
                    P.tt("dve", X2[:], tb17a[s16], tb17b[s16], ALU.add, [tb17a, tb17b], [X2])
                    yield
                    thr_b16 = thr[:].unsqueeze(2).to_broadcast([128, 32, 16])
                    lrdt_b16 = lrdt[:].unsqueeze(2).to_broadcast([128, 32, 16])
                    tauN_b = tauN[:].unsqueeze(1).to_broadcast([128, 32, 16])
                    P.tt("dve", angP[s16], thr_b16, tauN_b, ALU.mult, [thr, tauN], [angP])
                    P.tt("dve", tb17a[s16], lrdt_b16, tauN_b, ALU.mult, [lrdt, tauN], [tb17a])
                    yield
                    P.act(magP[s16], tb17a[s16], AF.Exp, [tb17a], [magP])
                    yield
                    yield from sin_of(E1n, angP[s16], angP, ph[:, 0:1], tb17a, tb17b, s16)
                    P.tt("dve", E1n[:], E1n[:], magP[s16], ALU.mult, [E1n, magP], [E1n])
                    yield
                    yield from sin_of(E2n, angP[s16], angP, ph[:, 1:2], tb17a, tb17b, s16)
                    P.tt("dve", E2n[:], E2n[:], magP[s16], ALU.mult, [E2n, magP], [E2n])
                    yield
                    thr_b17 = thr[:].unsqueeze(2).to_broadcast([128, 32, 17])
                    lrdt_b17 = lrdt[:].unsqueeze(2).to_broadcast([128, 32, 17])
                    tauP_b = tauP[:].unsqueeze(1).to_broadcast([128, 32, 17])
                    P.tt("dve", angP[:], thr_b17, tauP_b, ALU.mult, [thr, tauP], [angP])
                    P.tt("dve", tb17a[:], lrdt_b17, tauP_b, ALU.mult, [lrdt, tauP], [tb17a])
                    yield
                    P.act(magP[:], tb17a[:], AF.Exp, [tb17a], [magP])
                    yield
                    yield from sin_of(F1, angP[:], angP, ph[:, 2:3], tb17a, tb17b)
                    P.tt("dve", F1[:], F1[:], magP[:], ALU.mult, [F1, magP], [F1])
                    yield
                    yield from sin_of(F2, angP[:], angP, ph[:, 3:4], tb17a, tb17b)
                    P.tt("dve", F2[:], F2[:], magP[:], ALU.mult, [F2, magP], [F2])
                    yield
                    P.ts("dve", t32c[:], thr[:], 16.0, None, ALU.mult, None, [thr], [t32c])
                    reduce_pm_pi(phir[:], phir, t32c[:], t32a[:], t32a, [t32c])
                    yield
                    P.ts("dve", t32c[:], thr[:], 15.0, None, ALU.mult, None, [thr], [t32c])
                    reduce_pm_pi(th15r[:], th15r, t32c[:], t32a[:], t32a, [t32c])
                    yield
                    P.tt("dve", Tb0[:], Tb[:, :, 0:128], halo_s[:].unsqueeze(1).to_broadcast([128, 8, 128]), ALU.add, [Tb, halo_s], [Tb0])
                    if DBG:
                        P.dma("sp", dbg_tb, Tb[:].rearrange("p h j -> p (h j)"), r=[Tb], w=[Buf()], sem="S_dbg")

                bg = background()

                with ExitStack() as scA:
                    wu = sb(scA, "wu", [128, 8, 512], BF16)
                    srcu = w_in.rearrange("(k p) n -> p k n", p=128)
                    for k2 in range(2):
                        P.dma("pool", wu[:, 4 * k2:4 * k2 + 4, :], srcu[:, 4 * k2:4 * k2 + 4, 768:1280], w=[wu], sem="S_wu")
                    P.seal("S_wu", [wu])
                    hT2 = [sb(scA, f"hT{i}", [128, 8, 1024], BF16) for i in range(2)]
                    xts = Ring([sb(scA, f"xtA{i}", [128, D], F32) for i in range(3)])
                    xss = Ring([sb(scA, f"xsA{i}", [128, D], BF16) for i in range(2)])
                    sss = Ring([sb(scA, f"ssA{i}", [128, 1], F32) for i in range(4)])
                    tmps = Ring([sb(scA, f"tmA{i}", [128, 1], F32) for i in range(4)])
                    rsts = Ring([sb(scA, f"rsA{i}", [128, 1], F32) for i in range(4)])

                    def proj_steps(hb):
                        rb, jh = divmod(hb, 2)
                        hTc = hT2[hb % 2]
                        for s in range(16):
                            bank = pfr5.next()
                            for k in range(8):
                                P.mm(bank[0:64, :], hTc[:, k, s:1024:16], wu[:, k, :], k == 0, k == 7, [hTc, wu], [bank])
                            P.cp("act" if s % 2 == 0 else "dve", X[jh * 64:(jh + 1) * 64, rb, :, s, :],
                                 bank[0:64, :].rearrange("p (g c) -> p g c", g=32), [bank], [X])
                            yield

                    prev = None
                    ntile = 0
                    for hb in range(4):
                        pend = None
                        hTc = hT2[hb % 2]
                        for i in range(8):
                            xt = xts.next()
                            tok0 = hb * 1024 + i * 128
                            P.dma("sp", xt[:], x[tok0:tok0 + 128, :], w=[xt], sem=f"S_xA{ntile % 3}")
                            ntile += 1
                            next(bg, None)
                            if ntile == 1:
                                next(bg, None)
                            st2 = norm_transpose(xt, gpre, xss.next(), sss.next(), tmps.next(), rsts.next(),
                                                 hTc[:, :, i * 128:(i + 1) * 128], hTc, "act", defer=True)
                            if pend is not None:
                                pend()
                            pend = st2
                            if prev is not None:
                                next(prev, None)
                                next(prev, None)
                            next(bg, None)
                        pend()
                        if prev is not None:
                            for _ in prev:
                                pass
                        prev = proj_steps(hb)
                        if hb == 1:
                            dep = Buf()
                            dep.w = dict(hTc.b.w)

                            def wload_late(dst_tl, src, nk, sem):
                                srcv = src.rearrange("(k p) n -> p k n", p=128)
                                for k in range(nk):
                                    P.dma("pool", dst_tl[:, k, :], srcv[:, k, :], r=[dep], w=[dst_tl], sem=sem)
                                P.seal(sem, [dst_tl])
                            wload_late(wab, w_ab, 4, "S_wab")
                            wload_late(wsb_, w_sb, 4, "S_wsb")
                            wload_late(wglu, w_glu, 4, "S_wglu")
                            wload_late(wout, w_out, 8, "S_wout")
                    for _ in prev:
                        pass
                    for _ in bg:
                        pass
                    if DBG:
                        P.dma("sp", dbg_X, X[:].rearrange("p a g s c -> p (a g s c)"), r=[X], w=[Buf()], sem="S_dbg")
                    P.barrier()
                scS.close()
                if STOP == 3:
                    return True
                conv_jobs = []
                for k in range(8):
                    conv_jobs.append((wffi_b[k * 128:(k + 1) * 128, :], w_ffi[k * 128:(k + 1) * 128, :], cvb_i, "S_cvi"))
                for k in range(8):
                    conv_jobs.append((wffo_b[k * 512:(k + 1) * 512, :], w_ffo[k * 512:(k + 1) * 512, :], cvb_o, "S_cvo"))

                with ExitStack() as scB:
                    zc2 = sb(scB, "zc2", [128, 16, 128], BF16)
                    tmA = sb(scB, "tmA", [128, 4, 17, 16], BF16)
                    tmB = sb(scB, "tmB", [128, 4, 17, 16], BF16)
                    tmC = sb(scB, "tmC", [128, 4, 16, 16], BF16)
                    tmD = sb(scB, "tmD", [128, 4, 16, 16], BF16)
                    Pst = sb(scB, "Pst", [128, 4, 16, 16], BF16)
                    Qst2 = [sb(scB, f"Qst{i}", [128, 4, 17, 16], BF16) for i in range(2)]
                    Toep = sb(scB, "Toep", [128, 4, 512], BF16)
                    Kt = sb(scB, "Kt", [16, 4, 256], BF16)
                    PT = sb(scB, "PT", [128, 4, 2, 128], BF16)
                    UT = [sb(scB, f"UT{rb}", [128, 4, 2, 128], BF16) for rb in range(2)]
                    psi = sb(scB, "psi", [128, 4, 256], F32)
                    C1p = sb(scB, "C1p", [128, 4, 256], F32)
                    S1p = sb(scB, "S1p", [128, 4, 256], F32)
                    tbA = sb(scB, "tbA", [128, 4, 256], F32)
                    tbB = sb(scB, "tbB", [128, 4, 256], F32)
                    C1 = sb(scB, "C1", [128, 4, 128], F32)
                    S1 = sb(scB, "S1", [128, 4, 128], F32)
                    Zt = sb(scB, "Zt", [128, 4, 256], F32)
                    Zts = sb(scB, "Zts", [128, 4, 256], F32)
                    Sts = sb(scB, "Sts", [128, 4, 256], F32)
                    Sb = sb(scB, "Sb", [128, 4, 128], BF16)
                    St = psi
                    rtab = C1p
                    pz = [pf[1], pf[2]]
                    pzs = [pf[3], pf[4]]
                    Jpos = sb(scB, "Jpos", [128, 256], F32)
                    P.ts("dve", Jpos[:], Jv[:], 1.0, None, ALU.min, None, [Jv], [Jpos])
                    sgn = ph[:, 5:6]
                    P.op("dve", lambda e: e.memset(Toep[:], 0.0), [], [Toep])

                    def sincos(cos_tl, cos_ap, sin_tl, sin_ap, ang_ap, ang_tl, tA_tl, tB_tl, tA_ap, tB_ap):
                        reduce_pm_pi(tB_ap, tB_tl, ang_ap, tA_ap, tA_tl, [ang_tl])
                        P.act(sin_ap, tB_ap, AF.Sin, [tB_tl], [sin_tl])
                        P.act(tA_ap, tB_ap, AF.Abs, [tB_tl], [tA_tl])
                        P.act(cos_ap, tA_ap, AF.Sin, [tA_tl], [cos_tl], bias=halfpi[:], scale=-1.0)

                    for gb in range(8):
                        g0 = gb * 4
                        gs = slice(g0, g0 + 4)
                        Qst = Qst2[gb % 2]
                        P.tt("dve", tmA[:], Y1[:, gs, :].unsqueeze(2).to_broadcast([128, 4, 17, 16]),
                             F1[:, gs, :].unsqueeze(3).to_broadcast([128, 4, 17, 16]), ALU.mult, [Y1, F1], [tmA])
                        P.tt("dve", tmB[:], Y2[:, gs, :].unsqueeze(2).to_broadcast([128, 4, 17, 16]),
                             F2[:, gs, :].unsqueeze(3).to_broadcast([128, 4, 17, 16]), ALU.mult, [Y2, F2], [tmB])
                        P.tt("dve", Qst[:], tmA[:], tmB[:], ALU.add, [tmA, tmB], [Qst])
                        P.tt("dve", tmC[:], X1[:, gs, :].unsqueeze(2).to_broadcast([128, 4, 16, 16]),
                             E1n[:, gs, :].unsqueeze(3).to_broadcast([128, 4, 16, 16]), ALU.mult, [X1, E1n], [tmC])
                        P.tt("dve", tmD[:], X2[:, gs, :].unsqueeze(2).to_broadcast([128, 4, 16, 16]),
                             E2n[:, gs, :].unsqueeze(3).to_broadcast([128, 4, 16, 16]), ALU.mult, [X2, E2n], [tmD])
                        P.tt("dve", Pst[:], tmC[:], tmD[:], ALU.add, [tmC, tmD], [Pst])
                        for gp in range(2):
                            bank = pf[0] if gp == 0 else pf[5]
                            for gl in range(2):
                                g = gp * 2 + gl
                                P.mm(bank[0:16, gl * 256:(gl + 1) * 256], Pst[:, g, 0, :],
                                     Qst[:, g, 0:16, :].rearrange("p a b -> p (a b)"), True, True, [Pst, Qst], [bank], signal=(gl == 1))
                            P.cp("act", Kt[:, gp * 2:gp * 2 + 2, :].rearrange("p g n -> p (g n)"), bank[0:16, :], [bank], [Kt])
                        bank = pb[0]
                        for g in range(4):
                            for q in range(2):
                                P.tp(bank[:, (g * 2 + q) * 128:(g * 2 + q + 1) * 128],
                                     Pst[:, g, 8 * q:8 * q + 8, :].rearrange("p a b -> p (a b)"), ident[:], [Pst, ident], [bank],
                                     signal=(g == 3 and q == 1))
                        P.cp("act", PT[:].rearrange("p g q m -> p (g q m)"), bank[:, :], [bank], [PT])
                        for rb in range(2):
                            bank = pb[1]
                            for g in range(4):
                                for q in range(2):
                                    P.tp(bank[:, (g * 2 + q) * 128:(g * 2 + q + 1) * 128],
                                         X[:, rb, g0 + g, 8 * q:8 * q + 8, :].rearrange("p a b -> p (a b)"), ident[:], [X, ident], [bank],
                                         signal=(g == 3 and q == 1))
                            P.cp("act", UT[rb][:].rearrange("p g q m -> p (g q m)"), bank[:, :], [bank], [UT[rb]])
                        P.tt("dve", psi[:], phir[:, gs].unsqueeze(2).to_broadcast([128, 4, 256]),
                             Jv[:].unsqueeze(1).to_broadcast([128, 4, 256]), ALU.mult, [phir, Jv], [psi])
                        sincos(C1, C1[:], S1, S1[:], psi[:, :, 127:255], psi, Zt, Zts, Zt[:, :, 0:128], Zts[:, :, 0:128])
                        P.tt("dve", Kt[:, :, 0:16], Kt[:, :, 0:16], ddiag[:, gs, :], ALU.add, [Kt, ddiag], [Kt])
                        for q in range(2):
                            for sl in range(8):
                                sft = 8 * q + sl
                                P.dma("sp", Toep[16 * sl:16 * sl + 16, :, q * 256 + 16 * sft:(q + 1) * 256],
                                      Kt[:, :, 0:256 - 16 * sft], r=[Kt], w=[Toep], sem="S_toep")
                        P.seal("S_toep", [Toep])
                        P.tt("dve", psi[:], psi[:], th15r[:, gs].unsqueeze(2).to_broadcast([128, 4, 256]), ALU.subtract, [psi, th15r], [psi])
                        sincos(C1p, C1p[:], S1p, S1p[:], psi[:], psi, tbA, tbB, tbA[:], tbB[:])
                        for rb in range(2):
                            for g in range(4):
                                cs = slice(g * 128, (g + 1) * 128)
                                for q in range(2):
                                    P.mm(pz[rb][:, cs], PT[:, g, q, :], UT[rb][:, g, q, :], q == 0, q == 1, [PT, UT[rb]], [pz[rb]],
                                         signal=(q == 1 and g == 3))
                            for g in range(4):
                                cs = slice(g * 128, (g + 1) * 128)
                                for q in range(2):
                                    P.mm(pzs[rb][0:64, cs], PT[:, g, q, 64:128], UT[rb][:, g, q, :], q == 0, q == 1, [PT, UT[rb]], [pzs[rb]],
                                         signal=False)
                                for q in range(2):
                                    P.mm(pzs[rb][64:128, cs], PT[:, g, q, 0:64], UT[rb][:, g, q, :], q == 0, q == 1, [PT, UT[rb]], [pzs[rb]],
                                         signal=(q == 1 and g == 3))
                        m15b = mag15[:, gs].unsqueeze(2).to_broadcast([128, 4, 128])
                        P.tt("dve", C1[:], C1[:], m15b, ALU.mult, [C1, mag15], [C1])
                        P.tt("dve", S1[:], S1[:], m15b, ALU.mult, [S1, mag15], [S1])
                        for rb in range(2):
                            js = slice(rb * 128, (rb + 1) * 128)
                            zv = pz[rb][:, :].rearrange("p (g j) -> p g j", g=4)
                            zsv = pzs[rb][:, :].rearrange("p (g j) -> p g j", g=4)
                            P.tt("dve", tbA[:, :, js], zv, C1p[:, :, js], ALU.mult, [pz[rb], C1p], [tbA])
                            P.stt("dve", tbB[:, :, js], zsv, sgn, S1p[:, :, js], ALU.mult, ALU.mult, [pzs[rb], S1p, ph], [tbB])
                            P.tt("dve", Zt[:, :, js], tbA[:, :, js], tbB[:, :, js], ALU.add, [tbA, tbB], [Zt])
                            P.tt("dve", tbA[:, :, js], zsv, C1p[:, :, js], ALU.mult, [pzs[rb], C1p], [tbA])
                            P.stt("dve", tbB[:, :, js], zv, sgn, S1p[:, :, js], ALU.mult, ALU.mult, [pz[rb], S1p, ph], [tbB])
                            P.tt("dve", Zts[:, :, js], tbA[:, :, js], tbB[:, :, js], ALU.subtract, [tbA, tbB], [Zts])
                        P.tt("dve", rtab[:], r16[:, gs].unsqueeze(2).to_broadcast([128, 4, 256]),
                             Jpos[:].unsqueeze(1).to_broadcast([128, 4, 256]), ALU.mult, [r16, Jpos], [rtab])
                        P.scan(St[:].rearrange("p g j -> p (g j)"), rtab[:].rearrange("p g j -> p (g j)"),
                               Zt[:].rearrange("p g j -> p (g j)"), [rtab, Zt], [St])
                        P.scan(Sts[:].rearrange("p g j -> p (g j)"), rtab[:].rearrange("p g j -> p (g j)"),
                               Zts[:].rearrange("p g j -> p (g j)"), [rtab, Zts], [Sts])
                        P.tt("dve", tbA[:, :, 0:128], St[:, :, 127:255], C1[:], ALU.mult, [St, C1], [tbA])
                        P.stt("dve", tbB[:, :, 0:128], Sts[:, :, 127:255], sgn, S1[:], ALU.mult, ALU.mult, [Sts, S1, ph], [tbB])
                        P.tt("dve", Sb[:], tbA[:, :, 0:128], tbB[:, :, 0:128], ALU.subtract, [tbA, tbB], [Sb])
                        depc = Buf()
                        depc.w = dict(Sb.b.w)
                        for _ in range(2):
                            o_, i_, cb_, sm_ = conv_jobs.pop(0)
                            P.dma("pool", o_, i_, r=[depc], w=[cb_], sem=sm_)
                        if gb == 7:
                            P.seal("S_cvi", [cvb_i])
                            P.seal("S_cvo", [cvb_o])
                        for g in range(4):
                            bank = pf[0] if g % 2 == 0 else pf[5]
                            cs = slice(0, 256)
                            P.mm(bank[:, cs], UT[1][:, g, 0, :], Toep[:, g, 0:256], True, False, [UT[1], Toep], [bank], signal=False)
                            P.mm(bank[:, cs], UT[1][:, g, 1, :], Toep[:, g, 256:512], False, False, [UT[1], Toep], [bank], signal=False)
                            P.mm(bank[:, cs], Sb[:, g, :], Qst[:, g, 1:17, :].rearrange("p a b -> p (a b)"), False, True, [Sb, Qst], [bank])
                            gl = (g0 + g) % 8
                            P.act(zc2[:, :, gl * 16:(gl + 1) * 16], bank[:, cs].rearrange("p (s c) -> p s c", s=16),
                                  AF.Gelu_apprx_tanh, [bank], [zc2])
                        if gb % 2 == 1:
                            cc = gb // 2
                            for sh in range(2):
                                bank = pb[sh]
                                for s8 in range(8):
                                    s = sh * 8 + s8
                                    P.tp(bank[:, s8 * 128:(s8 + 1) * 128], zc2[:, s, :], ident[:], [zc2, ident], [bank], signal=(s8 == 7))
                                P.cp("act",
                                     zT[:, cc, :].rearrange("p (j s) -> p s j", s=16)[:, sh * 8:(sh + 1) * 8, :],
                                     bank[:, :].rearrange("p (s j) -> p s j", s=8), [bank], [zT])
                    if DBG:
                        P.dma("sp", dbg_zT, zT[:], r=[zT], w=[Buf()], sem="S_dbg")
                    P.barrier()
                    if STOP == 4:
                        return True

            with ExitStack() as scC:
                wq = sb(scC, "wq", [128, 8, 512], BF16)
                wk = sb(scC, "wk", [128, 8, 2, 128], BF16)
                wv = sb(scC, "wv", [128, 8, 128], BF16)
                wg = sb(scC, "wg", [128, 8, 2048], BF16)
                srcw = w_in.rearrange("(k p) n -> p k n", p=128)
                P.dma("pool", wv[:, :, :], srcw[:, :, 640:768], w=[wv], sem="S_wv")
                for kv in range(2):
                    for hh in range(2):
                        P.dma("pool", wk[:, :, kv, hh * 64:(hh + 1) * 64], srcw[:, :, 512 + kv * 64:512 + (kv + 1) * 64], w=[wk], sem="S_wk")
                P.seal("S_wv", [wv])
                P.seal("S_wk", [wk])
                for k2 in range(2):
                    P.dma("pool", wq[:, 4 * k2:4 * k2 + 4, :], srcw[:, 4 * k2:4 * k2 + 4, 0:512], w=[wq], sem="S_wq")
                P.seal("S_wq", [wq])
                depq = Buf()
                depq.w = dict(wq.b.w)
                for k2 in range(4):
                    P.dma("pool", wg[:, 2 * k2:2 * k2 + 2, :], srcw[:, 2 * k2:2 * k2 + 2, 1280:3328], r=[depq], w=[wg], sem="S_wg")
                P.seal("S_wg", [wg])

                xg = [sb(scC, f"xg{i}", [128, D], F32) for i in range(2)]
                xgr = Ring(xg)
                xrr = [sb(scC, f"xr{i}", [128, D], F32) for i in range(3)]
                xr_i = [0]
                xss = Ring([sb(scC, f"xsC{i}", [128, D], BF16) for i in range(2)])
                sss = Ring([sb(scC, f"ssC{i}", [128, 1], F32) for i in range(4)])
                tmps = Ring([sb(scC, f"tmC{i}", [128, 1], F32) for i in range(4)])
                rsts = Ring([sb(scC, f"rsC{i}", [128, 1], F32) for i in range(4)])
                hTg = sb(scC, "hTg", [128, 8, 512], BF16)
                qT = sb(scC, "qT", [128, 4, 512], BF16)
                kT = sb(scC, "kT", [128, 2, 640], BF16)
                vtok = sb(scC, "vtok", [128, 5, 128], BF16)
                attT = sb(scC, "attT", [128, 4, 512], BF16)
                zg = sb(scC, "zg", [128, 4, 512], BF16)
                sgt = Ring([sb(scC, f"sgt{i}", [128, 512], BF16) for i in range(3)])
                mT = sb(scC, "mT", [128, 8, 512], BF16)
                slog2 = [sb(scC, f"slog{i}", [128, 4, 256], F32) for i in range(2)]
                Pm2 = [sb(scC, f"Pm{i}", [128, 4, 256], BF16) for i in range(2)]
                PTs2 = [sb(scC, f"PTs{i}", [128, 4, 2, 128], BF16) for i in range(2)]
                attn2 = [sb(scC, f"attn{i}", [128, 512], BF16) for i in range(2)]
                mx2 = [sb(scC, f"mx{i}", [128, 4], F32) for i in range(2)]
                nmx2 = [sb(scC, f"nmx{i}", [128, 4], F32) for i in range(2)]
                rs2 = [sb(scC, f"rs{i}", [128, 4], F32) for i in range(4)]
                es2 = [sb(scC, f"es{i}", [128, 4], F32) for i in range(4)]
                dn2 = [sb(scC, f"dn{i}", [128, 4], F32) for i in range(4)]
                t1s = Ring([sb(scC, f"t1s{i}", [128, 512], F32) for i in range(2)])
                t2s = Ring([sb(scC, f"t2s{i}", [128, 512], F32) for i in range(1)])
                x1t = Ring([sb(scC, f"x1t{i}", [128, D], F32) for i in range(1)])
                ssp = Ring([sb(scC, f"ssp{i}", [128, 2], F32) for i in range(2)])
                ss1 = Ring([sb(scC, f"ss1{i}", [128, 1], F32) for i in range(2)])

                def proj_kv(src_hT_ap_fn, ntiles, hT_tl, kcol0, vt0):
                    n = ntiles * 128
                    for kv in range(2):
                        bank = pfr.next()
                        for k in range(8):
                            P.mm(bank[:, 0:n], wk[:, k, kv, :], src_hT_ap_fn(k, 0, n), k == 0, k == 7, [wk, hT_tl], [bank])
                        P.cp("act", kT[:, kv, kcol0:kcol0 + n], bank[:, 0:n], [bank], [kT])
                    for t in range(ntiles):
                        bank = pfr.next()
                        for k in range(8):
                            P.mm(bank[:, 0:128], src_hT_ap_fn(k, t * 128, 128), wv[:, k, :], k == 0, k == 7, [hT_tl, wv], [bank])
                        P.cp("dve", vtok[:, vt0 + t, :], bank[:, 0:128], [bank], [vtok])

                xt = xgr.next()
                P.dma("sp", xt[:], x[1920:2048, :], w=[xt], sem="S_xC0")
                norm_transpose(xt, gpre, xss.next(), sss.next(), tmps.next(), rsts.next(),
                               hTg[:, :, 0:128], hTg, "act")
                proj_kv(lambda k, c0, n: hTg[:, k, c0:c0 + n], 1, hTg, 0, 0)

                if STOP == 41:
                    return True
                xc_cnt = [1]

                def norm_tile_C(Gn, t):
                    xt = xgr.next()
                    tok0 = 2048 + Gn * 512 + t * 128
                    P.dma("sp", xt[:], x[tok0:tok0 + 128, :], w=[xt], sem=f"S_xC{xc_cnt[0] % 2}")
                    xc_cnt[0] += 1
                    return norm_transpose(xt, gpre, xss.next(), sss.next(), tmps.next(), rsts.next(),
                                          hTg[:, :, t * 128:(t + 1) * 128], hTg, "act", defer=True)

                for G in range(4):
                    m0 = G * 512
                    if G == 0:
                        pend = None
                        for t in range(4):
                            st2 = norm_tile_C(0, t)
                            if pend is not None:
                                pend()
                            pend = st2
                        pend()
                    if STOP == 42 and G == 0:
                        return True
                    for c in range(4):
                        bank = pfr.next()
                        for k in range(8):
                            P.mm(bank[:, :], wq[:, k, c * 128:(c + 1) * 128], hTg[:, k, :], k == 0, k == 7, [wq, hTg], [bank])
                        P.cp("act" if c % 2 == 0 else "dve", qT[:, c, :], bank[:, :], [bank], [qT])
                    proj_kv(lambda k, c0, n: hTg[:, k, c0:c0 + n], 4, hTg, 128, 1)
                    if STOP == 43 and G == 0:
                        return True
                    def S1(u):
                        t, hh = divmod(u, 2)
                        p = u % 2
                        for hl in range(4):
                            h = hh * 4 + hl
                            bank = pf[2 * p + (hl % 2)]
                            half = hl // 2
                            hs = slice(64 * (hl % 2), 64 * (hl % 2) + 64)
                            P.mm(bank[:, half * 256:half * 256 + 256], qT[hs, h // 2, t * 128:(t + 1) * 128],
                                 kT[hs, hh, t * 128:t * 128 + 256], True, True, [qT, kT], [bank], signal=(hl >= 2))

                    def S2(u):
                        t, hh = divmod(u, 2)
                        p = u % 2
                        first = (G == 0 and t == 0)
                        slog, Pm, mx, nmx, rs, es_ = slog2[p], Pm2[p], mx2[p], nmx2[p], rs2[u % 4], es2[u % 4]
                        for par in range(2):
                            bank = pf[2 * p + par]
                            bv = bank[:, :].rearrange("p (h j) -> p h j", h=2)
                            lsl = slice(par, par + 3, 2)
                            gsl = slice(hh * 4 + par, hh * 4 + par + 3, 2)
                            if first:
                                P.stt("dve", slog[:, lsl, 0:128], bv[:, :, 0:128], 0.125, Tb0[:, gsl, :],
                                      ALU.mult, ALU.add, [bank, Tb0], [slog])
                                P.stt("dve", slog[:, lsl, 128:256], bv[:, :, 128:256], 0.125, Tb[:, gsl, 128:256],
                                      ALU.mult, ALU.add, [bank, Tb], [slog])
                            else:
                                P.stt("dve", slog[:, lsl, :], bv, 0.125, Tb[:, gsl, :], ALU.mult, ALU.add, [bank, Tb], [slog])
                        sk = sinks[:, hh * 4:hh * 4 + 4]
                        P.rmax(mx[:], slog[:], [slog], [mx])
                        P.tt("dve", mx[:], mx[:], sk, ALU.max, [mx, sinks], [mx])
                        P.ts("dve", nmx[:], mx[:], -1.0, None, ALU.mult, None, [mx], [nmx])
                        P.tt("dve", es_[:], sk, mx[:], ALU.subtract, [sinks, mx], [es_])
                        for hl in range(4):
                            P.act(Pm[:, hl, :], slog[:, hl, :], AF.Exp, [slog, nmx], [Pm, rs], bias=nmx[:, hl:hl + 1], accum=rs[:, hl:hl + 1])
                        P.act(es_[:], es_[:], AF.Exp, [es_], [es_])

                    def S3(u):
                        p = u % 2
                        bank = pb[p]
                        for hl in range(4):
                            for kc in range(2):
                                P.tp(bank[:, (hl * 2 + kc) * 128:(hl * 2 + kc + 1) * 128], Pm2[p][:, hl, kc * 128:(kc + 1) * 128], ident[:],
                                     [Pm2[p], ident], [bank], signal=(hl == 3 and kc == 1))
                        P.cp("act", PTs2[p][:].rearrange("p h k q -> p (h k q)"), bank[:, :], [bank], [PTs2[p]])

                    def S4(u):
                        t, hh = divmod(u, 2)
                        p = u % 2
                        po = pf[4 + p]
                        dn, rs, es_ = dn2[u % 4], rs2[u % 4], es2[u % 4]
                        P.tt("dve", dn[:], rs[:], es_[:], ALU.add, [rs, es_], [dn])
                        P.recip(dn[:], dn[:], [dn], [dn])
                        for hl in range(4):
                            for kc in range(2):
                                P.mm(po[:, hl * 64:(hl + 1) * 64], PTs2[p][:, hl, kc, :], vtok[:, t + kc, hh * 64:hh * 64 + 64],
                                     kc == 0, kc == 1, [PTs2[p], vtok], [po], signal=(hl == 3 and kc == 1))
                        at = attn2[t % 2]
                        P.tt("dve", at[:, hh * 256:(hh + 1) * 256].rearrange("p (h d) -> p h d", h=4),
                             po[:, 0:256].rearrange("p (h d) -> p h d", h=4),
                             dn2[u % 4][:].unsqueeze(2).to_broadcast([128, 4, 64]), ALU.mult, [po, dn2[u % 4]], [at])
                        if hh == 1:
                            bank = pb[p]
                            for c in range(4):
                                P.tp(bank[:, c * 128:(c + 1) * 128], at[:, c * 128:(c + 1) * 128], ident[:], [at, ident], [bank], signal=(c == 3))
                            P.cp("act", attT[:, :, t * 128:(t + 1) * 128], bank[:, 0:512].rearrange("p (c j) -> p c j", c=4), [bank], [attT])

                    NU = 8
                    for i in range(NU + 3):
                        if i < NU:
                            S1(i)
                        if 0 <= i - 1 < NU:
                            S2(i - 1)
                        if 0 <= i - 2 < NU:
                            S3(i - 2)
                        if 0 <= i - 3 < NU:
                            S4(i - 3)
                    if STOP == 44 and G == 0:
                        return True
                    P.cp("dve", kT[:, :, 0:128], kT[:, :, 512:640], [kT], [kT])
                    P.cp("dve", vtok[:, 0, :], vtok[:, 4, :], [vtok], [vtok])
                    if STOP == 45 and G == 0:
                        return True
                    for co in range(4):
                        bank = pfr.next()
                        for c in range(4):
                            P.mm(bank[:, :], wglu[:, c, co * 128:(co + 1) * 128], zT[:, c, m0:m0 + 512], c == 0, c == 3, [wglu, zT], [bank])
                        sg = sgt.next()
                        P.act(sg[:], bank[:, :], AF.Sigmoid, [bank], [sg])
                        P.tt("dve", zg[:, co, :], zT[:, co, m0:m0 + 512], sg[:], ALU.mult, [zT, sg], [zg])
                    if STOP == 46 and G == 0:
                        return True
                    for fo in range(8):
                        fs = slice(fo * 128, (fo + 1) * 128)
                        bga = pfr.next()
                        for k in range(8):
                            P.mm(bga[:, :], wg[:, k, fo * 128:(fo + 1) * 128], hTg[:, k, :], k == 0, k == 7, [wg, hTg], [bga])
                        sga = sgt.next()
                        P.act(sga[:], bga[:, :], AF.Sigmoid, [bga], [sga])
                        bgs = pfr.next()
                        for k in range(8):
                            P.mm(bgs[:, :], wg[:, k, 1024 + fo * 128:1024 + (fo + 1) * 128], hTg[:, k, :], k == 0, k == 7, [wg, hTg], [bgs])
                        sgs = sgt.next()
                        P.act(sgs[:], bgs[:, :], AF.Sigmoid, [bgs], [sgs])
                        ba = pfr.next()
                        for c in range(4):
                            P.mm(ba[:, :], wab[:, c, fs], attT[:, c, :], c == 0, c == 3, [wab, attT], [ba])
                        t1 = t1s.next()
                        P.tt("dve", t1[:], ba[:, :], sga[:], ALU.mult, [ba, sga], [t1])
                        bs = pfr.next()
                        for c in range(4):
                            P.mm(bs[:, :], wsb_[:, c, fs], zg[:, c, :], c == 0, c == 3, [wsb_, zg], [bs])
                        t2 = t2s.next()
                        P.tt("dve", t2[:], bs[:, :], sgs[:], ALU.mult, [bs, sgs], [t2])
                        P.tt("dve", mT[:, fo, :], t1[:], t2[:], ALU.add, [t1, t2], [mT])
                    if STOP == 47 and G == 0:
                        return True
                    xrs, xr_sem = {}, {}

                    def load_xr(t):
                        idx = xr_i[0] % 3
                        xr_i[0] += 1
                        xr = xrr[idx]
                        tok0 = 2048 + m0 + t * 128
                        P.dma("sp", xr[:], x[tok0:tok0 + 128, :], w=[xr], sem=f"S_xR{idx}")
                        xrs[t] = xr
                        xr_sem[t] = idx

                    load_xr(0)
                    load_xr(1)
                    pendn = None
                    for t in range(4):
                        if G + 1 < 4:
                            st2n = norm_tile_C(G + 1, t)
                            if pendn is not None:
                                pendn()
                            pendn = st2n
                        b0, b1 = pfr.next(), pfr.next()
                        for half, bank in ((0, b0), (1, b1)):
                            for k in range(8):
                                P.mm(bank[:, :], mT[:, k, t * 128:(t + 1) * 128], wout[:, k, half * 512:(half + 1) * 512], k == 0, k == 7,
                                     [mT, wout], [bank])
                        sp_ = ssp.next()
                        j0, j1 = sgt.next(), sgt.next()
                        P.act(j0[:], b0[:, :], AF.Square, [b0], [j0, sp_], accum=sp_[:, 0:1])
                        P.act(j1[:], b1[:, :], AF.Square, [b1], [j1, sp_], accum=sp_[:, 1:2])
                        s1 = ss1.next()
                        P.tt("dve", s1[:], sp_[:, 0:1], sp_[:, 1:2], ALU.add, [sp_], [s1])
                        tm, rsd = tmps.next(), rsts.next()
                        rstd_from_ss(s1[:], s1, rsd, tm)
                        xo = x1t.next()
                        for half, bank in ((0, b0), (1, b1)):
                            hs_ = slice(half * 512, (half + 1) * 512)
                            P.stt("dve", xo[:, hs_], bank[:, :], rsd[:], gpost[:, hs_], ALU.mult, ALU.mult, [bank, rsd, gpost], [xo])
                        xr = xrs[t]
                        P.tt("dve", xr[:], xo[:], xr[:], ALU.add, [xo, xr], [xr])
                        r0 = m0 + t * 128
                        P.dma("pool", x1scr[r0:r0 + 128, :], xr[:], r=[xr], w=[Buf()], sem=f"S_x1w{xr_sem[t]}")
                        if t + 2 < 4:
                            load_xr(t + 2)
                    if pendn is not None:
                        pendn()
                P.barrier()
                if STOP == 5:
                    return True

            scABC.close()
            with ExitStack() as scD:
                wffi = sb(scD, "wffi", [128, 8, 4096], BF16)
                wffo = sb(scD, "wffo", [128, 32, D], BF16)
                wffi_bv = wffi_b.rearrange("(k p) n -> p k n", p=128)
                wffo_bv = wffo_b.rearrange("(k p) n -> p k n", p=128)
                wffi_q = [Buf() for _ in range(4)]
                for q4 in range(4):
                    P.dma("act", wffi[:, :, q4 * 1024:(q4 + 1) * 1024], wffi_bv[:, :, q4 * 1024:(q4 + 1) * 1024],
                          r=[cvb_i], w=[wffi_q[q4]], sem=f"S_wffi{q4}")
                wffo_p = [Buf() for _ in range(8)]
                depw = Buf()
                depw.w = dict(wffi_q[3].w)
                for k4 in range(8):
                    P.dma("pool", wffo[:, 4 * k4:4 * k4 + 4, :], wffo_bv[:, 4 * k4:4 * k4 + 4, :], r=[cvb_o, depw], w=[wffo_p[k4]],
                          sem=f"S_wffo{k4}")
                g2pre = sb(scD, "g2pre", [128, D], F32)
                g2post = sb(scD, "g2post", [128, D], F32)
                P.dma("sp", g2pre[:], gains[2, :, :], w=[g2pre], sem="S_c3")
                P.dma("sp", g2post[:], gains[3, :, :], w=[g2post], sem="S_c3")
                P.seal("S_c3", [g2pre, g2post])
                x1g = Ring([sb(scD, f"x1g{i}", [128, D], F32) for i in range(4)])
                xss = Ring([sb(scD, f"xsD{i}", [128, D], BF16) for i in range(2)])
                sss = Ring([sb(scD, f"ssD{i}", [128, 1], F32) for i in range(6)])
                tmps = Ring([sb(scD, f"tmD{i}", [128, 1], F32) for i in range(4)])
                rsts = Ring([sb(scD, f"rsD{i}", [128, 1], F32) for i in range(4)])
                h2T = sb(scD, "h2T", [128, 8, 256], BF16)
                ffT = sb(scD, "ffT", [128, 32, 256], BF16)
                rl = Ring([sb(scD, f"rl{i}", [128, 512], BF16) for i in range(2)])
                ot = Ring([sb(scD, f"ot{i}", [128, D], F32) for i in range(2)])
                ssp = Ring([sb(scD, f"sspD{i}", [128, 2], F32) for i in range(2)])
                ss1 = Ring([sb(scD, f"ss1D{i}", [128, 1], F32) for i in range(2)])
                outb = Buf()
                h2Tb = sb(scD, "h2Tb", [128, 8, 256], BF16)
                h2T2 = [h2T, h2Tb]
                xtiles = {}

                def norm_group(Gn, defer):
                    st2s = []
                    tl = []
                    for t in range(2):
                        xt = x1g.next()
                        r0 = Gn * 256 + t * 128
                        P.dma("sp", xt[:], x1scr[r0:r0 + 128, :], w=[xt], sem=f"S_xD{(Gn * 2 + t) % 4}")
                        tl.append(xt)
                        hdst = h2T2[Gn % 2]
                        st2s.append(norm_transpose(xt, g2pre, xss.next(), sss.next(), tmps.next(), rsts.next(),
                                                   hdst[:, :, t * 128:(t + 1) * 128], hdst, "act", defer=True))
                    xtiles[Gn] = tl
                    if defer:
                        return st2s
                    for f in st2s:
                        f()
                    return []

                norm_group(0, False)
                for G in range(8):
                    m0 = G * 256
                    xs_g = xtiles[G]
                    hcur = h2T2[G % 2]
                    for fp in range(16):
                        bank = pfr.next()
                        for j in range(2):
                            fc = fp * 2 + j
                            for k in range(8):
                                P.mm(bank[:, j * 256:(j + 1) * 256], wffi[:, k, fc * 128:(fc + 1) * 128], hcur[:, k, :], k == 0, k == 7,
                                     [wffi_q[fc // 8], hcur], [bank], signal=(j == 1 and k == 7))
                        r_ = rl.next()
                        P.act(r_[:], bank[:, :], AF.Relu, [bank], [r_])
                        P.tt("dve", ffT[:, fp * 2:fp * 2 + 2, :], r_[:].rearrange("p (a n) -> p a n", a=2),
                             r_[:].rearrange("p (a n) -> p a n", a=2), ALU.mult, [r_], [ffT])
                    nxt = norm_group(G + 1, True) if G + 1 < 8 else []
                    for t in range(2):
                        b0, b1 = pfr.next(), pfr.next()
                        for half, bank in ((0, b0), (1, b1)):
                            for fc in range(32):
                                P.mm(bank[:, :], ffT[:, fc, t * 128:(t + 1) * 128], wffo[:, fc, half * 512:(half + 1) * 512], fc == 0, fc == 31,
                                     [ffT, wffo_p[fc // 4]], [bank])
                        if t == 0:
                            for f in nxt:
                                f()
                        sp_ = ssp.next()
                        j0, j1 = rl.next(), rl.next()
                        P.act(j0[:], b0[:, :], AF.Square, [b0], [j0, sp_], accum=sp_[:, 0:1])
                        P.act(j1[:], b1[:, :], AF.Square, [b1], [j1, sp_], accum=sp_[:, 1:2])
                        s1 = ss1.next()
                        P.tt("dve", s1[:], sp_[:, 0:1], sp_[:, 1:2], ALU.add, [sp_], [s1])
                        tm, rsd = tmps.next(), rsts.next()
                        rstd_from_ss(s1[:], s1, rsd, tm)
                        xo = ot.next()
                        for half, bank in ((0, b0), (1, b1)):
                            hs_ = slice(half * 512, (half + 1) * 512)
                            P.stt("dve", xo[:, hs_], bank[:, :], rsd[:], g2post[:, hs_], ALU.mult, ALU.mult, [bank, rsd, g2post], [xo])
                        P.tt("dve", xo[:], xo[:], xs_g[t][:], ALU.add, [xo, xs_g[t]], [xo])
                        r0 = m0 + t * 128
                        P.dma("pool", out_d[r0:r0 + 128, :], xo[:], r=[xo], w=[outb], sem=f"S_ow{(G * 2 + t) % 2}")
                P.barrier()


            return False

        if body():
            scS.close()
            scABC.close()
            P.barrier()

        block = es.enter_context(nc.Block())
        P.emit(block)
    return nc


def _bucket(d):
    d = np.asarray(d)
    df = np.maximum(d, 1).astype(np.float32)
    large = 16 + (np.log(df / np.float32(16)) / np.float32(math.log(128 / 16)) * np.float32(16)).astype(np.int32)
    large = np.minimum(large, 31)
    return np.where(d < 16, d, large)


def _constants():
    c = {}
    c["ident"] = np.eye(128, dtype=np.float32)
    oh = np.zeros((33, 384), np.float32)
    for m in range(383):
        d = 255 - m
        if 0 <= d < 128:
            oh[int(_bucket(d)), m] = 1.0
        else:
            oh[32, m] = 1.0
    c["oh"] = oh
    cm = np.zeros((128, 2, 256), np.float32)
    iq = np.zeros((128, 2, 256), np.float32)
    for p in range(128):
        sl, cc = divmod(p, 16)
        for q in range(2):
            s = 8 * q + sl
            for s2 in range(16):
                if s2 >= s:
                    cm[p, q, s2 * 16:(s2 + 1) * 16] = 1.0
            iq[p, q, s * 16 + cc] = 1.0
    c["cmask"], c["identq"] = cm, iq
    ph = np.zeros((128, 8), np.float32)
    top, bot = slice(0, 64), slice(64, 128)
    ph[top, 0], ph[bot, 0] = PI / 2, 0.0
    ph[top, 1], ph[bot, 1] = PI, PI / 2
    ph[top, 2], ph[bot, 2] = PI / 2, PI
    ph[top, 3], ph[bot, 3] = PI, 3 * PI / 2
    ph[top, 4], ph[bot, 4] = 0.0, PI
    ph[top, 5], ph[bot, 5] = 1.0, -1.0
    c["ph"] = ph
    c["tauN"] = np.tile(-np.arange(16, dtype=np.float32), (128, 1))
    c["tauP"] = np.tile(np.arange(17, dtype=np.float32), (128, 1))
    c["Jv"] = np.tile(np.arange(256, dtype=np.float32), (128, 1))
    return c


def _prep_inputs(inp):
    f = lambda a: np.ascontiguousarray(np.asarray(a, dtype=np.float32))
    shared = dict(_constants())
    shared["w_in"] = f(inp["w_in"][0])
    shared["w_glu"] = f(inp["w_glu"][0])
    shared["w_ab"] = f(inp["w_attn_branch"][0])
    shared["w_sb"] = f(inp["w_ssm_branch"][0])
    shared["w_out"] = f(inp["w_out"][0])
    shared["w_ffi"] = f(inp["w_ff_in"][0])
    shared["w_ffo"] = f(inp["w_ff_out"][0])
    gains = np.stack([inp["norm_mix_pre"][0], inp["norm_mix_post"][0], inp["norm_mlp_pre"][0], inp["norm_mlp_post"][0]])
    shared["gains"] = f(np.broadcast_to(np.asarray(gains, np.float32)[:, None, :], (4, 128, D)))
    relb = np.empty((33, 8, 128), np.float32)
    relb[:32] = np.asarray(inp["rel_bias"], np.float32)[:, :, None]
    relb[32] = NEG
    shared["relb"] = relb
    shared["sinks"] = f(np.broadcast_to(np.asarray(inp["sinks"][0], np.float32)[None, :], (128, 8)))
    dup = lambda a: f(np.concatenate([a, a], axis=0))
    shared["lamr"] = dup(np.asarray(inp["lam_re"][0], np.float32).T)
    shared["lami"] = dup(np.asarray(inp["lam_im"][0], np.float32).T)
    shared["ldt"] = f(np.broadcast_to(np.asarray(inp["log_dt"][0], np.float32)[None, :], (128, 32)))
    shared["bre"] = dup(np.transpose(np.asarray(inp["b_re"][0], np.float32), (1, 0, 2)))
    shared["bim"] = dup(np.transpose(np.asarray(inp["b_im"][0], np.float32), (1, 0, 2)))
    shared["cre"] = dup(np.transpose(np.asarray(inp["c_re"][0], np.float32), (2, 0, 1)))
    shared["cim"] = dup(np.transpose(np.asarray(inp["c_im"][0], np.float32), (2, 0, 1)))
    dsk = np.asarray(inp["d_skip"][0], np.float32).reshape(32, 16)
    ddiag = np.zeros((16, 32, 16), np.float32)
    for c_ in range(16):
        ddiag[c_, :, c_] = dsk[:, c_]
    shared["ddiag"] = ddiag
    shared["dcol"] = f(np.tile(dsk.T, (8, 1)))
    xs = np.asarray(inp["x"], np.float32)
    in_maps = []
    for core in range(8):
        b, half = divmod(core, 2)
        m = dict(shared)
        if half == 0:
            xc = np.concatenate([np.zeros((2048, D), np.float32), xs[b, :2048]], axis=0)
            halo = np.full((128, 128), NEG, np.float32)
        else:
            xc = xs[b]
            halo = np.zeros((128, 128), np.float32)
        m["x"] = np.ascontiguousarray(xc)
        m["halo"] = halo
        in_maps.append(m)
    return in_maps


_NC_CACHE = {}


def kernel(**inputs):
    in_maps = _prep_inputs(inputs)
    if "nc" not in _NC_CACHE:
        _NC_CACHE["nc"] = build()
    nc = _NC_CACHE["nc"]
    res = run_bass_kernel_spmd(nc, in_maps, core_ids=list(range(8)))
    out = np.empty((4, 4096, D), np.float32)
    for core in range(8):
        b, half = divmod(core, 2)
        out[b, half * 2048:(half + 1) * 2048] = np.asarray(res.results[core]["out"], np.float32)
    return out
```
